# Optimizing a Trainium2 kernel written in Bass

```python
import jax, jax.numpy as jnp
from jax import lax
import numpy as np

D_MODEL = 1024
BATCH = 4
SEQ = 8192
DEPTH = 2
DEC_BATCH = 16
DEC_SEQ = 32
PAST_LEN = 2048

CHUNK = 64
N_A_LAYERS = DEPTH // 2
N_B_LAYERS = DEPTH - N_A_LAYERS
A_EXPAND = 128
A_HEADS = D_MODEL // A_EXPAND
A_DK = A_EXPAND
A_DV = D_MODEL // A_HEADS
B_HEAD_DIM = 64
B_Q_HEADS = D_MODEL // B_HEAD_DIM
B_KV_HEADS = 4
B_GROUP = B_Q_HEADS // B_KV_HEADS
WINDOW = 128
WIN_CHUNKS = WINDOW // CHUNK
D_FF = -(-8 * D_MODEL // (3 * 256)) * 256
ROPE_THETA = 10000.0
NORM_EPS = 1e-6

kernel_name = "hgrn2_yoco_swa_sink_stream_step"

F32 = jnp.float32


def rms_norm(x, w):
    xf = x.astype(F32)
    y = xf * lax.rsqrt(jnp.mean(xf * xf, axis=-1, keepdims=True) + NORM_EPS)
    return (y * w.astype(F32)).astype(x.dtype)


def rope(x, pos):
    half = x.shape[-1] // 2
    inv = ROPE_THETA ** (-jnp.arange(half, dtype=F32) / half)
    ang = pos.astype(F32)[:, None] * inv[None, :]
    cos = jnp.cos(ang)[:, None, :]
    sin = jnp.sin(ang)[:, None, :]
    xf = x.astype(F32)
    x1, x2 = xf[..., :half], xf[..., half:]
    return jnp.concatenate([x1 * cos - x2 * sin, x2 * cos + x1 * sin], axis=-1).astype(x.dtype)


def hgrn2_block(S, blk):
    q, k, v, lf = blk
    L = q.shape[1]
    b = jnp.cumsum(lf, axis=1)
    o_inter = jnp.einsum('blhk,bhkv->blhv', q * jnp.exp(b), S)
    causal = jnp.tril(jnp.ones((L, L), dtype=bool))
    e = b[:, :, None] - b[:, None, :]
    decay = jnp.exp(jnp.where(causal[None, :, :, None, None], e, -jnp.inf))
    scores = jnp.einsum('btshk,bshk->bhts', decay * q[:, :, None], k)
    o_intra = jnp.einsum('bhts,bshv->bthv', scores, v)
    b_last = b[:, -1]
    S_new = jnp.exp(b_last)[..., None] * S + jnp.einsum(
        'bshk,bshv->bhkv', k * jnp.exp(b_last[:, None] - b), v)
    return S_new, o_inter + o_intra


def hgrn2_scan(q, k, v, lf, S0, block_len):
    B, T = q.shape[:2]
    n = T // block_len

    def split(a):
        return a.reshape(B, n, block_len, *a.shape[2:]).swapaxes(0, 1)

    S, o = lax.scan(hgrn2_block, S0, (split(q), split(k), split(v), split(lf)))
    o = o.swapaxes(0, 1).reshape(B, T, A_HEADS, A_DV)
    return o, S


def hgrn2_mixer(h, w_in, lb, g_norm, w_out, S0, block_len):
    B, T, _ = h.shape
    q, f, i, g = jnp.split(h @ w_in, 4, axis=-1)
    lbf = lb.astype(F32)
    forget = lbf + (1.0 - lbf) * jax.nn.sigmoid(f.astype(F32))
    lf = jnp.log(forget)
    inp = 1.0 - forget
    shp = (B, T, A_HEADS, A_DK)
    o, S = hgrn2_scan(jax.nn.silu(q.astype(F32)).reshape(shp), inp.reshape(shp),
                      i.astype(F32).reshape(B, T, A_HEADS, A_DV), lf.reshape(shp),
                      S0.astype(F32), block_len)
    o = rms_norm(o, g_norm) * jax.nn.silu(g.astype(F32)).reshape(B, T, A_HEADS, A_DV)
    return o.reshape(B, T, D_MODEL).astype(h.dtype) @ w_out, S


def sink_attention(q, k, v, sinks, valid):
    s = jnp.einsum('bnqkgd,bnskd->bnkgqs', q.astype(F32), k.astype(F32)) * (B_HEAD_DIM ** -0.5)
    s = jnp.where(valid[None, :, None, None, None, :], s, -jnp.inf)
    sink = sinks.astype(F32).reshape(1, 1, B_KV_HEADS, B_GROUP, 1, 1)
    m = jnp.maximum(jnp.max(s, axis=-1, keepdims=True), sink)
    p = jnp.exp(s - m)
    denom = jnp.sum(p, axis=-1, keepdims=True) + jnp.exp(sink - m)
    return jnp.einsum('bnkgqs,bnskd->bnqkgd', p / denom, v.astype(F32))


def window_attention_prompt(q, k, v, sinks):
    B, T = q.shape[:2]
    n = T // CHUNK
    qb = q.reshape(B, n, CHUNK, B_KV_HEADS, B_GROUP, B_HEAD_DIM)

    def band(a):
        ap = jnp.pad(a, ((0, 0), (WIN_CHUNKS * CHUNK, 0), (0, 0), (0, 0)))
        ap = ap.reshape(B, n + WIN_CHUNKS, CHUNK, B_KV_HEADS, B_HEAD_DIM)
        return jnp.concatenate([ap[:, j:j + n] for j in range(WIN_CHUNKS + 1)], axis=2)

    key_pos = (jnp.arange(n)[:, None] - WIN_CHUNKS) * CHUNK + jnp.arange((WIN_CHUNKS + 1) * CHUNK)[None, :]
    o = sink_attention(qb, band(k), band(v), sinks, key_pos >= 0)
    return o.reshape(B, T, B_Q_HEADS * B_HEAD_DIM)


def window_attention_sample(q, k_all, v_all, sinks):
    B, T = q.shape[:2]
    Lk = k_all.shape[1]
    o = sink_attention(q.reshape(B, 1, T, B_KV_HEADS, B_GROUP, B_HEAD_DIM),
                       k_all[:, None], v_all[:, None], sinks, jnp.ones((1, Lk), dtype=bool))
    return o.reshape(B, T, B_Q_HEADS * B_HEAD_DIM)


def run_trunk(x, pos, hgrn_state, cache_k, cache_v, block_len,
              norm_mix_pre, norm_mix_post, norm_ffn_pre, norm_ffn_post, w_ffn_in, w_ffn_out,
              w_a_in, a_lower_bound, a_out_norm, w_a_out, kv_norm, w_kv, w_b_q, b_sinks, w_b_out):
    B, T, _ = x.shape
    lower = jnp.cumsum(jax.nn.softmax(a_lower_bound.astype(F32), axis=0), axis=0)
    new_states = []
    k_attn = v_attn = None
    for layer in range(DEPTH):
        h = rms_norm(x, norm_mix_pre[layer])
        if layer < N_A_LAYERS:
            mix, S = hgrn2_mixer(h, w_a_in[layer], lower[layer], a_out_norm[layer], w_a_out[layer],
                                 hgrn_state[layer], block_len)
            new_states.append(S.astype(x.dtype))
        else:
            if layer == N_A_LAYERS:
                k, v = jnp.split(rms_norm(x, kv_norm) @ w_kv, 2, axis=-1)
                k = rope(k.reshape(B, T, B_KV_HEADS, B_HEAD_DIM), pos)
                v = v.reshape(B, T, B_KV_HEADS, B_HEAD_DIM)
                if cache_k is None:
                    k_attn, v_attn = k, v
                else:
                    k_attn = jnp.concatenate([cache_k.astype(k.dtype), k], axis=1)
                    v_attn = jnp.concatenate([cache_v.astype(v.dtype), v], axis=1)
            j = layer - N_A_LAYERS
            q = rope((h @ w_b_q[j]).reshape(B, T, B_Q_HEADS, B_HEAD_DIM), pos)
            if cache_k is None:
                o = window_attention_prompt(q, k_attn, v_attn, b_sinks[j])
            else:
                o = window_attention_sample(q, k_attn, v_attn, b_sinks[j])
            mix = o.astype(x.dtype) @ w_b_out[j]
        x = x + rms_norm(mix, norm_mix_post[layer])
        hf = rms_norm(x, norm_ffn_pre[layer])
        a, b = jnp.split(hf @ w_ffn_in[layer], 2, axis=-1)
        x = x + rms_norm((jax.nn.silu(a) * b) @ w_ffn_out[layer], norm_ffn_post[layer])
    return x, jnp.stack(new_states), k_attn[:, -WINDOW:], v_attn[:, -WINDOW:]


def setup_inputs(seed: int = 0) -> dict:
    key = jax.random.key(seed)
    ks = jax.random.split(key, 24)
    nrm = jax.random.normal
    D = D_MODEL
    QW = B_Q_HEADS * B_HEAD_DIM
    KVW = B_KV_HEADS * B_HEAD_DIM
    return {
        "x_prompt": nrm(ks[0], (BATCH, SEQ, D), F32),
        "x_sample": nrm(ks[1], (DEC_BATCH, DEC_SEQ, D), F32),
        "state_hgrn": 0.5 * nrm(ks[2], (N_A_LAYERS, DEC_BATCH, A_HEADS, A_DK, A_DV), F32),
        "cache_k": nrm(ks[3], (DEC_BATCH, WINDOW, B_KV_HEADS, B_HEAD_DIM), F32),
        "cache_v": nrm(ks[4], (DEC_BATCH, WINDOW, B_KV_HEADS, B_HEAD_DIM), F32),
        "norm_mix_pre": 1.0 + 0.05 * nrm(ks[5], (DEPTH, D), F32),
        "norm_mix_post": 1.0 + 0.05 * nrm(ks[6], (DEPTH, D), F32),
        "norm_ffn_pre": 1.0 + 0.05 * nrm(ks[7], (DEPTH, D), F32),
        "norm_ffn_post": 1.0 + 0.05 * nrm(ks[8], (DEPTH, D), F32),
        "w_ffn_in": nrm(ks[9], (DEPTH, D, 2 * D_FF), F32) * D ** -0.5,
        "w_ffn_out": nrm(ks[10], (DEPTH, D_FF, D), F32) * D_FF ** -0.5,
        "w_a_in": nrm(ks[11], (N_A_LAYERS, D, 4 * D), F32) * D ** -0.5,
        "a_lower_bound": 0.1 * nrm(ks[12], (DEPTH, D), F32),
        "a_out_norm": 1.0 + 0.05 * nrm(ks[13], (N_A_LAYERS, A_HEADS, A_DV), F32),
        "w_a_out": nrm(ks[14], (N_A_LAYERS, D, D), F32) * D ** -0.5,
        "kv_norm": 1.0 + 0.05 * nrm(ks[15], (D,), F32),
        "w_kv": nrm(ks[16], (D, 2 * KVW), F32) * D ** -0.5,
        "w_b_q": nrm(ks[17], (N_B_LAYERS, D, QW), F32) * D ** -0.5,
        "b_sinks": 0.5 * nrm(ks[18], (N_B_LAYERS, B_Q_HEADS), F32),
        "w_b_out": nrm(ks[19], (N_B_LAYERS, QW, D), F32) * QW ** -0.5,
    }


def reference(x_prompt, x_sample, state_hgrn, cache_k, cache_v,
              norm_mix_pre, norm_mix_post, norm_ffn_pre, norm_ffn_post, w_ffn_in, w_ffn_out,
              w_a_in, a_lower_bound, a_out_norm, w_a_out, kv_norm, w_kv, w_b_q, b_sinks, w_b_out):
    weights = (norm_mix_pre, norm_mix_post, norm_ffn_pre, norm_ffn_post, w_ffn_in, w_ffn_out,
               w_a_in, a_lower_bound, a_out_norm, w_a_out, kv_norm, w_kv, w_b_q, b_sinks, w_b_out)
    Bp, Tp, _ = x_prompt.shape
    Ts = x_sample.shape[1]
    pos_prompt = jnp.arange(Tp)
    pos_sample = PAST_LEN + jnp.arange(Ts)
    zero_state = jnp.zeros((N_A_LAYERS, Bp, A_HEADS, A_DK, A_DV), x_prompt.dtype)
    y_prompt, st_p, k_p, v_p = run_trunk(x_prompt, pos_prompt, zero_state, None, None, CHUNK, *weights)
    y_sample, st_s, k_s, v_s = run_trunk(x_sample, pos_sample, state_hgrn, cache_k, cache_v, Ts, *weights)
    return (y_prompt, y_sample, st_p, st_s, k_p, v_p, k_s, v_s)
```

```python
import numpy as np
from contextlib import ExitStack
import concourse.bass as bass
import concourse.mybir as mybir
from concourse.bass_utils import run_bass_kernel_spmd

F32 = mybir.dt.float32
BF16 = mybir.dt.bfloat16
AF = mybir.ActivationFunctionType
ALU = mybir.AluOpType

D = 1024
FC = 8
NH = 8
HC = 22
NTM = 384
NMAIN = 4224
NPRE = 3968
NSMP = 64
WIN = 128
EPS = 1e-6
SLOT = 8192
NSLOT = 3
NTMP = 14
NCONST = 128 + 128 + NTM + 64 + 128
NPAR = 112


class Buf:
    __slots__ = ("name", "w", "r", "excl", "wsmall")

    def __init__(self, name, excl=False):
        self.name = name
        self.w = None
        self.r = {}
        self.excl = excl
        self.wsmall = False


class Prog:
    ENG = ["pe", "act", "dve", "pool", "sp"]
    KRING = 6

    def __init__(self):
        self.items = {e: [] for e in self.ENG}
        self.cnt = {e: 0 for e in self.ENG}
        self.waited = {}
        self.dma_i = {}
        self.dma_n = {}
        self.semkeys = set(self.ENG)
        self.phase = ""

    def _need(self, eng, deps, skip_same=True):
        for k, v in deps.items():
            if k == eng and skip_same:
                continue
            if self.waited.get((eng, k), 0) < v:
                self.items[eng].append(("wait", k, v))
                self.waited[(eng, k)] = v

    @staticmethod
    def _deps(reads, writes):
        deps = {}

        def add(d):
            if d is None:
                return
            k, v = d
            if deps.get(k, 0) < v:
                deps[k] = v

        for b in reads:
            add(b.w)
            if b.excl:
                for it in b.r.items():
                    add(it)
        for b in writes:
            add(b.w)
            for it in b.r.items():
                add(it)
        return deps

    def op(self, eng, fn, reads=(), writes=(), sig=True, small=False):
        self._need(eng, self._deps(reads, writes))
        if eng != "pe":
            own = 0
            for b in list(reads) + list(writes):
                if b.w is not None and b.w[0] == eng and b.w[1] > own:
                    own = b.w[1]
            if own and self.waited.get((eng, eng), 0) < own:
                self.items[eng].append(("wait", eng, own))
                self.waited[(eng, eng)] = own
        if sig:
            self.cnt[eng] += 1
            tick = self.cnt[eng]
        else:
            tick = self.cnt[eng] + 1
        self.items[eng].append(("inst", fn, sig, self.phase))
        for b in reads:
            if b.excl:
                b.w = (eng, tick)
                b.r = {}
            else:
                if b.r.get(eng, 0) < tick:
                    b.r[eng] = tick
        for b in writes:
            b.w = (eng, tick)
            b.r = {}
            b.wsmall = small

    def dma(self, q, out_ap, in_ap, reads=(), writes=()):
        i = self.dma_i.get(q, 0)
        self.dma_i[q] = i + 1
        semkey = ("d", q, i % self.KRING)
        self.semkeys.add(semkey)
        n_prev = self.dma_n.get(semkey, 0)
        deps = self._deps(reads, writes)
        if n_prev:
            deps[semkey] = 16 * n_prev
        self._need(q, deps, skip_same=False)
        self.dma_n[semkey] = n_prev + 1
        val = 16 * (n_prev + 1)
        self.items[q].append(("dma", out_ap, in_ap, semkey))
        for b in reads:
            b.r[semkey] = val
        for b in writes:
            b.w = (semkey, val)
            b.r = {}
            b.wsmall = False

    def finish(self, eng="sp"):
        deps = {k: 16 * n for k, n in self.dma_n.items()}
        self._need(eng, deps)

    def materialize(self, nc, st):
        sems = {}
        for i, k in enumerate(sorted(self.semkeys, key=str)):
            sems[k] = st.enter_context(nc.semaphore("s%d" % i))
        block = st.enter_context(nc.Block())
        starters = {"pe": block.tensor, "act": block.scalar, "dve": block.vector,
                    "pool": block.gpsimd, "sp": block.sync}
        for e in self.ENG:
            items = self.items[e]
            if not items:
                continue

            def body(eng, items=items, e=e):
                for it in items:
                    if it[0] == "wait":
                        eng.wait_ge(sems[it[1]], it[2])
                    elif it[0] == "inst":
                        ins = it[1](eng)
                        if it[2]:
                            ins.then_inc(sems[e], 1)
                    else:
                        eng.dma_start(out=it[1], in_=it[2]).then_inc(sems[it[3]], 16)

            starters[e](body)


class CB:
    def __init__(self, nc, st, name, n, w, dtype, parts=128):
        self.t = st.enter_context(nc.sbuf_tensor(name, [parts, n, w], dtype))
        self.b = [Buf("%s%d" % (name, i)) for i in range(n)]
        self.n = n


def stage_table():
    stages = []

    def add(name, cols):
        stages.append((name, cols))

    add("PF", 8 * 1024)
    add("AV", 8 * 1024)
    for h in range(NH):
        add("A%d" % h, 8 * 384)
    add("AO", 8 * 1024)
    for l in range(2):
        for j in range(11):
            add("FI%d_%d" % (l, j), 2 * 8 * 256)
        for q in range(4):
            add("FO%d_%d" % (l, q), 2 * HC * 128)
        if l == 0:
            add("KVa", 8 * 1024)
            add("KVb", 8 * 256)
            for c in range(4):
                add("Q%d" % c, 2 * 8 * 256)
            add("BO", 8 * 1024)
    off = {}
    o = 0
    for name, cols in stages:
        off[name] = (o, cols)
        o += cols
    return stages, off, o


STAGES, STOFF, TOTC = stage_table()
CHUNK_BOUNDS = []


def _mk_chunks():
    groups = [["PF", "AV"], ["A%d" % h for h in range(NH)] + ["AO"],
              ["FI0_%d" % j for j in range(11)], ["FO0_%d" % q for q in range(4)],
              ["KVa", "KVb"] + ["Q%d" % c for c in range(4)] + ["BO"],
              ["FI1_%d" % j for j in range(11)], ["FO1_%d" % q for q in range(4)]]
    res = []
    for g in groups:
        a = STOFF[g[0]][0]
        b = STOFF[g[-1]][0] + STOFF[g[-1]][1]
        res.append((a, b, g))
    return res


CHUNKS = _mk_chunks()


def kc_block(W):
    return W.reshape(8, 128, -1).transpose(1, 0, 2)


def build_wall(inp):
    f = np.float32
    w_a_in = np.asarray(inp["w_a_in"], f)[0]
    w_a_out = np.asarray(inp["w_a_out"], f)[0]
    w_kv = np.asarray(inp["w_kv"], f)
    w_q = np.asarray(inp["w_b_q"], f)[0]
    w_bo = np.asarray(inp["w_b_out"], f)[0]
    wall = np.empty((128, TOTC), f)

    def put(name, arr):
        o, c = STOFF[name]
        wall[:, o:o + c] = arr.reshape(128, c)

    put("PF", kc_block(w_a_in[:, 1024:2048]))
    put("AV", kc_block(w_a_in[:, 2048:3072]))
    for h in range(NH):
        s = slice(h * 128, (h + 1) * 128)
        W = np.concatenate([w_a_in[:, 0:1024][:, s], w_a_in[:, 1024:2048][:, s],
                            w_a_in[:, 3072:4096][:, s]], axis=1)
        put("A%d" % h, kc_block(W))
    put("AO", kc_block(w_a_out))
    for l in range(2):
        Wi = np.asarray(inp["w_ffn_in"], f)[l]
        Wo = np.asarray(inp["w_ffn_out"], f)[l].reshape(HC, 128, 1024).transpose(1, 0, 2)
        for j in range(11):
            blks = []
            for hc in (2 * j, 2 * j + 1):
                W = np.concatenate([Wi[:, hc * 128:(hc + 1) * 128],
                                    Wi[:, 2816 + hc * 128:2816 + (hc + 1) * 128]], axis=1)
                blks.append(kc_block(W))
            put("FI%d_%d" % (l, j), np.stack(blks, axis=1))
        for q in range(4):
            blks = [Wo[:, :, oc * 128:(oc + 1) * 128] for oc in (2 * q, 2 * q + 1)]
            put("FO%d_%d" % (l, q), np.stack(blks, axis=1))
    perm = np.concatenate([np.arange(32, 64), np.arange(0, 32)])
    kd, kp = [], []
    for kvh in range(4):
        Kh = w_kv[:, kvh * 64:(kvh + 1) * 64]
        kd += [Kh, Kh]
        kp += [Kh[:, perm], Kh[:, perm]]
    put("KVa", kc_block(np.concatenate(kd + kp, axis=1)))
    put("KVb", kc_block(w_kv[:, 256:512]))
    for c4 in range(4):
        blks = []
        for c in (2 * c4, 2 * c4 + 1):
            Wc = w_q[:, c * 128:(c + 1) * 128]
            Wp = np.concatenate([Wc[:, 0:64][:, perm], Wc[:, 64:128][:, perm]], axis=1)
            blks.append(kc_block(np.concatenate([Wc, Wp], axis=1)))
        put("Q%d" % c4, np.stack(blks, axis=1))
    put("BO", kc_block(w_bo))
    return wall


def rope_tables(pos):
    f = np.float32
    half = 32
    inv = 10000.0 ** (-np.arange(half, dtype=np.float64) / half)
    ang = pos.astype(np.float64)[None, :] * inv[:, None]
    cos = np.cos(ang).astype(f)
    sin = np.sin(ang).astype(f)
    p = np.arange(128)
    d = p % 64
    fi = d % 32
    sign = np.where(d < 32, -1.0, 1.0).astype(f)
    return np.stack([cos[fi], sin[fi] * sign[:, None]], axis=0).astype(f)


def col8(v):
    return np.asarray(v, np.float32).reshape(8, 128).T


class Builder:
    def __init__(self, do_pre=True, do_main=True, do_sample=True, debug=False):
        self.do_pre, self.do_main, self.do_sample, self.debug = do_pre, do_main, do_sample, debug
        self.taps = {}
        self.nc = bass.Bass("TRN2", target_bir_lowering=False)
        self.P = Prog()
        self.st = ExitStack()
        self.bank_i = 0
        self.tmp_i = 0
        self.tmpb_i = 0
        self.pt_i = 0
        self.slot_i = 0

    def dram_in(self, name, shape, dtype=F32):
        return self.nc.dram_tensor(name, list(shape), dtype, kind="ExternalInput").ap()

    def dram_out(self, name, shape, dtype=F32):
        return self.nc.dram_tensor(name, list(shape), dtype, kind="ExternalOutput").ap()

    def tap(self, name, ap, reads):
        if not self.debug or name in self.taps:
            return
        shape = list(ap.shape)
        d = self.nc.dram_tensor("tap_" + name, shape, F32, kind="ExternalOutput").ap()
        self.taps[name] = shape
        self.P.dma("pool", d, ap, reads=reads, writes=[self.youtb])

    def sb(self, name, shape, dtype):
        return self.st.enter_context(self.nc.sbuf_tensor(name, list(shape), dtype))

    def bank(self):
        i = self.bank_i % 8
        self.bank_i += 1
        return self.banks[i], self.bankb[i]

    def tmp(self):
        i = self.tmp_i % NTMP
        self.tmp_i += 1
        return self.TP.t[:, i, :], self.TP.b[i]

    def tmpb(self):
        i = self.tmpb_i % 4
        self.tmpb_i += 1
        return self.TB.t[:, i, :], self.TB.b[i]

    def pt(self):
        i = self.pt_i % 4
        self.pt_i += 1
        return self.PT.t[:, i, :], self.PT.b[i]

    @staticmethod
    def _small(ap):
        n = 1
        for d in ap.shape[1:]:
            n *= d
        return n <= 128

    def MM(self, out, lhsT, rhs, start, stop, reads, bankb, sig=False):
        self.P.op("pe", lambda e: e.matmul(out, lhsT=lhsT, rhs=rhs, start=start, stop=stop),
                  reads=reads, writes=[bankb], sig=sig)

    def TR(self, out, in_, ident, reads, bankb, sig=False):
        self.P.op("pe", lambda e: e.transpose(out, in_, ident), reads=reads, writes=[bankb], sig=sig)

    def ACT(self, out, in_, func, reads, writes, scale=None, bias=None):
        kw = {}
        if scale is not None:
            kw["scale"] = scale
        if bias is not None:
            kw["bias"] = bias
        self.P.op("act", lambda e: e.activation(out=out, in_=in_, func=func, **kw),
                  reads=reads, writes=writes, small=self._small(out))

    def TT(self, eng, out, in0, in1, op, reads, writes):
        self.P.op(eng, lambda e: e.tensor_tensor(out=out, in0=in0, in1=in1, op=op),
                  reads=reads, writes=writes, small=self._small(out))

    def TS(self, eng, out, in0, s1, s2, op0, op1, reads, writes):
        if op1 is None:
            self.P.op(eng, lambda e: e.tensor_scalar(out=out, in0=in0, scalar1=s1, scalar2=None, op0=op0),
                      reads=reads, writes=writes, small=self._small(out))
        else:
            self.P.op(eng, lambda e: e.tensor_scalar(out=out, in0=in0, scalar1=s1, scalar2=s2, op0=op0, op1=op1),
                      reads=reads, writes=writes, small=self._small(out))

    def STT(self, out, in0, scalar, in1, op0, op1, reads, writes):
        self.P.op("dve", lambda e: e.scalar_tensor_tensor(out=out, in0=in0, scalar=scalar, in1=in1,
                                                          op0=op0, op1=op1), reads=reads, writes=writes,
                  small=self._small(out))

    def CP(self, eng, out, in_, reads, writes):
        if eng == "act":
            self.P.op("act", lambda e: e.copy(out=out, in_=in_), reads=reads, writes=writes, small=self._small(out))
        else:
            self.P.op(eng, lambda e: e.tensor_copy(out=out, in_=in_), reads=reads, writes=writes, small=self._small(out))

    def RECIP(self, out, in_, reads, writes):
        self.P.op("dve", lambda e: e.reciprocal(out=out, in_=in_), reads=reads, writes=writes, small=self._small(out))

    def load_stage(self, name):
        o, c = STOFF[name]
        i = self.slot_i % NSLOT
        self.slot_i += 1
        ck = None
        for k, (a, b, g) in enumerate(CHUNKS):
            if a <= o < b:
                ck = self.chunkb[k]
        self.P.dma("sp", self.ring[i][:, 0:c], self.WB[:, o:o + c], reads=[ck], writes=[self.ringb[i]])
        return self.ring[i], self.ringb[i]

    def build(self):
        nc, P = self.nc, self.P
        self.xT = self.dram_in("xT", [D, NMAIN])
        self.xpT = self.dram_in("xpT", [D, NPRE])
        self.xsT = self.dram_in("xsT", [D, NSMP])
        self.st0 = self.dram_in("st0", [2, NH, 128, 128])
        self.ck = self.dram_in("ck", [2, 128, 256])
        self.cv = self.dram_in("cv", [2, 128, 256])
        self.ckT = self.dram_in("ckT", [2, 4, 64, 128])
        self.ropeM = self.dram_in("ropeM", [2, 128, NMAIN])
        self.ropeS = self.dram_in("ropeS", [2, 128, NSMP])
        self.wall = self.dram_in("wall", [128, TOTC])
        self.par = self.dram_in("par", [128, NPAR])
        self.cst = self.dram_in("cst", [128, NCONST])
        self.WB = nc.dram_tensor("WB", [128, TOTC], BF16, kind="Internal").ap()

        self.yT = self.dram_out("yT", [D, NMAIN])
        self.ysT = self.dram_out("ysT", [D, NSMP])
        self.Sp = self.dram_out("Sp", [NH, 128, 128])
        self.Ss = self.dram_out("Ss", [2, NH, 128, 128])
        self.kpT = self.dram_out("kpT", [4, 64, 128])
        self.vp = self.dram_out("vp", [128, 256])
        self.ksT = self.dram_out("ksT", [4, 64, NSMP])
        self.vs = self.dram_out("vs", [2, 32, 256])
        self.kc = self.dram_out("kc", [2, 96, 256])
        self.vc = self.dram_out("vc", [2, 96, 256])

        st = self.st
        self.banks = [st.enter_context(nc.psum_tensor("pb%d" % i, [128, 512], F32)) for i in range(8)]
        self.bankb = [Buf("pb%d" % i, excl=True) for i in range(8)]
        self.X = CB(nc, st, "X", FC, NTM, F32)
        self.H = CB(nc, st, "H", FC, NTM, BF16)
        self.SQ = CB(nc, st, "SQ", FC, NTM, BF16)
        self.MIX = CB(nc, st, "MIX", FC, NTM, F32)
        self.AR = CB(nc, st, "AR", 24, NTM, BF16)
        self.SG = CB(nc, st, "SG", FC, NTM, BF16)
        self.ON = CB(nc, st, "ON", FC, NTM, BF16)
        self.VT = CB(nc, st, "VT", NTM // 128, 1024, BF16)
        self.TP = CB(nc, st, "TP", NTMP, NTM, F32)
        self.TB = CB(nc, st, "TB", 4, NTM, BF16)
        self.PT = CB(nc, st, "PT", 4, NTM, BF16)
        self.KH = CB(nc, st, "KH", 3, NTM, F32)
        self.LF = CB(nc, st, "LF", 1, NTM, F32)
        self.KHT = CB(nc, st, "KHT", 2, NTM, BF16)
        self.S = CB(nc, st, "S", NH, 128, F32)
        self.SS = CB(nc, st, "SS", 2 * NH, 128, F32)
        self.SBF = CB(nc, st, "SBF", NH, NTM, BF16)
        self.EBL = CB(nc, st, "EBL", NH, 4, F32)
        self.KT = CB(nc, st, "KT", 4, 128 + NTM, BF16)
        self.VA = CB(nc, st, "VA", 2 + NTM // 64, 256, BF16, parts=64)
        self.VAF = CB(nc, st, "VAF", 2, 256, F32, parts=64)
        self.KTC = CB(nc, st, "KTC", 8, 128, BF16)
        self.KTCF = CB(nc, st, "KTCF", 8, 128, F32)
        self.VC = CB(nc, st, "VC", 2, 256, BF16)
        self.VCF = CB(nc, st, "VCF", 2, 256, F32)
        self.ROPE = CB(nc, st, "ROPE", 2, NTM, F32)
        self.ring = [self.sb("ring%d" % i, [128, SLOT], BF16) for i in range(NSLOT)]
        self.ringb = [Buf("ring%d" % i) for i in range(NSLOT)]
        self.CST = self.sb("CST", [128, NCONST], F32)
        self.cstb = Buf("cst")
        self.PAR = self.sb("PAR", [128, NPAR], F32)
        self.parb = Buf("par")
        self.DER = self.sb("DER", [128, 40], F32)
        self.derb = Buf("der")
        self.ONESB = self.sb("ONESB", [128, 128], BF16)
        self.onesb = Buf("onesb")
        self.chunkb = [Buf("wchunk%d" % k) for k in range(len(CHUNKS))]
        self.youtb = Buf("yout")

        self.ident = self.CST[:, 0:128]
        self.mask = self.CST[:, 128:256]
        self.rmask128 = self.CST[:, 256:256 + NTM]
        self.rmask32 = self.CST[:, 256 + NTM:256 + NTM + 64]
        onesf = self.CST[:, 256 + NTM + 64:256 + NTM + 64 + 128]

        PIECE = 8192
        self.pieces = []
        o = 0
        while o < TOTC:
            e = min(TOTC, o + PIECE)
            self.pieces.append((o, e, Buf("wpiece%d" % o)))
            o = e
        self.piece_next = 0
        P.dma("act", self.CST[:, :], self.cst, writes=[self.cstb])
        P.dma("act", self.PAR[:, :], self.par, writes=[self.parb])
        self.CP("dve", self.ONESB[:, :], onesf, reads=[self.cstb], writes=[self.onesb])
        DER = self.DER
        a0 = self.PAR[:, 80:88]
        a1 = self.PAR[:, 88:96]
        self.one_col = self.PAR[:, 105:106]
        self.eps_col = self.PAR[:, 104:105]
        self.TT("dve", DER[:, 0:8], a1, a0, ALU.subtract, reads=[self.parb], writes=[self.derb])
        self.ACT(DER[:, 0:8], DER[:, 0:8], AF.Exp, reads=[self.derb], writes=[self.derb])
        self.ACT(DER[:, 0:8], DER[:, 0:8], AF.Ln, reads=[self.derb, self.parb], writes=[self.derb],
                 bias=self.one_col, scale=1.0)
        self.ACT(DER[:, 0:8], DER[:, 0:8], AF.Exp, reads=[self.derb], writes=[self.derb], scale=-1.0)
        self.TS("dve", DER[:, 8:16], DER[:, 0:8], -1.0, 1.0, ALU.mult, ALU.add, reads=[self.derb], writes=[self.derb])
        self.TS("dve", DER[:, 16:24], DER[:, 8:16], -1.0, None, ALU.mult, None, reads=[self.derb], writes=[self.derb])
        self.ACT(DER[:, 24:32], self.PAR[:, 96:104], AF.Exp, reads=[self.parb, self.derb], writes=[self.derb])
        self.lb, self.oml, self.noml, self.esink = DER[:, 0:8], DER[:, 8:16], DER[:, 16:24], DER[:, 24:32]
        self.tap("DER", DER[:, 0:32], [self.derb])

        for h in range(NH):
            self.P.op("dve", lambda e, h=h: e.memset(self.S.t[:, h, :], 0.0), writes=[self.S.b[h]])

        t0 = 0
        npre_tiles = (NPRE + NTM - 1) // NTM
        per_tile = (len(self.pieces) - 2 + max(1, npre_tiles - 2) - 1) // max(1, npre_tiles - 2)
        while t0 < NPRE and self.do_pre:
            nt = min(NTM, NPRE - t0)
            self.load_x(self.xpT, NPRE, t0, nt)
            if t0 == 0:
                self.emit_conv(2, after=self.X.b)
            else:
                self.emit_conv(per_tile)
            self.hgrn_layer(nt, 128, full=False)
            t0 += nt
        t0 = 0
        ti = 0
        while t0 < NMAIN and self.do_main:
            nt = min(NTM, NMAIN - t0)
            last = (t0 + nt == NMAIN)
            self.load_x(self.xT, NMAIN, t0, nt)
            self.emit_conv(len(self.pieces))
            self.load_rope(self.ropeM, NMAIN, t0, nt)
            self.hgrn_layer(nt, 128, full=True)
            self.ffn(0, nt)
            self.attn_layer_main(nt, first=(ti == 0), last=last)
            self.ffn(1, nt)
            self.store_y(self.yT, NMAIN, t0, nt)
            t0 += nt
            ti += 1
        for h in range(NH):
            P.dma("pool", self.Sp[h], self.S.t[:, h, :], reads=[self.S.b[h]], writes=[self.youtb])
        if self.do_sample:
            self.emit_conv(len(self.pieces))
            self.sample_phase()
        P.finish("sp")
        P.materialize(nc, st)
        return nc

    def _fix_chunk_deps(self):
        P = self.P
        i = 0
        cnt = {}
        self.chunk_deps = []
        for k, (a, b, g) in enumerate(CHUNKS):
            deps = {}
            for o in range(a, b, 16384):
                semkey = ("d", "pool", i % P.KRING)
                cnt[semkey] = cnt.get(semkey, 0) + 1
                deps[semkey] = 16 * cnt[semkey]
                i += 1
            self.chunk_deps.append(deps)
        self.chunk_bufs = []
        for k, deps in enumerate(self.chunk_deps):
            bl = []
            for sk, v in deps.items():
                b = Buf("wchunk%d_%s" % (k, sk[2]))
                b.w = (sk, v)
                bl.append(b)
            self.chunk_bufs.append(bl)

    def emit_conv(self, n, after=()):
        for _ in range(n):
            if self.piece_next >= len(self.pieces):
                return
            a, b, pb = self.pieces[self.piece_next]
            self.piece_next += 1
            self.P.dma("pool", self.WB[:, a:b], self.wall[:, a:b], reads=list(after), writes=[pb])

    def load_stage(self, name):
        o, c = STOFF[name]
        i = self.slot_i % NSLOT
        self.slot_i += 1
        bl = [pb for (a, b, pb) in self.pieces if a < o + c and b > o]
        for (a, b, pb) in self.pieces:
            if a < o + c and b > o:
                assert pb.w is not None, "conversion piece for %s not issued yet" % name
        self.P.dma("sp", self.ring[i][:, 0:c], self.WB[:, o:o + c], reads=bl, writes=[self.ringb[i]])
        return self.ring[i], self.ringb[i]

    def _old_load_stage(self, name):
        o, c = STOFF[name]
        i = self.slot_i % NSLOT
        self.slot_i += 1
        bl = None
        for k, (a, b, g) in enumerate(CHUNKS):
            if a <= o < b:
                bl = self.chunk_bufs[k]
        self.P.dma("sp", self.ring[i][:, 0:c], self.WB[:, o:o + c], reads=bl, writes=[self.ringb[i]])
        return self.ring[i], self.ringb[i]

    def load_x(self, src, ntot, t0, nt):
        v = src.rearrange("(c p) t -> p c t", p=128)
        for c in range(FC):
            self.P.dma("act", self.X.t[:, c, 0:nt], v[:, c, t0:t0 + nt], writes=[self.X.b[c]])

    def load_rope(self, src, ntot, t0, nt):
        v = src.rearrange("a p t -> p a t")
        self.P.dma("act", self.ROPE.t[:, :, 0:nt], v[:, :, t0:t0 + nt], writes=self.ROPE.b)

    def store_y(self, dst, ntot, t0, nt):
        v = dst.rearrange("(c p) t -> p c t", p=128)
        self.P.dma("pool", v[:, :, t0:t0 + nt], self.MIX.t[:, :, 0:nt], reads=self.MIX.b, writes=[self.youtb])

    def stats_rstd(self, srcs, nt, scale):
        bk, bb = self.bank()
        n = len(srcs)
        for i, (ap, b) in enumerate(srcs):
            self.MM(bk[:, 0:nt], self.ONESB[:, :], ap, i == 0, i == n - 1, [self.onesb, b], bb, sig=(i == n - 1))
        r, rb = self.tmp()
        self.ACT(r[:, 0:nt], bk[:, 0:nt], AF.Ln, reads=[bb, self.parb], writes=[rb], scale=scale, bias=self.eps_col)
        self.ACT(r[:, 0:nt], r[:, 0:nt], AF.Exp, reads=[rb], writes=[rb], scale=-0.5)
        return r, rb

    def norm_sq(self, nt):
        for c in range(FC):
            self.ACT(self.SQ.t[:, c, 0:nt], self.X.t[:, c, 0:nt], AF.Square, reads=[self.X.b[c]], writes=[self.SQ.b[c]])

    def prenorm(self, nt, widx_list, dsts):
        self.norm_sq(nt)
        r, rb = self.stats_rstd([(self.SQ.t[:, c, 0:nt], self.SQ.b[c]) for c in range(FC)], nt, 1.0 / D)
        for widx, dst in zip(widx_list, dsts):
            for c in range(FC):
                self.STT(dst.t[:, c, 0:nt], self.X.t[:, c, 0:nt], self.PAR[:, widx * 8 + c:widx * 8 + c + 1],
                         r[:, 0:nt], ALU.mult, ALU.mult, reads=[self.X.b[c], self.parb, rb], writes=[dst.b[c]])

    def evac_mix(self, bk, bb, oc, nt, widx):
        self.ACT(self.MIX.t[:, oc, 0:nt], bk[:, 0:nt], AF.Copy, reads=[bb, self.parb], writes=[self.MIX.b[oc]],
                 scale=self.PAR[:, widx * 8 + oc:widx * 8 + oc + 1])
        self.ACT(self.SQ.t[:, oc, 0:nt], bk[:, 0:nt], AF.Square, reads=[bb], writes=[self.SQ.b[oc]])

    def postnorm_add(self, nt, widx, to_mix=False):
        r, rb = self.stats_rstd([(self.SQ.t[:, c, 0:nt], self.SQ.b[c]) for c in range(FC)], nt, 1.0 / D)
        dst = self.MIX if to_mix else self.X
        for c in range(FC):
            t, tb = self.tmp()
            self.TT("dve", t[:, 0:nt], self.MIX.t[:, c, 0:nt], r[:, 0:nt], ALU.mult, reads=[self.MIX.b[c], rb], writes=[tb])
            self.TT("pool" if c % 2 == 0 else "dve", dst.t[:, c, 0:nt], self.X.t[:, c, 0:nt], t[:, 0:nt], ALU.add,
                    reads=[tb, self.X.b[c]], writes=[dst.b[c]])

    def recip_sig(self, src_ap, src_reads, nt):
        t, tb = self.tmp()
        self.ACT(t[:, 0:nt], src_ap, AF.Exp, reads=src_reads, writes=[tb], scale=-1.0)
        self.TS("dve", t[:, 0:nt], t[:, 0:nt], 1.0, None, ALU.add, None, reads=[tb], writes=[tb])
        self.RECIP(t[:, 0:nt], t[:, 0:nt], reads=[tb], writes=[tb])
        return t, tb

    def sigmoid_from(self, src_ap, src_reads, nt):
        t, tb = self.tmp()
        self.ACT(t[:, 0:nt], src_ap, AF.Exp, reads=src_reads, writes=[tb], scale=-1.0)
        self.ACT(t[:, 0:nt], t[:, 0:nt], AF.Ln, reads=[tb, self.parb], writes=[tb], bias=self.one_col, scale=1.0)
        self.ACT(t[:, 0:nt], t[:, 0:nt], AF.Exp, reads=[tb], writes=[tb], scale=-1.0)
        return t, tb

    def hgrn_layer(self, nt, BL, full, sample=False):
        P = self.P
        nb = nt // BL
        rmask = self.rmask32 if BL == 32 else self.rmask128
        P.phase = "hg_norm"
        self.prenorm(nt, [0], [self.H])
        H = self.H
        if sample:
            self.tap("X0", self.X.t[:, :, 0:nt], self.X.b)
            self.tap("H0", H.t[:, :, 0:nt], H.b)

        def state(h, j):
            if sample:
                return self.SS.t[:, j * NH + h, :], self.SS.b[j * NH + h]
            return self.S.t[:, h, :], self.S.b[h]

        P.phase = "hg_v"
        wv, wvb = self.load_stage("AV")
        wv3 = wv[:, 0:8192].rearrange("p (k n) -> p k n", k=8)
        for j in range(nb):
            for half in range(2):
                bk, bb = self.bank()
                for kc in range(8):
                    self.MM(bk[0:BL, 0:512], H.t[:, kc, j * BL:(j + 1) * BL], wv3[:, kc, half * 512:(half + 1) * 512],
                            kc == 0, kc == 7, [H.b[kc], wvb], bb, sig=(kc == 7))
                self.CP("act" if half == 0 else "dve", self.VT.t[0:BL, j, half * 512:(half + 1) * 512], bk[0:BL, 0:512],
                        reads=[bb], writes=[self.VT.b[j]])
        if sample:
            self.tap("VT", self.VT.t[0:BL, 0:nb, :], self.VT.b)
        if not full:
            wf, wfb = self.load_stage("PF")
            wf3 = wf[:, 0:8192].rearrange("p (k n) -> p k n", k=8)

        SF = self.MIX

        def phase1(h):
            P.phase = "hg_p1"
            if full:
                wa, wab = self.load_stage("A%d" % h)
                wa3 = wa[:, 0:3072].rearrange("p (k n) -> p k n", k=8)
            pf, pfb = self.bank()
            for kc in range(8):
                lw = wa3[:, kc, 128:256] if full else wf3[:, kc, h * 128:(h + 1) * 128]
                self.MM(pf[:, 0:nt], lw, H.t[:, kc, 0:nt], kc == 0, kc == 7, [H.b[kc], wab if full else wfb], pfb, sig=(kc == 7))
            self.ACT(SF.t[:, h, 0:nt], pf[:, 0:nt], AF.Sigmoid, reads=[pfb], writes=[SF.b[h]])
            if full:
                pq, pqb = self.bank()
                for kc in range(8):
                    self.MM(pq[:, 0:nt], wa3[:, kc, 0:128], H.t[:, kc, 0:nt], kc == 0, kc == 7, [H.b[kc], wab], pqb, sig=(kc == 7))
                sq_, sqb = self.tmp()
                self.ACT(sq_[:, 0:nt], pq[:, 0:nt], AF.Sigmoid, reads=[pqb], writes=[sqb])
                self.TT("dve", self.AR.t[:, h, 0:nt], pq[:, 0:nt], sq_[:, 0:nt], ALU.mult, reads=[pqb, sqb], writes=[self.AR.b[h]])
                pg, pgb = self.bank()
                for kc in range(8):
                    self.MM(pg[:, 0:nt], wa3[:, kc, 256:384], H.t[:, kc, 0:nt], kc == 0, kc == 7, [H.b[kc], wab], pgb, sig=(kc == 7))
                sg_, sgb = self.tmp()
                self.ACT(sg_[:, 0:nt], pg[:, 0:nt], AF.Sigmoid, reads=[pgb], writes=[sgb])
                self.TT("dve", self.SG.t[:, h, 0:nt], pg[:, 0:nt], sg_[:, 0:nt], ALU.mult, reads=[pgb, sgb], writes=[self.SG.b[h]])

        ctx = {}

        def s1ln(h):
            P.phase = "hg_s1"
            s, sb_ = SF.t[:, h, :], SF.b[h]
            lf, lfb = self.LF.t[:, 0, :], self.LF.b[0]
            self.ACT(lf[:, 0:nt], s[:, 0:nt], AF.Ln, reads=[sb_, self.derb], writes=[lfb],
                     scale=self.oml[:, h:h + 1], bias=self.lb[:, h:h + 1])

        def s1a(h):
            P.phase = "hg_s1"
            s, sb_ = SF.t[:, h, :], SF.b[h]
            lf, lfb = self.LF.t[:, 0, :], self.LF.b[0]
            kk, kkb = self.tmp()
            self.TS("dve", kk[:, 0:nt], s[:, 0:nt], self.noml[:, h:h + 1], self.oml[:, h:h + 1], ALU.mult, ALU.add,
                    reads=[sb_, self.derb], writes=[kkb])
            bcs, bcb = self.tmp()
            P.op("dve", lambda e, bcs=bcs, lf=lf: e.tensor_tensor_scan(out=bcs[:, 0:nt], data0=rmask[:, 0:nt], data1=lf[:, 0:nt],
                                                                        initial=0.0, op0=ALU.mult, op1=ALU.add),
                 reads=[lfb, self.cstb], writes=[bcb])
            ctx[("s1", h)] = (lf, lfb, kk, kkb, bcs, bcb)

        def s1b(h):
            P.phase = "hg_s1"
            lf, lfb, kk, kkb, bcs, bcb = ctx.pop(("s1", h))
            b3 = bcs[:, 0:nt].rearrange("p (j t) -> p j t", t=BL)
            self.ACT(self.EBL.t[:, h, 0:nb], b3[:, :, BL - 1], AF.Exp, reads=[bcb], writes=[self.EBL.b[h]])
            eh, ehb = self.tmp()
            for j in range(nb):
                self.ACT(eh[:, j * BL:(j + 1) * BL], bcs[:, j * BL:(j + 1) * BL], AF.Exp, reads=[bcb], writes=[ehb],
                         scale=-1.0, bias=bcs[:, (j + 1) * BL - 1:(j + 1) * BL])
            kh, khb = self.KH.t[:, h % 3, :], self.KH.b[h % 3]
            self.TT("pool", kh[:, 0:nt], kk[:, 0:nt], eh[:, 0:nt], ALU.mult, reads=[kkb, ehb], writes=[khb])
            if sample and h == 0:
                self.tap("lf", lf[:, 0:nt], [lfb])
                self.tap("kk", kk[:, 0:nt], [kkb])
                self.tap("bcs", bcs[:, 0:nt], [bcb])
                self.tap("kh", kh[:, 0:nt], [khb])
                self.tap("eh", eh[:, 0:nt], [ehb])
                self.tap("ebl", self.EBL.t[:, h, 0:nb], [self.EBL.b[h]])
            if full:
                eb, ebb = self.tmp()
                self.ACT(eb[:, 0:nt], bcs[:, 0:nt], AF.Exp, reads=[bcb], writes=[ebb])
                enb, enbb = self.tmp()
                self.ACT(enb[:, 0:nt], bcs[:, 0:nt], AF.Exp, reads=[bcb], writes=[enbb], scale=-1.0)
                QT, KTt = self.AR.t[:, h, :], self.AR.t[:, 8 + h, :]
                qtb, ktb = self.AR.b[h], self.AR.b[8 + h]
                self.TT("pool", KTt[:, 0:nt], kk[:, 0:nt], enb[:, 0:nt], ALU.mult, reads=[kkb, enbb], writes=[ktb])
                self.TT("pool", QT[:, 0:nt], QT[:, 0:nt], eb[:, 0:nt], ALU.mult, reads=[qtb, ebb], writes=[qtb])
                if sample and h == 0:
                    self.tap("QT", QT[:, 0:nt], [qtb])
                    self.tap("KTt", KTt[:, 0:nt], [ktb])
                    self.tap("SG", self.SG.t[:, h, 0:nt], [self.SG.b[h]])

        def s2a(h):
            P.phase = "hg_s2"
            kh, khb = self.KH.t[:, h % 3, :], self.KH.b[h % 3]
            QT, KTt = self.AR.t[:, h, :], self.AR.t[:, 8 + h, :]
            qtb, ktb = self.AR.b[h], self.AR.b[8 + h]
            ki = h % 2
            KHT, khtb = self.KHT.t[:, ki, :], self.KHT.b[ki]
            tb_, tbb = self.bank()
            for j in range(nb):
                self.TR(tb_[0:BL, j * 128:(j + 1) * 128], kh[:, j * BL:(j + 1) * BL], self.ident, [khb, self.cstb], tbb, sig=(j == nb - 1))
            kht3 = KHT[:, 0:nb * 128].rearrange("p (j k) -> p j k", k=128)
            tb3 = tb_[:, 0:nb * 128].rearrange("p (j k) -> p j k", k=128)
            self.CP("act", kht3[0:BL, :, :], tb3[0:BL, :, :], reads=[tbb], writes=[khtb])
            if sample and h == 0:
                self.tap("KHT", kht3[0:BL, :, :], [khtb])
            su, sub = self.bank()
            for j in range(nb):
                self.MM(su[:, j * 128:(j + 1) * 128], kht3[0:BL, j, :], self.VT.t[0:BL, j, h * 128:(h + 1) * 128], True, True,
                        [khtb, self.VT.b[j]], sub, sig=(j == nb - 1))
            SBF3 = self.SBF.t[:, h, 0:nb * 128].rearrange("p (j v) -> p j v", v=128)
            for j in range(nb):
                Sap, Sb = state(h, j)
                if full:
                    self.CP("dve", SBF3[:, j, :], Sap, reads=[Sb], writes=[self.SBF.b[h]])
                self.STT(Sap, Sap, self.EBL.t[:, h, j:j + 1], su[:, j * 128:(j + 1) * 128], ALU.mult, ALU.add,
                         reads=[Sb, self.EBL.b[h], sub], writes=[Sb])
            ctx[("s2", h)] = SBF3

        def s2b(h):
            P.phase = "hg_s2"
            SBF3 = ctx.pop(("s2", h))
            if not full:
                return
            QT, KTt = self.AR.t[:, h, :], self.AR.t[:, 8 + h, :]
            qtb, ktb = self.AR.b[h], self.AR.b[8 + h]
            sc, scb = self.bank()
            for j in range(nb):
                self.MM(sc[0:BL, j * BL:(j + 1) * BL], KTt[:, j * BL:(j + 1) * BL], QT[:, j * BL:(j + 1) * BL], True, True,
                        [ktb, qtb], scb, sig=(j == nb - 1))
            pm, pmb = self.pt()
            pm3 = pm[:, 0:nt].rearrange("p (j t) -> p j t", t=BL)
            sc3 = sc[:, 0:nt].rearrange("p (j t) -> p j t", t=BL)
            for j in range(nb):
                self.TT("dve", pm3[0:BL, j, :], sc3[0:BL, j, :], self.mask[0:BL, 0:BL], ALU.mult, reads=[scb, self.cstb], writes=[pmb])
            bo, bob = self.bank()
            for j in range(nb):
                self.MM(bo[:, j * BL:(j + 1) * BL], self.VT.t[0:BL, j, h * 128:(h + 1) * 128], pm3[0:BL, j, :], True, False,
                        [self.VT.b[j], pmb], bob)
                self.MM(bo[:, j * BL:(j + 1) * BL], SBF3[:, j, :], QT[:, j * BL:(j + 1) * BL], False, True,
                        [self.SBF.b[h], qtb], bob, sig=(j == nb - 1))
            osq, osqb = self.tmpb()
            self.ACT(osq[:, 0:nt], bo[:, 0:nt], AF.Square, reads=[bob], writes=[osqb])
            stb, stbb = self.bank()
            self.MM(stb[:, 0:nt], self.ONESB[:, :], osq[:, 0:nt], True, True, [self.onesb, osqb], stbb, sig=True)
            ctx[("s2c", h)] = (bo, bob, stb, stbb, pm, pmb)

        def s2c(h):
            P.phase = "hg_s2"
            if not full:
                return
            bo, bob, stb, stbb, pm, pmb = ctx.pop(("s2c", h))
            r, rb = self.tmp()
            self.ACT(r[:, 0:nt], stb[:, 0:nt], AF.Ln, reads=[stbb, self.parb], writes=[rb], scale=1.0 / 128, bias=self.eps_col)
            self.ACT(r[:, 0:nt], r[:, 0:nt], AF.Exp, reads=[rb], writes=[rb], scale=-0.5)
            t2, t2b = self.tmp()
            self.STT(t2[:, 0:nt], bo[:, 0:nt], self.PAR[:, 72 + h:73 + h], r[:, 0:nt], ALU.mult, ALU.mult,
                     reads=[bob, self.parb, rb], writes=[t2b])
            self.TT("pool", self.ON.t[:, h, 0:nt], t2[:, 0:nt], self.SG.t[:, h, 0:nt], ALU.mult,
                    reads=[t2b, self.SG.b[h]], writes=[self.ON.b[h]])
            if sample and h == 0:
                self.tap("pm", pm[0:BL, 0:nt], [pmb])
                self.tap("t2", t2[:, 0:nt], [t2b])
                self.tap("ON0", self.ON.t[:, h, 0:nt], [self.ON.b[h]])

        for h in range(NH):
            phase1(h)
        SK = 2
        s1ln(0)
        for step in range(NH + SK + 1):
            if step < NH:
                s1a(step)
            if 0 <= step - SK - 1 < NH:
                s2c(step - SK - 1)
            if 0 <= step - SK < NH:
                s2a(step - SK)
            if step + 1 < NH:
                s1ln(step + 1)
            if step < NH:
                s1b(step)
            if 0 <= step - SK < NH:
                s2b(step - SK)
        if not full:
            return
        P.phase = "hg_out"
        self.out_proj("AO", nt, 1)
        if sample:
            self.tap("X1", self.X.t[:, :, 0:nt], self.X.b)

    def out_proj(self, stage, nt, widx):
        w, wb = self.load_stage(stage)
        w3 = w[:, 0:8192].rearrange("p (k n) -> p k n", k=8)
        for oc in range(FC):
            bk, bb = self.bank()
            for kc in range(8):
                self.MM(bk[:, 0:nt], w3[:, kc, oc * 128:(oc + 1) * 128], self.ON.t[:, kc, 0:nt], kc == 0, kc == 7,
                        [self.ON.b[kc], wb], bb, sig=(kc == 7))
            self.evac_mix(bk, bb, oc, nt, widx)
        self.P.phase = self.P.phase + "_post"
        self.postnorm_add(nt, widx)

    def ffn(self, l, nt):
        self.P.phase = "ffn_norm"
        self.prenorm(nt, [2 + 4 * l], [self.H])
        H = self.H
        self.P.phase = "ffn_in"
        for j in range(11):
            w, wb = self.load_stage("FI%d_%d" % (l, j))
            w4 = w[:, 0:4096].rearrange("p (a k n) -> p a k n", a=2, k=8)
            for a in range(2):
                hc = 2 * j + a
                pa, pab = self.bank()
                for kc in range(8):
                    self.MM(pa[:, 0:nt], w4[:, a, kc, 0:128], H.t[:, kc, 0:nt], kc == 0, kc == 7, [H.b[kc], wb], pab, sig=(kc == 7))
                pb, pbb = self.bank()
                for kc in range(8):
                    self.MM(pb[:, 0:nt], w4[:, a, kc, 128:256], H.t[:, kc, 0:nt], kc == 0, kc == 7, [H.b[kc], wb], pbb, sig=(kc == 7))
                sg_, sgb = self.sigmoid_from(pa[:, 0:nt], [pab], nt)
                t1, t1b = self.tmp()
                self.TT("dve", t1[:, 0:nt], pa[:, 0:nt], sg_[:, 0:nt], ALU.mult, reads=[pab, sgb], writes=[t1b])
                self.TT("dve", self.AR.t[:, hc, 0:nt], pb[:, 0:nt], t1[:, 0:nt], ALU.mult, reads=[pbb, t1b], writes=[self.AR.b[hc]])
        self.P.phase = "ffn_out"
        for q in range(4):
            w, wb = self.load_stage("FO%d_%d" % (l, q))
            w4 = w[:, 0:2 * HC * 128].rearrange("p (a k n) -> p a k n", a=2, k=HC)
            for a in range(2):
                oc = 2 * q + a
                bk, bb = self.bank()
                for hc in range(HC):
                    self.MM(bk[:, 0:nt], w4[:, a, hc, :], self.AR.t[:, hc, 0:nt], hc == 0, hc == HC - 1,
                            [self.AR.b[hc], wb], bb, sig=(hc == HC - 1))
                self.evac_mix(bk, bb, oc, nt, 3 + 4 * l)
        self.P.phase = "ffn_post"
        self.postnorm_add(nt, 3 + 4 * l, to_mix=(l == 1))

    def rope_evac(self, p1, p1b, p2, p2b, nt, parts=128):
        t1, t1b = self.tmp()
        t2, t2b = self.tmp()
        self.TT("dve", t1[:, 0:nt], p1[:, 0:nt], self.ROPE.t[:, 0, 0:nt], ALU.mult, reads=[p1b, self.ROPE.b[0]], writes=[t1b])
        self.TT("dve", t2[:, 0:nt], p2[:, 0:nt], self.ROPE.t[:, 1, 0:nt], ALU.mult, reads=[p2b, self.ROPE.b[1]], writes=[t2b])
        self.TT("pool", t1[:, 0:nt], t1[:, 0:nt], t2[:, 0:nt], ALU.add, reads=[t1b, t2b], writes=[t1b])
        return t1, t1b

    def kv_q_proj(self, nt, kcol0, vchunk0, vrows, kout=None, vout=None):
        self.P.phase = "kv_norm"
        self.prenorm(nt, [8, 4], [self.SQ_H2(), self.H])
        self.P.phase = "kvq_proj"
        HK = self.H2
        H = self.H
        w, wb = self.load_stage("KVa")
        w3 = w[:, 0:8192].rearrange("p (k n) -> p k n", k=8)
        for kvh in range(4):
            p1, p1b = self.bank()
            for kc in range(8):
                self.MM(p1[:, 0:nt], w3[:, kc, kvh * 128:(kvh + 1) * 128], HK.t[:, kc, 0:nt], kc == 0, kc == 7, [HK.b[kc], wb], p1b, sig=(kc == 7))
            p2, p2b = self.bank()
            for kc in range(8):
                self.MM(p2[:, 0:nt], w3[:, kc, 512 + kvh * 128:512 + (kvh + 1) * 128], HK.t[:, kc, 0:nt], kc == 0, kc == 7, [HK.b[kc], wb], p2b, sig=(kc == 7))
            kf, kfb = self.rope_evac(p1, p1b, p2, p2b, nt)
            self.CP("act", self.KT.t[:, kvh, kcol0:kcol0 + nt], kf[:, 0:nt], reads=[kfb], writes=[self.KT.b[kvh]])
            if kout is not None:
                kout(kvh, kf, kfb)
        w, wb = self.load_stage("KVb")
        w3 = w[:, 0:2048].rearrange("p (k n) -> p k n", k=8)
        nch = nt // vrows
        for i in range(nch):
            bk, bb = self.bank()
            for kc in range(8):
                self.MM(bk[0:vrows, 0:256], HK.t[:, kc, i * vrows:(i + 1) * vrows], w3[:, kc, :], kc == 0, kc == 7, [HK.b[kc], wb], bb, sig=(kc == 7))
            self.CP("act", self.VA.t[0:vrows, vchunk0 + i, :], bk[0:vrows, 0:256], reads=[bb], writes=[self.VA.b[vchunk0 + i]])
            if vout is not None:
                vout(i, bk, bb)
        for c4 in range(4):
            w, wb = self.load_stage("Q%d" % c4)
            w4 = w[:, 0:4096].rearrange("p (a k n) -> p a k n", a=2, k=8)
            for a in range(2):
                c = 2 * c4 + a
                p1, p1b = self.bank()
                for kc in range(8):
                    self.MM(p1[:, 0:nt], w4[:, a, kc, 0:128], H.t[:, kc, 0:nt], kc == 0, kc == 7, [H.b[kc], wb], p1b, sig=(kc == 7))
                p2, p2b = self.bank()
                for kc in range(8):
                    self.MM(p2[:, 0:nt], w4[:, a, kc, 128:256], H.t[:, kc, 0:nt], kc == 0, kc == 7, [H.b[kc], wb], p2b, sig=(kc == 7))
                qf, qfb = self.rope_evac(p1, p1b, p2, p2b, nt)
                self.CP("act", self.AR.t[:, c, 0:nt], qf[:, 0:nt], reads=[qfb], writes=[self.AR.b[c]])

    def SQ_H2(self):
        class V:
            pass
        v = V()
        v.t = self.AR.t[:, 8:16, :]
        v.b = self.AR.b[8:16]
        self.H2 = v
        return v

    def attn_group(self, kvh, base, q_rhs, q_reads, nq, keyblocks, O_ap, D_ap, ob, db):
        nk = len(keyblocks)
        sb_, sbb = self.bank()
        N = 2 * nq
        for i, (kT, kr, vv, vr, ns) in enumerate(keyblocks):
            self.MM(sb_[0:ns, i * 128:i * 128 + N], kT, q_rhs, True, True, kr + q_reads, sbb, sig=(i == nk - 1))
        pt, ptb = self.pt()
        nsmax = max(kb[4] for kb in keyblocks)
        if all(kb[4] == nsmax for kb in keyblocks) and N == 128:
            self.ACT(pt[0:nsmax, 0:nk * 128], sb_[0:nsmax, 0:nk * 128], AF.Exp, reads=[sbb], writes=[ptb], scale=0.125)
        else:
            for i, kb in enumerate(keyblocks):
                self.ACT(pt[0:kb[4], i * 128:i * 128 + N], sb_[0:kb[4], i * 128:i * 128 + N], AF.Exp, reads=[sbb], writes=[ptb], scale=0.125)

        def part_b():
            for i, (kT, kr, vv, vr, ns) in enumerate(keyblocks):
                self.MM(O_ap, vv, pt[0:ns, i * 128:i * 128 + N], i == 0, i == nk - 1, vr + [ptb], ob, sig=(i == nk - 1))
            for i, (kT, kr, vv, vr, ns) in enumerate(keyblocks):
                self.MM(D_ap, self.ONESB[0:ns, 0:64], pt[0:ns, i * 128:i * 128 + N], i == 0, i == nk - 1, [self.onesb, ptb], db, sig=(i == nk - 1))
        return part_b

    def attn_finish(self, kvh, Ob, obb, Db, dbb, ncols, dst_cols):
        g, nq, col0 = dst_cols
        r, rb = self.tmp()
        D4 = Db[:, 0:ncols].rearrange("p (g i t) -> p g i t", i=2, t=nq)
        O4 = Ob[:, 0:ncols].rearrange("p (g i t) -> p g i t", i=2, t=nq)
        r4 = r[:, 0:ncols].rearrange("p (g i t) -> p g i t", i=2, t=nq)
        for i in range(2):
            c = 2 * kvh + i
            self.TS("dve", r4[:, :, i, :], D4[:, :, i, :], self.esink[:, c:c + 1], None, ALU.add, None,
                    reads=[dbb, self.derb], writes=[rb])
        self.RECIP(r[:, 0:ncols], r[:, 0:ncols], reads=[rb], writes=[rb])
        for i in range(2):
            c = 2 * kvh + i
            dst = self.ON.t[:, c, col0:col0 + g * nq].rearrange("p (g t) -> p g t", t=nq)
            self.TT("dve", dst, O4[:, :, i, :], r4[:, :, i, :], ALU.mult, reads=[obb, rb], writes=[self.ON.b[c]])

    def attn_layer_main(self, nt, first, last):
        P = self.P
        nch = nt // 64

        def kout(kvh, kf, kfb):
            if last:
                P.dma("pool", self.kpT[kvh], kf[0:64, nt - 128:nt], reads=[kfb], writes=[self.youtb])

        def vout(i, bk, bb):
            if last and i >= nch - 2:
                k = i - (nch - 2)
                self.CP("dve", self.VAF.t[0:64, k, :], bk[0:64, 0:256], reads=[bb], writes=[self.VAF.b[k]])
                P.dma("pool", self.vp[k * 64:(k + 1) * 64, :], self.VAF.t[0:64, k, :], reads=[self.VAF.b[k]], writes=[self.youtb])

        self.kv_q_proj(nt, 128, 2, 64, kout, vout)
        self.P.phase = "attn"
        GQ = min(4, NTM // 128)
        DEPTH = 3
        queue = []
        for kvh in range(4):
            for g0 in range(0, nch, GQ):
                gn = min(GQ, nch - g0)
                Ob, obb = self.bank()
                Db, dbb = self.bank()
                for qi in range(gn):
                    qc = g0 + qi
                    for base in (0, 64):
                        q_rhs = self.AR.t[base:base + 64, 2 * kvh:2 * kvh + 2, qc * 64:(qc + 1) * 64]
                        q_reads = [self.AR.b[2 * kvh], self.AR.b[2 * kvh + 1]]
                        kbs = []
                        for i in (qc, qc + 1, qc + 2):
                            if first and i < 2:
                                continue
                            kbs.append((self.KT.t[base:base + 64, kvh, i * 64:(i + 1) * 64], [self.KT.b[kvh]],
                                        self.VA.t[0:64, i, kvh * 64:(kvh + 1) * 64], [self.VA.b[i]], 64))
                        cols = slice(qi * 128, (qi + 1) * 128)
                        pb = self.attn_group(kvh, base, q_rhs, q_reads, 64, kbs, Ob[base:base + 64, cols], Db[base:base + 64, cols], obb, dbb)
                        lastg = (qi == gn - 1 and base == 64)

                        def item(pb=pb, lastg=lastg, kvh=kvh, Ob=Ob, obb=obb, Db=Db, dbb=dbb, gn=gn, g0=g0):
                            pb()
                            if lastg:
                                self.attn_finish(kvh, Ob, obb, Db, dbb, gn * 128, (gn, 64, g0 * 64))
                        queue.append(item)
                        if len(queue) > DEPTH:
                            queue.pop(0)()
        while queue:
            queue.pop(0)()
        for kvh in range(4):
            self.CP("pool", self.KT.t[:, kvh, 0:128], self.KT.t[:, kvh, nt:nt + 128], reads=[self.KT.b[kvh]], writes=[self.KT.b[kvh]])
        for k in range(2):
            self.CP("pool", self.VA.t[0:64, k, :], self.VA.t[0:64, nch + k, :], reads=[self.VA.b[nch + k]], writes=[self.VA.b[k]])
        self.P.phase = "attn_out"
        self.out_proj("BO", nt, 5)

    def sample_phase(self):
        P = self.P
        nt = NSMP
        self.load_x(self.xsT, NSMP, 0, nt)
        self.load_rope(self.ropeS, NSMP, 0, nt)
        for j in range(2):
            for h in range(NH):
                P.dma("act", self.SS.t[:, j * NH + h, :], self.st0[j, h], writes=[self.SS.b[j * NH + h]])
        for j in range(2):
            for kvh in range(4):
                i = j * 4 + kvh
                P.dma("act", self.KTCF.t[0:64, i, :], self.ckT[j, kvh], writes=[self.KTCF.b[i]])
                P.dma("act", self.KTCF.t[64:128, i, :], self.ckT[j, kvh], writes=[self.KTCF.b[i]])
                self.CP("pool", self.KTC.t[:, i, :], self.KTCF.t[:, i, :], reads=[self.KTCF.b[i]], writes=[self.KTC.b[i]])
            P.dma("act", self.VCF.t[:, j, :], self.cv[j], writes=[self.VCF.b[j]])
            self.CP("pool", self.VC.t[:, j, :], self.VCF.t[:, j, :], reads=[self.VCF.b[j]], writes=[self.VC.b[j]])
            P.dma("pool", self.kc[j], self.ck[j, 32:128, :], writes=[self.youtb])
            P.dma("pool", self.vc[j], self.cv[j, 32:128, :], writes=[self.youtb])
        self.hgrn_layer(nt, 32, full=True, sample=True)
        for j in range(2):
            for h in range(NH):
                P.dma("pool", self.Ss[j, h], self.SS.t[:, j * NH + h, :], reads=[self.SS.b[j * NH + h]], writes=[self.youtb])
        self.ffn(0, nt)
        self.tap("X2", self.X.t[:, :, 0:nt], self.X.b)

        def kout(kvh, kf, kfb):
            P.dma("pool", self.ksT[kvh], kf[0:64, 0:nt], reads=[kfb], writes=[self.youtb])

        def vout(i, bk, bb):
            self.CP("dve", self.VAF.t[0:32, i, :], bk[0:32, 0:256], reads=[bb], writes=[self.VAF.b[i]])
            P.dma("pool", self.vs[i], self.VAF.t[0:32, i, :], reads=[self.VAF.b[i]], writes=[self.youtb])

        self.kv_q_proj(nt, 0, 0, 32, kout, vout)
        for kvh in range(4):
            Ob, obb = self.bank()
            Db, dbb = self.bank()
            for j in range(2):
                for base in (0, 64):
                    q_rhs = self.AR.t[base:base + 64, 2 * kvh:2 * kvh + 2, j * 32:(j + 1) * 32]
                    q_reads = [self.AR.b[2 * kvh], self.AR.b[2 * kvh + 1]]
                    ci = j * 4 + kvh
                    kbs = [(self.KTC.t[base:base + 64, ci, :], [self.KTC.b[ci]], self.VC.t[:, j, kvh * 64:(kvh + 1) * 64], [self.VC.b[j]], 128),
                           (self.KT.t[base:base + 64, kvh, j * 32:(j + 1) * 32], [self.KT.b[kvh]],
                            self.VA.t[0:32, j, kvh * 64:(kvh + 1) * 64], [self.VA.b[j]], 32)]
                    cols = slice(j * 64, (j + 1) * 64)
                    self.attn_group(kvh, base, q_rhs, q_reads, 32, kbs, Ob[base:base + 64, cols], Db[base:base + 64, cols], obb, dbb)()
            self.attn_finish(kvh, Ob, obb, Db, dbb, 128, (2, 32, 0))
        self.tap("QR", self.AR.t[:, 0:8, 0:nt], self.AR.b[0:8])
        self.tap("KR", self.KT.t[:, :, 0:nt], self.KT.b)
        self.tap("ATT", self.ON.t[:, :, 0:nt], self.ON.b)
        self.out_proj("BO", nt, 5)
        self.tap("X3", self.X.t[:, :, 0:nt], self.X.b)
        self.ffn(1, nt)
        self.store_y(self.ysT, NSMP, 0, nt)


_NC_CACHE = {}


def get_nc():
    if "nc" not in _NC_CACHE:
        b = Builder()
        _NC_CACHE["nc"] = b.build()
    return _NC_CACHE["nc"]


def make_in_maps(x_prompt, x_sample, state_hgrn, cache_k, cache_v,
                 norm_mix_pre, norm_mix_post, norm_ffn_pre, norm_ffn_post, w_ffn_in, w_ffn_out,
                 w_a_in, a_lower_bound, a_out_norm, w_a_out, kv_norm, w_kv, w_b_q, b_sinks, w_b_out):
    f = np.float32
    inp = dict(w_ffn_in=w_ffn_in, w_ffn_out=w_ffn_out, w_a_in=w_a_in, w_a_out=w_a_out, w_kv=w_kv,
               w_b_q=w_b_q, w_b_out=w_b_out)
    wall = build_wall(inp)
    par = np.zeros((128, NPAR), f)
    vecs = [norm_mix_pre[0], norm_mix_post[0], norm_ffn_pre[0], norm_ffn_post[0],
            norm_mix_pre[1], norm_mix_post[1], norm_ffn_pre[1], norm_ffn_post[1], kv_norm]
    for i, v in enumerate(vecs):
        par[:, i * 8:(i + 1) * 8] = col8(v)
    par[:, 72:80] = np.asarray(a_out_norm, f)[0].T
    par[:, 80:88] = col8(np.asarray(a_lower_bound)[0])
    par[:, 88:96] = col8(np.asarray(a_lower_bound)[1])
    sk = np.asarray(b_sinks, f)[0]
    for c in range(8):
        par[0:64, 96 + c] = sk[2 * c]
        par[64:128, 96 + c] = sk[2 * c + 1]
    par[:, 104] = EPS
    par[:, 105] = 1.0
    cst = np.zeros((128, NCONST), f)
    cst[:, 0:128] = np.eye(128, dtype=f)
    cst[:, 128:256] = np.triu(np.ones((128, 128), f))
    rm = np.ones(NTM, f)
    rm[0::128] = 0.0
    cst[:, 256:256 + NTM] = rm[None, :]
    rm32 = np.ones(64, f)
    rm32[0::32] = 0.0
    cst[:, 256 + NTM:256 + NTM + 64] = rm32[None, :]
    cst[:, 256 + NTM + 64:] = 1.0

    xp = np.asarray(x_prompt, f)
    xs = np.asarray(x_sample, f)
    ck = np.asarray(cache_k, f)
    cv = np.asarray(cache_v, f)
    st = np.asarray(state_hgrn, f)
    ropeS = rope_tables(np.concatenate([2048 + np.arange(32), 2048 + np.arange(32)]))
    in_maps = []
    zeros_pre = np.zeros((D, NPRE), f)
    for c in range(8):
        b, role = c // 2, c % 2
        t0 = 0 if role == 0 else 8192 - NMAIN
        m = {
            "xT": np.ascontiguousarray(xp[b, t0:t0 + NMAIN, :].T),
            "xpT": zeros_pre if role == 0 else np.ascontiguousarray(xp[b, 0:NPRE, :].T),
            "xsT": np.ascontiguousarray(xs[2 * c:2 * c + 2].reshape(NSMP, D).T),
            "st0": np.ascontiguousarray(st[0, 2 * c:2 * c + 2]),
            "ck": np.ascontiguousarray(ck[2 * c:2 * c + 2].reshape(2, 128, 256)),
            "cv": np.ascontiguousarray(cv[2 * c:2 * c + 2].reshape(2, 128, 256)),
            "ckT": np.ascontiguousarray(ck[2 * c:2 * c + 2].transpose(0, 2, 3, 1)),
            "ropeM": rope_tables(t0 + np.arange(NMAIN)),
            "ropeS": ropeS,
            "wall": wall,
            "par": par,
            "cst": cst,
        }
        in_maps.append(m)
    return in_maps


def kernel(**inputs):
    f = np.float32
    in_maps = make_in_maps(**inputs)
    nc = get_nc()
    res = run_bass_kernel_spmd(nc, in_maps, core_ids=list(range(8)))
    R = res.results
    y_p = np.empty((4, 8192, D), f)
    st_p = np.empty((1, 4, NH, 128, 128), f)
    k_p = np.empty((4, 128, 4, 64), f)
    v_p = np.empty((4, 128, 4, 64), f)
    y_s = np.empty((16, 32, D), f)
    st_s = np.empty((1, 16, NH, 128, 128), f)
    k_s = np.empty((16, 128, 4, 64), f)
    v_s = np.empty((16, 128, 4, 64), f)
    for c in range(8):
        b, role = c // 2, c % 2
        r = R[c]
        yT = np.asarray(r["yT"])
        if role == 0:
            y_p[b, 0:4096] = yT[:, 0:4096].T
        else:
            y_p[b, 4096:8192] = yT[:, NMAIN - 4096:NMAIN].T
            st_p[0, b] = np.asarray(r["Sp"])
            k_p[b] = np.asarray(r["kpT"]).transpose(2, 0, 1)
            v_p[b] = np.asarray(r["vp"]).reshape(128, 4, 64)
        y_s[2 * c:2 * c + 2] = np.asarray(r["ysT"]).T.reshape(2, 32, D)
        st_s[0, 2 * c:2 * c + 2] = np.asarray(r["Ss"])
        ksT = np.asarray(r["ksT"])
        knew = ksT.transpose(2, 0, 1).reshape(2, 32, 4, 64)
        k_s[2 * c:2 * c + 2, 0:96] = np.asarray(r["kc"]).reshape(2, 96, 4, 64)
        k_s[2 * c:2 * c + 2, 96:128] = knew
        v_s[2 * c:2 * c + 2, 0:96] = np.asarray(r["vc"]).reshape(2, 96, 4, 64)
        v_s[2 * c:2 * c + 2, 96:128] = np.asarray(r["vs"]).reshape(2, 32, 4, 64)
    return (y_p, y_s, st_p, st_s, k_p, v_p, k_s, v_s)
```

```python
import numpy as np
from contextlib import ExitStack
import concourse.bass as bass
import concourse.mybir as mybir
from concourse.bass_utils import run_bass_kernel_spmd

F32 = mybir.dt.float32
BF16 = mybir.dt.bfloat16
AF = mybir.ActivationFunctionType
ALU = mybir.AluOpType

D = 1024
FC = 8
NH = 8
HC = 22
NTM = 384
NMAIN = 4224
NPRE = 3968
NSMP = 64
WIN = 128
EPS = 1e-6
SLOT = 8192
NSLOT = 3
NTMP = 14
NCONST = 128 + 128 + NTM + 64 + 128
NPAR = 112


class Buf:
    __slots__ = ("name", "w", "r", "excl", "wsmall")

    def __init__(self, name, excl=False):
        self.name = name
        self.w = None
        self.r = {}
        self.excl = excl
        self.wsmall = False


class Prog:
    ENG = ["pe", "act", "dve", "pool", "sp"]
    KRING = 6

    def __init__(self):
        self.items = {e: [] for e in self.ENG}
        self.cnt = {e: 0 for e in self.ENG}
        self.waited = {}
        self.dma_i = {}
        self.dma_n = {}
        self.semkeys = set(self.ENG)
        self.phase = ""

    def _need(self, eng, deps, skip_same=True):
        for k, v in deps.items():
            if k == eng and skip_same:
                continue
            if self.waited.get((eng, k), 0) < v:
                self.items[eng].append(("wait", k, v))
                self.waited[(eng, k)] = v

    @staticmethod
    def _deps(reads, writes):
        deps = {}

        def add(d):
            if d is None:
                return
            k, v = d
            if deps.get(k, 0) < v:
                deps[k] = v

        for b in reads:
            add(b.w)
            if b.excl:
                for it in b.r.items():
                    add(it)
        for b in writes:
            add(b.w)
            for it in b.r.items():
                add(it)
        return deps

    def op(self, eng, fn, reads=(), writes=(), sig=True, small=False):
        self._need(eng, self._deps(reads, writes))
        if eng != "pe":
            own = 0
            for b in list(reads) + list(writes):
                if b.w is not None and b.w[0] == eng and b.w[1] > own:
                    own = b.w[1]
            if own and self.waited.get((eng, eng), 0) < own:
                self.items[eng].append(("wait", eng, own))
                self.waited[(eng, eng)] = own
        if sig:
            self.cnt[eng] += 1
            tick = self.cnt[eng]
        else:
            tick = self.cnt[eng] + 1
        self.items[eng].append(("inst", fn, sig, self.phase))
        for b in reads:
            if b.excl:
                b.w = (eng, tick)
                b.r = {}
            else:
                if b.r.get(eng, 0) < tick:
                    b.r[eng] = tick
        for b in writes:
            b.w = (eng, tick)
            b.r = {}
            b.wsmall = small

    def dma(self, q, out_ap, in_ap, reads=(), writes=()):
        i = self.dma_i.get(q, 0)
        self.dma_i[q] = i + 1
        semkey = ("d", q, i % self.KRING)
        self.semkeys.add(semkey)
        n_prev = self.dma_n.get(semkey, 0)
        deps = self._deps(reads, writes)
        if n_prev:
            deps[semkey] = 16 * n_prev
        self._need(q, deps, skip_same=False)
        self.dma_n[semkey] = n_prev + 1
        val = 16 * (n_prev + 1)
        self.items[q].append(("dma", out_ap, in_ap, semkey))
        for b in reads:
            b.r[semkey] = val
        for b in writes:
            b.w = (semkey, val)
            b.r = {}
            b.wsmall = False

    def finish(self, eng="sp"):
        deps = {k: 16 * n for k, n in self.dma_n.items()}
        self._need(eng, deps)

    def materialize(self, nc, st):
        sems = {}
        for i, k in enumerate(sorted(self.semkeys, key=str)):
            sems[k] = st.enter_context(nc.semaphore("s%d" % i))
        block = st.enter_context(nc.Block())
        starters = {"pe": block.tensor, "act": block.scalar, "dve": block.vector,
                    "pool": block.gpsimd, "sp": block.sync}
        for e in self.ENG:
            items = self.items[e]
            if not items:
                continue

            def body(eng, items=items, e=e):
                for it in items:
                    if it[0] == "wait":
                        eng.wait_ge(sems[it[1]], it[2])
                    elif it[0] == "inst":
                        ins = it[1](eng)
                        if it[2]:
                            ins.then_inc(sems[e], 1)
                    else:
                        eng.dma_start(out=it[1], in_=it[2]).then_inc(sems[it[3]], 16)

            starters[e](body)


class CB:
    def __init__(self, nc, st, name, n, w, dtype, parts=128):
        self.t = st.enter_context(nc.sbuf_tensor(name, [parts, n, w], dtype))
        self.b = [Buf("%s%d" % (name, i)) for i in range(n)]
        self.n = n


def stage_table():
    stages = []

    def add(name, cols):
        stages.append((name, cols))

    add("PF", 8 * 1024)
    add("AV", 8 * 1024)
    for h in range(NH):
        add("A%d" % h, 8 * 384)
    add("AO", 8 * 1024)
    for l in range(2):
        for j in range(11):
            add("FI%d_%d" % (l, j), 2 * 8 * 256)
        for q in range(4):
            add("FO%d_%d" % (l, q), 2 * HC * 128)
        if l == 0:
            add("KVa", 8 * 1024)
            add("KVb", 8 * 256)
            for c in range(4):
                add("Q%d" % c, 2 * 8 * 256)
            add("BO", 8 * 1024)
    off = {}
    o = 0
    for name, cols in stages:
        off[name] = (o, cols)
        o += cols
    return stages, off, o


STAGES, STOFF, TOTC = stage_table()
CHUNK_BOUNDS = []


def _mk_chunks():
    groups = [["PF", "AV"], ["A%d" % h for h in range(NH)] + ["AO"],
              ["FI0_%d" % j for j in range(11)], ["FO0_%d" % q for q in range(4)],
              ["KVa", "KVb"] + ["Q%d" % c for c in range(4)] + ["BO"],
              ["FI1_%d" % j for j in range(11)], ["FO1_%d" % q for q in range(4)]]
    res = []
    for g in groups:
        a = STOFF[g[0]][0]
        b = STOFF[g[-1]][0] + STOFF[g[-1]][1]
        res.append((a, b, g))
    return res


CHUNKS = _mk_chunks()


def kc_block(W):
    return W.reshape(8, 128, -1).transpose(1, 0, 2)


def build_wall(inp):
    f = np.float32
    w_a_in = np.asarray(inp["w_a_in"], f)[0]
    w_a_out = np.asarray(inp["w_a_out"], f)[0]
    w_kv = np.asarray(inp["w_kv"], f)
    w_q = np.asarray(inp["w_b_q"], f)[0]
    w_bo = np.asarray(inp["w_b_out"], f)[0]
    wall = np.empty((128, TOTC), f)

    def put(name, arr):
        o, c = STOFF[name]
        wall[:, o:o + c] = arr.reshape(128, c)

    put("PF", kc_block(w_a_in[:, 1024:2048]))
    put("AV", kc_block(w_a_in[:, 2048:3072]))
    for h in range(NH):
        s = slice(h * 128, (h + 1) * 128)
        W = np.concatenate([w_a_in[:, 0:1024][:, s], w_a_in[:, 1024:2048][:, s],
                            w_a_in[:, 3072:4096][:, s]], axis=1)
        put("A%d" % h, kc_block(W))
    put("AO", kc_block(w_a_out))
    for l in range(2):
        Wi = np.asarray(inp["w_ffn_in"], f)[l]
        Wo = np.asarray(inp["w_ffn_out"], f)[l].reshape(HC, 128, 1024).transpose(1, 0, 2)
        for j in range(11):
            blks = []
            for hc in (2 * j, 2 * j + 1):
                W = np.concatenate([Wi[:, hc * 128:(hc + 1) * 128],
                                    Wi[:, 2816 + hc * 128:2816 + (hc + 1) * 128]], axis=1)
                blks.append(kc_block(W))
            put("FI%d_%d" % (l, j), np.stack(blks, axis=1))
        for q in range(4):
            blks = [Wo[:, :, oc * 128:(oc + 1) * 128] for oc in (2 * q, 2 * q + 1)]
            put("FO%d_%d" % (l, q), np.stack(blks, axis=1))
    perm = np.concatenate([np.arange(32, 64), np.arange(0, 32)])
    kd, kp = [], []
    for kvh in range(4):
        Kh = w_kv[:, kvh * 64:(kvh + 1) * 64]
        kd += [Kh, Kh]
        kp += [Kh[:, perm], Kh[:, perm]]
    put("KVa", kc_block(np.concatenate(kd + kp, axis=1)))
    put("KVb", kc_block(w_kv[:, 256:512]))
    for c4 in range(4):
        blks = []
        for c in (2 * c4, 2 * c4 + 1):
            Wc = w_q[:, c * 128:(c + 1) * 128]
            Wp = np.concatenate([Wc[:, 0:64][:, perm], Wc[:, 64:128][:, perm]], axis=1)
            blks.append(kc_block(np.concatenate([Wc, Wp], axis=1)))
        put("Q%d" % c4, np.stack(blks, axis=1))
    put("BO", kc_block(w_bo))
    return wall


def rope_tables(pos):
    f = np.float32
    half = 32
    inv = 10000.0 ** (-np.arange(half, dtype=np.float64) / half)
    ang = pos.astype(np.float64)[None, :] * inv[:, None]
    cos = np.cos(ang).astype(f)
    sin = np.sin(ang).astype(f)
    p = np.arange(128)
    d = p % 64
    fi = d % 32
    sign = np.where(d < 32, -1.0, 1.0).astype(f)
    return np.stack([cos[fi], sin[fi] * sign[:, None]], axis=0).astype(f)


def col8(v):
    return np.asarray(v, np.float32).reshape(8, 128).T


class Builder:
    def __init__(self, do_pre=True, do_main=True, do_sample=True, debug=False):
        self.do_pre, self.do_main, self.do_sample, self.debug = do_pre, do_main, do_sample, debug
        self.taps = {}
        self.nc = bass.Bass("TRN2", target_bir_lowering=False)
        self.P = Prog()
        self.st = ExitStack()
        self.bank_i = 0
        self.tmp_i = 0
        self.tmpb_i = 0
        self.pt_i = 0
        self.slot_i = 0

    def dram_in(self, name, shape, dtype=F32):
        return self.nc.dram_tensor(name, list(shape), dtype, kind="ExternalInput").ap()

    def dram_out(self, name, shape, dtype=F32):
        return self.nc.dram_tensor(name, list(shape), dtype, kind="ExternalOutput").ap()

    def tap(self, name, ap, reads):
        if not self.debug or name in self.taps:
            return
        shape = list(ap.shape)
        d = self.nc.dram_tensor("tap_" + name, shape, F32, kind="ExternalOutput").ap()
        self.taps[name] = shape
        self.P.dma("pool", d, ap, reads=reads, writes=[self.youtb])

    def sb(self, name, shape, dtype):
        return self.st.enter_context(self.nc.sbuf_tensor(name, list(shape), dtype))

    def bank(self):
        i = self.bank_i % 8
        self.bank_i += 1
        return self.banks[i], self.bankb[i]

    def tmp(self):
        i = self.tmp_i % NTMP
        self.tmp_i += 1
        return self.TP.t[:, i, :], self.TP.b[i]

    def tmpb(self):
        i = self.tmpb_i % 4
        self.tmpb_i += 1
        return self.TB.t[:, i, :], self.TB.b[i]

    def pt(self):
        i = self.pt_i % 4
        self.pt_i += 1
        return self.PT.t[:, i, :], self.PT.b[i]

    @staticmethod
    def _small(ap):
        n = 1
        for d in ap.shape[1:]:
            n *= d
        return n <= 128

    def MM(self, out, lhsT, rhs, start, stop, reads, bankb, sig=False):
        self.P.op("pe", lambda e: e.matmul(out, lhsT=lhsT, rhs=rhs, start=start, stop=stop),
                  reads=reads, writes=[bankb], sig=sig)

    def TR(self, out, in_, ident, reads, bankb, sig=False):
        self.P.op("pe", lambda e: e.transpose(out, in_, ident), reads=reads, writes=[bankb], sig=sig)

    def ACT(self, out, in_, func, reads, writes, scale=None, bias=None):
        kw = {}
        if scale is not None:
            kw["scale"] = scale
        if bias is not None:
            kw["bias"] = bias
        self.P.op("act", lambda e: e.activation(out=out, in_=in_, func=func, **kw),
                  reads=reads, writes=writes, small=self._small(out))

    def TT(self, eng, out, in0, in1, op, reads, writes):
        self.P.op(eng, lambda e: e.tensor_tensor(out=out, in0=in0, in1=in1, op=op),
                  reads=reads, writes=writes, small=self._small(out))

    def TS(self, eng, out, in0, s1, s2, op0, op1, reads, writes):
        if op1 is None:
            self.P.op(eng, lambda e: e.tensor_scalar(out=out, in0=in0, scalar1=s1, scalar2=None, op0=op0),
                      reads=reads, writes=writes, small=self._small(out))
        else:
            self.P.op(eng, lambda e: e.tensor_scalar(out=out, in0=in0, scalar1=s1, scalar2=s2, op0=op0, op1=op1),
                      reads=reads, writes=writes, small=self._small(out))

    def STT(self, out, in0, scalar, in1, op0, op1, reads, writes):
        self.P.op("dve", lambda e: e.scalar_tensor_tensor(out=out, in0=in0, scalar=scalar, in1=in1,
                                                          op0=op0, op1=op1), reads=reads, writes=writes,
                  small=self._small(out))

    def CP(self, eng, out, in_, reads, writes):
        if eng == "act":
            self.P.op("act", lambda e: e.copy(out=out, in_=in_), reads=reads, writes=writes, small=self._small(out))
        else:
            self.P.op(eng, lambda e: e.tensor_copy(out=out, in_=in_), reads=reads, writes=writes, small=self._small(out))

    def RECIP(self, out, in_, reads, writes):
        self.P.op("dve", lambda e: e.reciprocal(out=out, in_=in_), reads=reads, writes=writes, small=self._small(out))

    def load_stage(self, name):
        o, c = STOFF[name]
        i = self.slot_i % NSLOT
        self.slot_i += 1
        ck = None
        for k, (a, b, g) in enumerate(CHUNKS):
            if a <= o < b:
                ck = self.chunkb[k]
        self.P.dma("sp", self.ring[i][:, 0:c], self.WB[:, o:o + c], reads=[ck], writes=[self.ringb[i]])
        return self.ring[i], self.ringb[i]

    def build(self):
        nc, P = self.nc, self.P
        self.xT = self.dram_in("xT", [D, NMAIN])
        self.xpT = self.dram_in("xpT", [D, NPRE])
        self.xsT = self.dram_in("xsT", [D, NSMP])
        self.st0 = self.dram_in("st0", [2, NH, 128, 128])
        self.ck = self.dram_in("ck", [2, 128, 256])
        self.cv = self.dram_in("cv", [2, 128, 256])
        self.ckT = self.dram_in("ckT", [2, 4, 64, 128])
        self.ropeM = self.dram_in("ropeM", [2, 128, NMAIN])
        self.ropeS = self.dram_in("ropeS", [2, 128, NSMP])
        self.wall = self.dram_in("wall", [128, TOTC])
        self.par = self.dram_in("par", [128, NPAR])
        self.cst = self.dram_in("cst", [128, NCONST])
        self.WB = nc.dram_tensor("WB", [128, TOTC], BF16, kind="Internal").ap()

        self.yT = self.dram_out("yT", [D, NMAIN])
        self.ysT = self.dram_out("ysT", [D, NSMP])
        self.Sp = self.dram_out("Sp", [NH, 128, 128])
        self.Ss = self.dram_out("Ss", [2, NH, 128, 128])
        self.kpT = self.dram_out("kpT", [4, 64, 128])
        self.vp = self.dram_out("vp", [128, 256])
        self.ksT = self.dram_out("ksT", [4, 64, NSMP])
        self.vs = self.dram_out("vs", [2, 32, 256])
        self.kc = self.dram_out("kc", [2, 96, 256])
        self.vc = self.dram_out("vc", [2, 96, 256])

        st = self.st
        self.banks = [st.enter_context(nc.psum_tensor("pb%d" % i, [128, 512], F32)) for i in range(8)]
        self.bankb = [Buf("pb%d" % i, excl=True) for i in range(8)]
        self.X = CB(nc, st, "X", FC, NTM, F32)
        self.H = CB(nc, st, "H", FC, NTM, BF16)
        self.SQ = CB(nc, st, "SQ", FC, NTM, BF16)
        self.MIX = CB(nc, st, "MIX", FC, NTM, F32)
        self.AR = CB(nc, st, "AR", 24, NTM, BF16)
        self.SG = CB(nc, st, "SG", FC, NTM, BF16)
        self.ON = CB(nc, st, "ON", FC, NTM, BF16)
        self.VT = CB(nc, st, "VT", NTM // 128, 1024, BF16)
        self.TP = CB(nc, st, "TP", NTMP, NTM, F32)
        self.TB = CB(nc, st, "TB", 4, NTM, BF16)
        self.PT = CB(nc, st, "PT", 4, NTM, BF16)
        self.KH = CB(nc, st, "KH", 3, NTM, F32)
        self.LF = CB(nc, st, "LF", 1, NTM, F32)
        self.KHT = CB(nc, st, "KHT", 2, NTM, BF16)
        self.S = CB(nc, st, "S", NH, 128, F32)
        self.SS = CB(nc, st, "SS", 2 * NH, 128, F32)
        self.SBF = CB(nc, st, "SBF", NH, NTM, BF16)
        self.EBL = CB(nc, st, "EBL", NH, 4, F32)
        self.KT = CB(nc, st, "KT", 4, 128 + NTM, BF16)
        self.VA = CB(nc, st, "VA", 2 + NTM // 64, 256, BF16, parts=64)
        self.VAF = CB(nc, st, "VAF", 2, 256, F32, parts=64)
        self.KTC = CB(nc, st, "KTC", 8, 128, BF16)
        self.KTCF = CB(nc, st, "KTCF", 8, 128, F32)
        self.VC = CB(nc, st, "VC", 2, 256, BF16)
        self.VCF = CB(nc, st, "VCF", 2, 256, F32)
        self.ROPE = CB(nc, st, "ROPE", 2, NTM, F32)
        self.ring = [self.sb("ring%d" % i, [128, SLOT], BF16) for i in range(NSLOT)]
        self.ringb = [Buf("ring%d" % i) for i in range(NSLOT)]
        self.CST = self.sb("CST", [128, NCONST], F32)
        self.cstb = Buf("cst")
        self.PAR = self.sb("PAR", [128, NPAR], F32)
        self.parb = Buf("par")
        self.DER = self.sb("DER", [128, 40], F32)
        self.derb = Buf("der")
        self.ONESB = self.sb("ONESB", [128, 128], BF16)
        self.onesb = Buf("onesb")
        self.chunkb = [Buf("wchunk%d" % k) for k in range(len(CHUNKS))]
        self.youtb = Buf("yout")

        self.ident = self.CST[:, 0:128]
        self.mask = self.CST[:, 128:256]
        self.rmask128 = self.CST[:, 256:256 + NTM]
        self.rmask32 = self.CST[:, 256 + NTM:256 + NTM + 64]
        onesf = self.CST[:, 256 + NTM + 64:256 + NTM + 64 + 128]

        PIECE = 8192
        self.pieces = []
        o = 0
        while o < TOTC:
            e = min(TOTC, o + PIECE)
            self.pieces.append((o, e, Buf("wpiece%d" % o)))
            o = e
        self.piece_next = 0
        P.dma("act", self.CST[:, :], self.cst, writes=[self.cstb])
        P.dma("act", self.PAR[:, :], self.par, writes=[self.parb])
        self.CP("dve", self.ONESB[:, :], onesf, reads=[self.cstb], writes=[self.onesb])
        DER = self.DER
        a0 = self.PAR[:, 80:88]
        a1 = self.PAR[:, 88:96]
        self.one_col = self.PAR[:, 105:106]
        self.eps_col = self.PAR[:, 104:105]
        self.TT("dve", DER[:, 0:8], a1, a0, ALU.subtract, reads=[self.parb], writes=[self.derb])
        self.ACT(DER[:, 0:8], DER[:, 0:8], AF.Exp, reads=[self.derb], writes=[self.derb])
        self.ACT(DER[:, 0:8], DER[:, 0:8], AF.Ln, reads=[self.derb, self.parb], writes=[self.derb],
                 bias=self.one_col, scale=1.0)
        self.ACT(DER[:, 0:8], DER[:, 0:8], AF.Exp, reads=[self.derb], writes=[self.derb], scale=-1.0)
        self.TS("dve", DER[:, 8:16], DER[:, 0:8], -1.0, 1.0, ALU.mult, ALU.add, reads=[self.derb], writes=[self.derb])
        self.TS("dve", DER[:, 16:24], DER[:, 8:16], -1.0, None, ALU.mult, None, reads=[self.derb], writes=[self.derb])
        self.ACT(DER[:, 24:32], self.PAR[:, 96:104], AF.Exp, reads=[self.parb, self.derb], writes=[self.derb])
        self.lb, self.oml, self.noml, self.esink = DER[:, 0:8], DER[:, 8:16], DER[:, 16:24], DER[:, 24:32]
        self.tap("DER", DER[:, 0:32], [self.derb])

        if self.do_sample:
            self.sample_loads()
        for h in range(NH):
            self.P.op("dve", lambda e, h=h: e.memset(self.S.t[:, h, :], 0.0), writes=[self.S.b[h]])

        t0 = 0
        npre_tiles = (NPRE + NTM - 1) // NTM
        per_tile = (len(self.pieces) - 2 + max(1, npre_tiles - 2) - 1) // max(1, npre_tiles - 2)
        while t0 < NPRE and self.do_pre:
            nt = min(NTM, NPRE - t0)
            self.load_x(self.xpT, NPRE, t0, nt)
            if t0 == 0:
                self.emit_conv(2, after=self.X.b)
            else:
                self.emit_conv(per_tile)
            self.hgrn_layer(nt, 128, full=False)
            t0 += nt
        t0 = 0
        ti = 0
        while t0 < NMAIN and self.do_main:
            nt = min(NTM, NMAIN - t0)
            last = (t0 + nt == NMAIN)
            self.load_x(self.xT, NMAIN, t0, nt)
            self.emit_conv(len(self.pieces))
            self.load_rope(self.ropeM, NMAIN, t0, nt)
            self.hgrn_layer(nt, 128, full=True)
            self.ffn(0, nt)
            self.attn_layer_main(nt, first=(ti == 0), last=last)
            self.ffn(1, nt)
            self.store_y(self.yT, NMAIN, t0, nt)
            t0 += nt
            ti += 1
        for h in range(NH):
            P.dma("pool", self.Sp[h], self.S.t[:, h, :], reads=[self.S.b[h]], writes=[self.youtb])
        if self.do_sample:
            self.emit_conv(len(self.pieces))
            self.sample_phase()
        P.finish("sp")
        P.materialize(nc, st)
        return nc

    def _fix_chunk_deps(self):
        P = self.P
        i = 0
        cnt = {}
        self.chunk_deps = []
        for k, (a, b, g) in enumerate(CHUNKS):
            deps = {}
            for o in range(a, b, 16384):
                semkey = ("d", "pool", i % P.KRING)
                cnt[semkey] = cnt.get(semkey, 0) + 1
                deps[semkey] = 16 * cnt[semkey]
                i += 1
            self.chunk_deps.append(deps)
        self.chunk_bufs = []
        for k, deps in enumerate(self.chunk_deps):
            bl = []
            for sk, v in deps.items():
                b = Buf("wchunk%d_%s" % (k, sk[2]))
                b.w = (sk, v)
                bl.append(b)
            self.chunk_bufs.append(bl)

    def emit_conv(self, n, after=()):
        for _ in range(n):
            if self.piece_next >= len(self.pieces):
                return
            a, b, pb = self.pieces[self.piece_next]
            self.piece_next += 1
            self.P.dma("pool", self.WB[:, a:b], self.wall[:, a:b], reads=list(after), writes=[pb])

    def load_stage(self, name):
        o, c = STOFF[name]
        i = self.slot_i % NSLOT
        self.slot_i += 1
        bl = [pb for (a, b, pb) in self.pieces if a < o + c and b > o]
        for (a, b, pb) in self.pieces:
            if a < o + c and b > o:
                assert pb.w is not None, "conversion piece for %s not issued yet" % name
        self.P.dma("sp", self.ring[i][:, 0:c], self.WB[:, o:o + c], reads=bl, writes=[self.ringb[i]])
        return self.ring[i], self.ringb[i]

    def _old_load_stage(self, name):
        o, c = STOFF[name]
        i = self.slot_i % NSLOT
        self.slot_i += 1
        bl = None
        for k, (a, b, g) in enumerate(CHUNKS):
            if a <= o < b:
                bl = self.chunk_bufs[k]
        self.P.dma("sp", self.ring[i][:, 0:c], self.WB[:, o:o + c], reads=bl, writes=[self.ringb[i]])
        return self.ring[i], self.ringb[i]

    def load_x(self, src, ntot, t0, nt):
        v = src.rearrange("(c p) t -> p c t", p=128)
        for c in range(FC):
            self.P.dma("act", self.X.t[:, c, 0:nt], v[:, c, t0:t0 + nt], writes=[self.X.b[c]])

    def load_rope(self, src, ntot, t0, nt):
        v = src.rearrange("a p t -> p a t")
        self.P.dma("act", self.ROPE.t[:, :, 0:nt], v[:, :, t0:t0 + nt], writes=self.ROPE.b)

    def store_y(self, dst, ntot, t0, nt):
        v = dst.rearrange("(c p) t -> p c t", p=128)
        self.P.dma("pool", v[:, :, t0:t0 + nt], self.MIX.t[:, :, 0:nt], reads=self.MIX.b, writes=[self.youtb])

    def stats_rstd(self, srcs, nt, scale):
        bk, bb = self.bank()
        n = len(srcs)
        for i, (ap, b) in enumerate(srcs):
            self.MM(bk[:, 0:nt], self.ONESB[:, :], ap, i == 0, i == n - 1, [self.onesb, b], bb, sig=(i == n - 1))
        r, rb = self.tmp()
        self.ACT(r[:, 0:nt], bk[:, 0:nt], AF.Ln, reads=[bb, self.parb], writes=[rb], scale=scale, bias=self.eps_col)
        self.ACT(r[:, 0:nt], r[:, 0:nt], AF.Exp, reads=[rb], writes=[rb], scale=-0.5)
        return r, rb

    def norm_sq(self, nt):
        for c in range(FC):
            self.ACT(self.SQ.t[:, c, 0:nt], self.X.t[:, c, 0:nt], AF.Square, reads=[self.X.b[c]], writes=[self.SQ.b[c]])

    def prenorm(self, nt, widx_list, dsts):
        self.norm_sq(nt)
        r, rb = self.stats_rstd([(self.SQ.t[:, c, 0:nt], self.SQ.b[c]) for c in range(FC)], nt, 1.0 / D)
        for widx, dst in zip(widx_list, dsts):
            for c in range(FC):
                self.STT(dst.t[:, c, 0:nt], self.X.t[:, c, 0:nt], self.PAR[:, widx * 8 + c:widx * 8 + c + 1],
                         r[:, 0:nt], ALU.mult, ALU.mult, reads=[self.X.b[c], self.parb, rb], writes=[dst.b[c]])

    def evac_mix(self, bk, bb, oc, nt, widx):
        self.ACT(self.MIX.t[:, oc, 0:nt], bk[:, 0:nt], AF.Copy, reads=[bb, self.parb], writes=[self.MIX.b[oc]],
                 scale=self.PAR[:, widx * 8 + oc:widx * 8 + oc + 1])
        self.ACT(self.SQ.t[:, oc, 0:nt], bk[:, 0:nt], AF.Square, reads=[bb], writes=[self.SQ.b[oc]])

    def postnorm_add(self, nt, widx, to_mix=False):
        r, rb = self.stats_rstd([(self.SQ.t[:, c, 0:nt], self.SQ.b[c]) for c in range(FC)], nt, 1.0 / D)
        dst = self.MIX if to_mix else self.X
        for c in range(FC):
            t, tb = self.tmp()
            self.TT("dve", t[:, 0:nt], self.MIX.t[:, c, 0:nt], r[:, 0:nt], ALU.mult, reads=[self.MIX.b[c], rb], writes=[tb])
            self.TT("pool" if c % 2 == 0 else "dve", dst.t[:, c, 0:nt], self.X.t[:, c, 0:nt], t[:, 0:nt], ALU.add,
                    reads=[tb, self.X.b[c]], writes=[dst.b[c]])

    def recip_sig(self, src_ap, src_reads, nt):
        t, tb = self.tmp()
        self.ACT(t[:, 0:nt], src_ap, AF.Exp, reads=src_reads, writes=[tb], scale=-1.0)
        self.TS("dve", t[:, 0:nt], t[:, 0:nt], 1.0, None, ALU.add, None, reads=[tb], writes=[tb])
        self.RECIP(t[:, 0:nt], t[:, 0:nt], reads=[tb], writes=[tb])
        return t, tb

    def sigmoid_from(self, src_ap, src_reads, nt):
        t, tb = self.tmp()
        self.ACT(t[:, 0:nt], src_ap, AF.Exp, reads=src_reads, writes=[tb], scale=-1.0)
        self.ACT(t[:, 0:nt], t[:, 0:nt], AF.Ln, reads=[tb, self.parb], writes=[tb], bias=self.one_col, scale=1.0)
        self.ACT(t[:, 0:nt], t[:, 0:nt], AF.Exp, reads=[tb], writes=[tb], scale=-1.0)
        return t, tb

    def hgrn_layer(self, nt, BL, full, sample=False):
        P = self.P
        nb = nt // BL
        rmask = self.rmask32 if BL == 32 else self.rmask128
        P.phase = "hg_norm"
        self.prenorm(nt, [0], [self.H])
        H = self.H
        if sample:
            self.tap("X0", self.X.t[:, :, 0:nt], self.X.b)
            self.tap("H0", H.t[:, :, 0:nt], H.b)

        def state(h, j):
            if sample:
                return self.SS.t[:, j * NH + h, :], self.SS.b[j * NH + h]
            return self.S.t[:, h, :], self.S.b[h]

        P.phase = "hg_v"
        wv, wvb = self.load_stage("AV")
        wv3 = wv[:, 0:8192].rearrange("p (k n) -> p k n", k=8)
        for j in range(nb):
            for half in range(2):
                bk, bb = self.bank()
                for kc in range(8):
                    self.MM(bk[0:BL, 0:512], H.t[:, kc, j * BL:(j + 1) * BL], wv3[:, kc, half * 512:(half + 1) * 512],
                            kc == 0, kc == 7, [H.b[kc], wvb], bb, sig=(kc == 7))
                self.CP("act" if half == 0 else "dve", self.VT.t[0:BL, j, half * 512:(half + 1) * 512], bk[0:BL, 0:512],
                        reads=[bb], writes=[self.VT.b[j]])
        if sample:
            self.tap("VT", self.VT.t[0:BL, 0:nb, :], self.VT.b)
        if not full:
            wf, wfb = self.load_stage("PF")
            wf3 = wf[:, 0:8192].rearrange("p (k n) -> p k n", k=8)

        SF = self.MIX

        def phase1(h):
            P.phase = "hg_p1"
            if full:
                wa, wab = self.load_stage("A%d" % h)
                wa3 = wa[:, 0:3072].rearrange("p (k n) -> p k n", k=8)
            pf, pfb = self.bank()
            for kc in range(8):
                lw = wa3[:, kc, 128:256] if full else wf3[:, kc, h * 128:(h + 1) * 128]
                self.MM(pf[:, 0:nt], lw, H.t[:, kc, 0:nt], kc == 0, kc == 7, [H.b[kc], wab if full else wfb], pfb, sig=(kc == 7))
            self.ACT(SF.t[:, h, 0:nt], pf[:, 0:nt], AF.Sigmoid, reads=[pfb], writes=[SF.b[h]])
            if full:
                pq, pqb = self.bank()
                for kc in range(8):
                    self.MM(pq[:, 0:nt], wa3[:, kc, 0:128], H.t[:, kc, 0:nt], kc == 0, kc == 7, [H.b[kc], wab], pqb, sig=(kc == 7))
                sq_, sqb = self.tmp()
                self.ACT(sq_[:, 0:nt], pq[:, 0:nt], AF.Sigmoid, reads=[pqb], writes=[sqb])
                self.TT("dve", self.AR.t[:, h, 0:nt], pq[:, 0:nt], sq_[:, 0:nt], ALU.mult, reads=[pqb, sqb], writes=[self.AR.b[h]])
                pg, pgb = self.bank()
                for kc in range(8):
                    self.MM(pg[:, 0:nt], wa3[:, kc, 256:384], H.t[:, kc, 0:nt], kc == 0, kc == 7, [H.b[kc], wab], pgb, sig=(kc == 7))
                sg_, sgb = self.tmp()
                self.ACT(sg_[:, 0:nt], pg[:, 0:nt], AF.Sigmoid, reads=[pgb], writes=[sgb])
                self.TT("dve", self.SG.t[:, h, 0:nt], pg[:, 0:nt], sg_[:, 0:nt], ALU.mult, reads=[pgb, sgb], writes=[self.SG.b[h]])

        ctx = {}

        def s1ln(h):
            P.phase = "hg_s1"
            s, sb_ = SF.t[:, h, :], SF.b[h]
            lf, lfb = self.LF.t[:, 0, :], self.LF.b[0]
            self.ACT(lf[:, 0:nt], s[:, 0:nt], AF.Ln, reads=[sb_, self.derb], writes=[lfb],
                     scale=self.oml[:, h:h + 1], bias=self.lb[:, h:h + 1])

        def s1a(h):
            P.phase = "hg_s1"
            s, sb_ = SF.t[:, h, :], SF.b[h]
            lf, lfb = self.LF.t[:, 0, :], self.LF.b[0]
            kk, kkb = self.tmp()
            self.TS("dve", kk[:, 0:nt], s[:, 0:nt], self.noml[:, h:h + 1], self.oml[:, h:h + 1], ALU.mult, ALU.add,
                    reads=[sb_, self.derb], writes=[kkb])
            bcs, bcb = self.tmp()
            P.op("dve", lambda e, bcs=bcs, lf=lf: e.tensor_tensor_scan(out=bcs[:, 0:nt], data0=rmask[:, 0:nt], data1=lf[:, 0:nt],
                                                                        initial=0.0, op0=ALU.mult, op1=ALU.add),
                 reads=[lfb, self.cstb], writes=[bcb])
            ctx[("s1", h)] = (lf, lfb, kk, kkb, bcs, bcb)

        def s1b(h):
            P.phase = "hg_s1"
            lf, lfb, kk, kkb, bcs, bcb = ctx.pop(("s1", h))
            b3 = bcs[:, 0:nt].rearrange("p (j t) -> p j t", t=BL)
            self.ACT(self.EBL.t[:, h, 0:nb], b3[:, :, BL - 1], AF.Exp, reads=[bcb], writes=[self.EBL.b[h]])
            eh, ehb = self.tmp()
            for j in range(nb):
                self.ACT(eh[:, j * BL:(j + 1) * BL], bcs[:, j * BL:(j + 1) * BL], AF.Exp, reads=[bcb], writes=[ehb],
                         scale=-1.0, bias=bcs[:, (j + 1) * BL - 1:(j + 1) * BL])
            kh, khb = self.KH.t[:, h % 3, :], self.KH.b[h % 3]
            self.TT("pool", kh[:, 0:nt], kk[:, 0:nt], eh[:, 0:nt], ALU.mult, reads=[kkb, ehb], writes=[khb])
            if sample and h == 0:
                self.tap("lf", lf[:, 0:nt], [lfb])
                self.tap("kk", kk[:, 0:nt], [kkb])
                self.tap("bcs", bcs[:, 0:nt], [bcb])
                self.tap("kh", kh[:, 0:nt], [khb])
                self.tap("eh", eh[:, 0:nt], [ehb])
                self.tap("ebl", self.EBL.t[:, h, 0:nb], [self.EBL.b[h]])
            if full:
                eb, ebb = self.tmp()
                self.ACT(eb[:, 0:nt], bcs[:, 0:nt], AF.Exp, reads=[bcb], writes=[ebb])
                enb, enbb = self.tmp()
                self.ACT(enb[:, 0:nt], bcs[:, 0:nt], AF.Exp, reads=[bcb], writes=[enbb], scale=-1.0)
                QT, KTt = self.AR.t[:, h, :], self.AR.t[:, 8 + h, :]
                qtb, ktb = self.AR.b[h], self.AR.b[8 + h]
                self.TT("pool", KTt[:, 0:nt], kk[:, 0:nt], enb[:, 0:nt], ALU.mult, reads=[kkb, enbb], writes=[ktb])
                self.TT("pool", QT[:, 0:nt], QT[:, 0:nt], eb[:, 0:nt], ALU.mult, reads=[qtb, ebb], writes=[qtb])
                if sample and h == 0:
                    self.tap("QT", QT[:, 0:nt], [qtb])
                    self.tap("KTt", KTt[:, 0:nt], [ktb])
                    self.tap("SG", self.SG.t[:, h, 0:nt], [self.SG.b[h]])

        def s2a(h):
            P.phase = "hg_s2"
            kh, khb = self.KH.t[:, h % 3, :], self.KH.b[h % 3]
            QT, KTt = self.AR.t[:, h, :], self.AR.t[:, 8 + h, :]
            qtb, ktb = self.AR.b[h], self.AR.b[8 + h]
            ki = h % 2
            KHT, khtb = self.KHT.t[:, ki, :], self.KHT.b[ki]
            tb_, tbb = self.bank()
            for j in range(nb):
                self.TR(tb_[0:BL, j * 128:(j + 1) * 128], kh[:, j * BL:(j + 1) * BL], self.ident, [khb, self.cstb], tbb, sig=(j == nb - 1))
            kht3 = KHT[:, 0:nb * 128].rearrange("p (j k) -> p j k", k=128)
            tb3 = tb_[:, 0:nb * 128].rearrange("p (j k) -> p j k", k=128)
            self.CP("act", kht3[0:BL, :, :], tb3[0:BL, :, :], reads=[tbb], writes=[khtb])
            if sample and h == 0:
                self.tap("KHT", kht3[0:BL, :, :], [khtb])
            su, sub = self.bank()
            for j in range(nb):
                self.MM(su[:, j * 128:(j + 1) * 128], kht3[0:BL, j, :], self.VT.t[0:BL, j, h * 128:(h + 1) * 128], True, True,
                        [khtb, self.VT.b[j]], sub, sig=(j == nb - 1))
            SBF3 = self.SBF.t[:, h, 0:nb * 128].rearrange("p (j v) -> p j v", v=128)
            for j in range(nb):
                Sap, Sb = state(h, j)
                if full:
                    self.CP("dve", SBF3[:, j, :], Sap, reads=[Sb], writes=[self.SBF.b[h]])
                self.STT(Sap, Sap, self.EBL.t[:, h, j:j + 1], su[:, j * 128:(j + 1) * 128], ALU.mult, ALU.add,
                         reads=[Sb, self.EBL.b[h], sub], writes=[Sb])
            ctx[("s2", h)] = SBF3

        def s2b(h):
            P.phase = "hg_s2"
            SBF3 = ctx.pop(("s2", h))
            if not full:
                return
            QT, KTt = self.AR.t[:, h, :], self.AR.t[:, 8 + h, :]
            qtb, ktb = self.AR.b[h], self.AR.b[8 + h]
            sc, scb = self.bank()
            for j in range(nb):
                self.MM(sc[0:BL, j * BL:(j + 1) * BL], KTt[:, j * BL:(j + 1) * BL], QT[:, j * BL:(j + 1) * BL], True, True,
                        [ktb, qtb], scb, sig=(j == nb - 1))
            pm, pmb = self.pt()
            pm3 = pm[:, 0:nt].rearrange("p (j t) -> p j t", t=BL)
            sc3 = sc[:, 0:nt].rearrange("p (j t) -> p j t", t=BL)
            for j in range(nb):
                self.TT("dve", pm3[0:BL, j, :], sc3[0:BL, j, :], self.mask[0:BL, 0:BL], ALU.mult, reads=[scb, self.cstb], writes=[pmb])
            bo, bob = self.bank()
            for j in range(nb):
                self.MM(bo[:, j * BL:(j + 1) * BL], self.VT.t[0:BL, j, h * 128:(h + 1) * 128], pm3[0:BL, j, :], True, False,
                        [self.VT.b[j], pmb], bob)
                self.MM(bo[:, j * BL:(j + 1) * BL], SBF3[:, j, :], QT[:, j * BL:(j + 1) * BL], False, True,
                        [self.SBF.b[h], qtb], bob, sig=(j == nb - 1))
            osq, osqb = self.tmpb()
            self.ACT(osq[:, 0:nt], bo[:, 0:nt], AF.Square, reads=[bob], writes=[osqb])
            stb, stbb = self.bank()
            self.MM(stb[:, 0:nt], self.ONESB[:, :], osq[:, 0:nt], True, True, [self.onesb, osqb], stbb, sig=True)
            ctx[("s2c", h)] = (bo, bob, stb, stbb, pm, pmb)

        def s2c(h):
            P.phase = "hg_s2"
            if not full:
                return
            bo, bob, stb, stbb, pm, pmb = ctx.pop(("s2c", h))
            r, rb = self.tmp()
            self.ACT(r[:, 0:nt], stb[:, 0:nt], AF.Ln, reads=[stbb, self.parb], writes=[rb], scale=1.0 / 128, bias=self.eps_col)
            self.ACT(r[:, 0:nt], r[:, 0:nt], AF.Exp, reads=[rb], writes=[rb], scale=-0.5)
            t2, t2b = self.tmp()
            self.STT(t2[:, 0:nt], bo[:, 0:nt], self.PAR[:, 72 + h:73 + h], r[:, 0:nt], ALU.mult, ALU.mult,
                     reads=[bob, self.parb, rb], writes=[t2b])
            self.TT("pool", self.ON.t[:, h, 0:nt], t2[:, 0:nt], self.SG.t[:, h, 0:nt], ALU.mult,
                    reads=[t2b, self.SG.b[h]], writes=[self.ON.b[h]])
            if sample and h == 0:
                self.tap("pm", pm[0:BL, 0:nt], [pmb])
                self.tap("t2", t2[:, 0:nt], [t2b])
                self.tap("ON0", self.ON.t[:, h, 0:nt], [self.ON.b[h]])

        for h in range(NH):
            phase1(h)
        SK = 2
        s1ln(0)
        for step in range(NH + SK + 1):
            if step < NH:
                s1a(step)
            if 0 <= step - SK - 1 < NH:
                s2c(step - SK - 1)
            if 0 <= step - SK < NH:
                s2a(step - SK)
            if step + 1 < NH:
                s1ln(step + 1)
            if step < NH:
                s1b(step)
            if 0 <= step - SK < NH:
                s2b(step - SK)
        if not full:
            return
        P.phase = "hg_out"
        self.out_proj("AO", nt, 1)
        if sample:
            self.tap("X1", self.X.t[:, :, 0:nt], self.X.b)

    def out_proj(self, stage, nt, widx):
        w, wb = self.load_stage(stage)
        w3 = w[:, 0:8192].rearrange("p (k n) -> p k n", k=8)
        for oc in range(FC):
            bk, bb = self.bank()
            for kc in range(8):
                self.MM(bk[:, 0:nt], w3[:, kc, oc * 128:(oc + 1) * 128], self.ON.t[:, kc, 0:nt], kc == 0, kc == 7,
                        [self.ON.b[kc], wb], bb, sig=(kc == 7))
            self.evac_mix(bk, bb, oc, nt, widx)
        self.P.phase = self.P.phase + "_post"
        self.postnorm_add(nt, widx)

    def ffn(self, l, nt):
        self.P.phase = "ffn_norm"
        self.prenorm(nt, [2 + 4 * l], [self.H])
        H = self.H
        self.P.phase = "ffn_in"
        for j in range(11):
            w, wb = self.load_stage("FI%d_%d" % (l, j))
            w4 = w[:, 0:4096].rearrange("p (a k n) -> p a k n", a=2, k=8)
            for a in range(2):
                hc = 2 * j + a
                pa, pab = self.bank()
                for kc in range(8):
                    self.MM(pa[:, 0:nt], w4[:, a, kc, 0:128], H.t[:, kc, 0:nt], kc == 0, kc == 7, [H.b[kc], wb], pab, sig=(kc == 7))
                pb, pbb = self.bank()
                for kc in range(8):
                    self.MM(pb[:, 0:nt], w4[:, a, kc, 128:256], H.t[:, kc, 0:nt], kc == 0, kc == 7, [H.b[kc], wb], pbb, sig=(kc == 7))
                sg_, sgb = self.sigmoid_from(pa[:, 0:nt], [pab], nt)
                t1, t1b = self.tmp()
                self.TT("dve", t1[:, 0:nt], pa[:, 0:nt], sg_[:, 0:nt], ALU.mult, reads=[pab, sgb], writes=[t1b])
                self.TT("dve", self.AR.t[:, hc, 0:nt], pb[:, 0:nt], t1[:, 0:nt], ALU.mult, reads=[pbb, t1b], writes=[self.AR.b[hc]])
        self.P.phase = "ffn_out"
        for q in range(4):
            w, wb = self.load_stage("FO%d_%d" % (l, q))
            w4 = w[:, 0:2 * HC * 128].rearrange("p (a k n) -> p a k n", a=2, k=HC)
            for a in range(2):
                oc = 2 * q + a
                bk, bb = self.bank()
                for hc in range(HC):
                    self.MM(bk[:, 0:nt], w4[:, a, hc, :], self.AR.t[:, hc, 0:nt], hc == 0, hc == HC - 1,
                            [self.AR.b[hc], wb], bb, sig=(hc == HC - 1))
                self.evac_mix(bk, bb, oc, nt, 3 + 4 * l)
        self.P.phase = "ffn_post"
        self.postnorm_add(nt, 3 + 4 * l, to_mix=(l == 1))

    def rope_evac(self, p1, p1b, p2, p2b, nt, parts=128):
        t1, t1b = self.tmp()
        t2, t2b = self.tmp()
        self.TT("dve", t1[:, 0:nt], p1[:, 0:nt], self.ROPE.t[:, 0, 0:nt], ALU.mult, reads=[p1b, self.ROPE.b[0]], writes=[t1b])
        self.TT("dve", t2[:, 0:nt], p2[:, 0:nt], self.ROPE.t[:, 1, 0:nt], ALU.mult, reads=[p2b, self.ROPE.b[1]], writes=[t2b])
        self.TT("pool", t1[:, 0:nt], t1[:, 0:nt], t2[:, 0:nt], ALU.add, reads=[t1b, t2b], writes=[t1b])
        return t1, t1b

    def kv_q_proj(self, nt, kcol0, vchunk0, vrows, kout=None, vout=None):
        self.P.phase = "kv_norm"
        self.prenorm(nt, [8, 4], [self.SQ_H2(), self.H])
        self.P.phase = "kvq_proj"
        HK = self.H2
        H = self.H
        w, wb = self.load_stage("KVa")
        w3 = w[:, 0:8192].rearrange("p (k n) -> p k n", k=8)
        for kvh in range(4):
            p1, p1b = self.bank()
            for kc in range(8):
                self.MM(p1[:, 0:nt], w3[:, kc, kvh * 128:(kvh + 1) * 128], HK.t[:, kc, 0:nt], kc == 0, kc == 7, [HK.b[kc], wb], p1b, sig=(kc == 7))
            p2, p2b = self.bank()
            for kc in range(8):
                self.MM(p2[:, 0:nt], w3[:, kc, 512 + kvh * 128:512 + (kvh + 1) * 128], HK.t[:, kc, 0:nt], kc == 0, kc == 7, [HK.b[kc], wb], p2b, sig=(kc == 7))
            kf, kfb = self.rope_evac(p1, p1b, p2, p2b, nt)
            self.CP("act", self.KT.t[:, kvh, kcol0:kcol0 + nt], kf[:, 0:nt], reads=[kfb], writes=[self.KT.b[kvh]])
            if kout is not None:
                kout(kvh, kf, kfb)
        w, wb = self.load_stage("KVb")
        w3 = w[:, 0:2048].rearrange("p (k n) -> p k n", k=8)
        nch = nt // vrows
        for i in range(nch):
            bk, bb = self.bank()
            for kc in range(8):
                self.MM(bk[0:vrows, 0:256], HK.t[:, kc, i * vrows:(i + 1) * vrows], w3[:, kc, :], kc == 0, kc == 7, [HK.b[kc], wb], bb, sig=(kc == 7))
            self.CP("act", self.VA.t[0:vrows, vchunk0 + i, :], bk[0:vrows, 0:256], reads=[bb], writes=[self.VA.b[vchunk0 + i]])
            if vout is not None:
                vout(i, bk, bb)
        for c4 in range(4):
            w, wb = self.load_stage("Q%d" % c4)
            w4 = w[:, 0:4096].rearrange("p (a k n) -> p a k n", a=2, k=8)
            for a in range(2):
                c = 2 * c4 + a
                p1, p1b = self.bank()
                for kc in range(8):
                    self.MM(p1[:, 0:nt], w4[:, a, kc, 0:128], H.t[:, kc, 0:nt], kc == 0, kc == 7, [H.b[kc], wb], p1b, sig=(kc == 7))
                p2, p2b = self.bank()
                for kc in range(8):
                    self.MM(p2[:, 0:nt], w4[:, a, kc, 128:256], H.t[:, kc, 0:nt], kc == 0, kc == 7, [H.b[kc], wb], p2b, sig=(kc == 7))
                qf, qfb = self.rope_evac(p1, p1b, p2, p2b, nt)
                self.CP("act", self.AR.t[:, c, 0:nt], qf[:, 0:nt], reads=[qfb], writes=[self.AR.b[c]])

    def SQ_H2(self):
        class V:
            pass
        v = V()
        v.t = self.AR.t[:, 8:16, :]
        v.b = self.AR.b[8:16]
        self.H2 = v
        return v

    def attn_group(self, kvh, base, q_rhs, q_reads, nq, keyblocks, O_ap, D_ap, ob, db):
        nk = len(keyblocks)
        sb_, sbb = self.bank()
        N = 2 * nq
        for i, (kT, kr, vv, vr, ns) in enumerate(keyblocks):
            self.MM(sb_[0:ns, i * 128:i * 128 + N], kT, q_rhs, True, True, kr + q_reads, sbb, sig=(i == nk - 1))
        pt, ptb = self.pt()
        nsmax = max(kb[4] for kb in keyblocks)
        if all(kb[4] == nsmax for kb in keyblocks) and N == 128:
            self.ACT(pt[0:nsmax, 0:nk * 128], sb_[0:nsmax, 0:nk * 128], AF.Exp, reads=[sbb], writes=[ptb], scale=0.125)
        else:
            for i, kb in enumerate(keyblocks):
                self.ACT(pt[0:kb[4], i * 128:i * 128 + N], sb_[0:kb[4], i * 128:i * 128 + N], AF.Exp, reads=[sbb], writes=[ptb], scale=0.125)

        def part_b():
            for i, (kT, kr, vv, vr, ns) in enumerate(keyblocks):
                self.MM(O_ap, vv, pt[0:ns, i * 128:i * 128 + N], i == 0, i == nk - 1, vr + [ptb], ob, sig=(i == nk - 1))
            for i, (kT, kr, vv, vr, ns) in enumerate(keyblocks):
                self.MM(D_ap, self.ONESB[0:ns, 0:64], pt[0:ns, i * 128:i * 128 + N], i == 0, i == nk - 1, [self.onesb, ptb], db, sig=(i == nk - 1))
        return part_b

    def attn_finish(self, kvh, Ob, obb, Db, dbb, ncols, dst_cols):
        g, nq, col0 = dst_cols
        r, rb = self.tmp()
        D4 = Db[:, 0:ncols].rearrange("p (g i t) -> p g i t", i=2, t=nq)
        O4 = Ob[:, 0:ncols].rearrange("p (g i t) -> p g i t", i=2, t=nq)
        r4 = r[:, 0:ncols].rearrange("p (g i t) -> p g i t", i=2, t=nq)
        for i in range(2):
            c = 2 * kvh + i
            self.TS("dve", r4[:, :, i, :], D4[:, :, i, :], self.esink[:, c:c + 1], None, ALU.add, None,
                    reads=[dbb, self.derb], writes=[rb])
        self.RECIP(r[:, 0:ncols], r[:, 0:ncols], reads=[rb], writes=[rb])
        for i in range(2):
            c = 2 * kvh + i
            dst = self.ON.t[:, c, col0:col0 + g * nq].rearrange("p (g t) -> p g t", t=nq)
            self.TT("dve", dst, O4[:, :, i, :], r4[:, :, i, :], ALU.mult, reads=[obb, rb], writes=[self.ON.b[c]])

    def attn_layer_main(self, nt, first, last):
        P = self.P
        nch = nt // 64

        def kout(kvh, kf, kfb):
            if last:
                P.dma("pool", self.kpT[kvh], kf[0:64, nt - 128:nt], reads=[kfb], writes=[self.youtb])

        def vout(i, bk, bb):
            if last and i >= nch - 2:
                k = i - (nch - 2)
                self.CP("dve", self.VAF.t[0:64, k, :], bk[0:64, 0:256], reads=[bb], writes=[self.VAF.b[k]])
                P.dma("pool", self.vp[k * 64:(k + 1) * 64, :], self.VAF.t[0:64, k, :], reads=[self.VAF.b[k]], writes=[self.youtb])

        self.kv_q_proj(nt, 128, 2, 64, kout, vout)
        self.P.phase = "attn"
        GQ = min(4, NTM // 128)
        DEPTH = 3
        queue = []
        for kvh in range(4):
            for g0 in range(0, nch, GQ):
                gn = min(GQ, nch - g0)
                Ob, obb = self.bank()
                Db, dbb = self.bank()
                for qi in range(gn):
                    qc = g0 + qi
                    for base in (0, 64):
                        q_rhs = self.AR.t[base:base + 64, 2 * kvh:2 * kvh + 2, qc * 64:(qc + 1) * 64]
                        q_reads = [self.AR.b[2 * kvh], self.AR.b[2 * kvh + 1]]
                        kbs = []
                        for i in (qc, qc + 1, qc + 2):
                            if first and i < 2:
                                continue
                            kbs.append((self.KT.t[base:base + 64, kvh, i * 64:(i + 1) * 64], [self.KT.b[kvh]],
                                        self.VA.t[0:64, i, kvh * 64:(kvh + 1) * 64], [self.VA.b[i]], 64))
                        cols = slice(qi * 128, (qi + 1) * 128)
                        pb = self.attn_group(kvh, base, q_rhs, q_reads, 64, kbs, Ob[base:base + 64, cols], Db[base:base + 64, cols], obb, dbb)
                        lastg = (qi == gn - 1 and base == 64)

                        def item(pb=pb, lastg=lastg, kvh=kvh, Ob=Ob, obb=obb, Db=Db, dbb=dbb, gn=gn, g0=g0):
                            pb()
                            if lastg:
                                self.attn_finish(kvh, Ob, obb, Db, dbb, gn * 128, (gn, 64, g0 * 64))
                        queue.append(item)
                        if len(queue) > DEPTH:
                            queue.pop(0)()
        while queue:
            queue.pop(0)()
        for kvh in range(4):
            self.CP("pool", self.KT.t[:, kvh, 0:128], self.KT.t[:, kvh, nt:nt + 128], reads=[self.KT.b[kvh]], writes=[self.KT.b[kvh]])
        for k in range(2):
            self.CP("pool", self.VA.t[0:64, k, :], self.VA.t[0:64, nch + k, :], reads=[self.VA.b[nch + k]], writes=[self.VA.b[k]])
        self.P.phase = "attn_out"
        self.out_proj("BO", nt, 5)

    def sample_loads(self):
        P = self.P
        for j in range(2):
            for h in range(NH):
                P.dma("sp", self.SS.t[:, j * NH + h, :], self.st0[j, h], writes=[self.SS.b[j * NH + h]])
        for j in range(2):
            for kvh in range(4):
                i = j * 4 + kvh
                P.dma("sp", self.KTCF.t[0:64, i, :], self.ckT[j, kvh], writes=[self.KTCF.b[i]])
                P.dma("sp", self.KTCF.t[64:128, i, :], self.ckT[j, kvh], writes=[self.KTCF.b[i]])
                self.CP("pool", self.KTC.t[:, i, :], self.KTCF.t[:, i, :], reads=[self.KTCF.b[i]], writes=[self.KTC.b[i]])
            P.dma("sp", self.VCF.t[:, j, :], self.cv[j], writes=[self.VCF.b[j]])
            self.CP("pool", self.VC.t[:, j, :], self.VCF.t[:, j, :], reads=[self.VCF.b[j]], writes=[self.VC.b[j]])
            P.dma("pool", self.kc[j], self.ck[j, 32:128, :], writes=[self.youtb])
            P.dma("pool", self.vc[j], self.cv[j, 32:128, :], writes=[self.youtb])

    def sample_phase(self):
        P = self.P
        nt = NSMP
        self.load_x(self.xsT, NSMP, 0, nt)
        self.load_rope(self.ropeS, NSMP, 0, nt)
        self.hgrn_layer(nt, 32, full=True, sample=True)
        for j in range(2):
            for h in range(NH):
                P.dma("pool", self.Ss[j, h], self.SS.t[:, j * NH + h, :], reads=[self.SS.b[j * NH + h]], writes=[self.youtb])
        self.ffn(0, nt)
        self.tap("X2", self.X.t[:, :, 0:nt], self.X.b)

        def kout(kvh, kf, kfb):
            P.dma("pool", self.ksT[kvh], kf[0:64, 0:nt], reads=[kfb], writes=[self.youtb])

        def vout(i, bk, bb):
            self.CP("dve", self.VAF.t[0:32, i, :], bk[0:32, 0:256], reads=[bb], writes=[self.VAF.b[i]])
            P.dma("pool", self.vs[i], self.VAF.t[0:32, i, :], reads=[self.VAF.b[i]], writes=[self.youtb])

        self.kv_q_proj(nt, 0, 0, 32, kout, vout)
        for kvh in range(4):
            Ob, obb = self.bank()
            Db, dbb = self.bank()
            for j in range(2):
                for base in (0, 64):
                    q_rhs = self.AR.t[base:base + 64, 2 * kvh:2 * kvh + 2, j * 32:(j + 1) * 32]
                    q_reads = [self.AR.b[2 * kvh], self.AR.b[2 * kvh + 1]]
                    ci = j * 4 + kvh
                    kbs = [(self.KTC.t[base:base + 64, ci, :], [self.KTC.b[ci]], self.VC.t[:, j, kvh * 64:(kvh + 1) * 64], [self.VC.b[j]], 128),
                           (self.KT.t[base:base + 64, kvh, j * 32:(j + 1) * 32], [self.KT.b[kvh]],
                            self.VA.t[0:32, j, kvh * 64:(kvh + 1) * 64], [self.VA.b[j]], 32)]
                    cols = slice(j * 64, (j + 1) * 64)
                    self.attn_group(kvh, base, q_rhs, q_reads, 32, kbs, Ob[base:base + 64, cols], Db[base:base + 64, cols], obb, dbb)()
            self.attn_finish(kvh, Ob, obb, Db, dbb, 128, (2, 32, 0))
        self.tap("QR", self.AR.t[:, 0:8, 0:nt], self.AR.b[0:8])
        self.tap("KR", self.KT.t[:, :, 0:nt], self.KT.b)
        self.tap("ATT", self.ON.t[:, :, 0:nt], self.ON.b)
        self.out_proj("BO", nt, 5)
        self.tap("X3", self.X.t[:, :, 0:nt], self.X.b)
        self.ffn(1, nt)
        self.store_y(self.ysT, NSMP, 0, nt)


_NC_CACHE = {}


def get_nc():
    if "nc" not in _NC_CACHE:
        b = Builder()
        _NC_CACHE["nc"] = b.build()
    return _NC_CACHE["nc"]


def make_in_maps(x_prompt, x_sample, state_hgrn, cache_k, cache_v,
                 norm_mix_pre, norm_mix_post, norm_ffn_pre, norm_ffn_post, w_ffn_in, w_ffn_out,
                 w_a_in, a_lower_bound, a_out_norm, w_a_out, kv_norm, w_kv, w_b_q, b_sinks, w_b_out):
    f = np.float32
    inp = dict(w_ffn_in=w_ffn_in, w_ffn_out=w_ffn_out, w_a_in=w_a_in, w_a_out=w_a_out, w_kv=w_kv,
               w_b_q=w_b_q, w_b_out=w_b_out)
    wall = build_wall(inp)
    par = np.zeros((128, NPAR), f)
    vecs = [norm_mix_pre[0], norm_mix_post[0], norm_ffn_pre[0], norm_ffn_post[0],
            norm_mix_pre[1], norm_mix_post[1], norm_ffn_pre[1], norm_ffn_post[1], kv_norm]
    for i, v in enumerate(vecs):
        par[:, i * 8:(i + 1) * 8] = col8(v)
    par[:, 72:80] = np.asarray(a_out_norm, f)[0].T
    par[:, 80:88] = col8(np.asarray(a_lower_bound)[0])
    par[:, 88:96] = col8(np.asarray(a_lower_bound)[1])
    sk = np.asarray(b_sinks, f)[0]
    for c in range(8):
        par[0:64, 96 + c] = sk[2 * c]
        par[64:128, 96 + c] = sk[2 * c + 1]
    par[:, 104] = EPS
    par[:, 105] = 1.0
    cst = np.zeros((128, NCONST), f)
    cst[:, 0:128] = np.eye(128, dtype=f)
    cst[:, 128:256] = np.triu(np.ones((128, 128), f))
    rm = np.ones(NTM, f)
    rm[0::128] = 0.0
    cst[:, 256:256 + NTM] = rm[None, :]
    rm32 = np.ones(64, f)
    rm32[0::32] = 0.0
    cst[:, 256 + NTM:256 + NTM + 64] = rm32[None, :]
    cst[:, 256 + NTM + 64:] = 1.0

    xp = np.asarray(x_prompt, f)
    xs = np.asarray(x_sample, f)
    ck = np.asarray(cache_k, f)
    cv = np.asarray(cache_v, f)
    st = np.asarray(state_hgrn, f)
    ropeS = rope_tables(np.concatenate([2048 + np.arange(32), 2048 + np.arange(32)]))
    in_maps = []
    zeros_pre = np.zeros((D, NPRE), f)
    for c in range(8):
        b, role = c // 2, c % 2
        t0 = 0 if role == 0 else 8192 - NMAIN
        m = {
            "xT": np.ascontiguousarray(xp[b, t0:t0 + NMAIN, :].T),
            "xpT": zeros_pre if role == 0 else np.ascontiguousarray(xp[b, 0:NPRE, :].T),
            "xsT": np.ascontiguousarray(xs[2 * c:2 * c + 2].reshape(NSMP, D).T),
            "st0": np.ascontiguousarray(st[0, 2 * c:2 * c + 2]),
            "ck": np.ascontiguousarray(ck[2 * c:2 * c + 2].reshape(2, 128, 256)),
            "cv": np.ascontiguousarray(cv[2 * c:2 * c + 2].reshape(2, 128, 256)),
            "ckT": np.ascontiguousarray(ck[2 * c:2 * c + 2].transpose(0, 2, 3, 1)),
            "ropeM": rope_tables(t0 + np.arange(NMAIN)),
            "ropeS": ropeS,
            "wall": wall,
            "par": par,
            "cst": cst,
        }
        in_maps.append(m)
    return in_maps


def kernel(**inputs):
    f = np.float32
    in_maps = make_in_maps(**inputs)
    nc = get_nc()
    res = run_bass_kernel_spmd(nc, in_maps, core_ids=list(range(8)))
    R = res.results
    y_p = np.empty((4, 8192, D), f)
    st_p = np.empty((1, 4, NH, 128, 128), f)
    k_p = np.empty((4, 128, 4, 64), f)
    v_p = np.empty((4, 128, 4, 64), f)
    y_s = np.empty((16, 32, D), f)
    st_s = np.empty((1, 16, NH, 128, 128), f)
    k_s = np.empty((16, 128, 4, 64), f)
    v_s = np.empty((16, 128, 4, 64), f)
    for c in range(8):
        b, role = c // 2, c % 2
        r = R[c]
        yT = np.asarray(r["yT"])
        if role == 0:
            y_p[b, 0:4096] = yT[:, 0:4096].T
        else:
            y_p[b, 4096:8192] = yT[:, NMAIN - 4096:NMAIN].T
            st_p[0, b] = np.asarray(r["Sp"])
            k_p[b] = np.asarray(r["kpT"]).transpose(2, 0, 1)
            v_p[b] = np.asarray(r["vp"]).reshape(128, 4, 64)
        y_s[2 * c:2 * c + 2] = np.asarray(r["ysT"]).T.reshape(2, 32, D)
        st_s[0, 2 * c:2 * c + 2] = np.asarray(r["Ss"])
        ksT = np.asarray(r["ksT"])
        knew = ksT.transpose(2, 0, 1).reshape(2, 32, 4, 64)
        k_s[2 * c:2 * c + 2, 0:96] = np.asarray(r["kc"]).reshape(2, 96, 4, 64)
        k_s[2 * c:2 * c + 2, 96:128] = knew
        v_s[2 * c:2 * c + 2, 0:96] = np.asarray(r["vc"]).reshape(2, 96, 4, 64)
        v_s[2 * c:2 * c + 2, 96:128] = np.asarray(r["vs"]).reshape(2, 32, 4, 64)
    return (y_p, y_s, st_p, st_s, k_p, v_p, k_s, v_s)
```

```python
import numpy as np
from contextlib import ExitStack
import concourse.bass as bass
import concourse.mybir as mybir
from concourse.bass_utils import run_bass_kernel_spmd

F32 = mybir.dt.float32
BF16 = mybir.dt.bfloat16
AF = mybir.ActivationFunctionType
ALU = mybir.AluOpType

D = 1024
FC = 8
NH = 8
HC = 22
NTM = 384
NMAIN = 4224
NPRE = 3968
NSMP = 64
WIN = 128
EPS = 1e-6
SLOT = 8192
NSLOT = 3
NTMP = 14
NCONST = 128 + 128 + NTM + 64 + 128
NPAR = 112


class Buf:
    __slots__ = ("name", "w", "r", "excl", "wsmall")

    def __init__(self, name, excl=False):
        self.name = name
        self.w = None
        self.r = {}
        self.excl = excl
        self.wsmall = False


class Prog:
    ENG = ["pe", "act", "dve", "pool", "sp"]
    KRING = 8

    def __init__(self):
        self.items = {e: [] for e in self.ENG}
        self.cnt = {e: 0 for e in self.ENG}
        self.waited = {}
        self.dma_i = {}
        self.dma_n = {}
        self.semkeys = set(self.ENG)
        self.phase = ""

    def _need(self, eng, deps, skip_same=True):
        for k, v in deps.items():
            if k == eng and skip_same:
                continue
            if self.waited.get((eng, k), 0) < v:
                self.items[eng].append(("wait", k, v))
                self.waited[(eng, k)] = v

    @staticmethod
    def _deps(reads, writes):
        deps = {}

        def add(d):
            if d is None:
                return
            k, v = d
            if deps.get(k, 0) < v:
                deps[k] = v

        for b in reads:
            add(b.w)
            if b.excl:
                for it in b.r.items():
                    add(it)
        for b in writes:
            add(b.w)
            for it in b.r.items():
                add(it)
        return deps

    def op(self, eng, fn, reads=(), writes=(), sig=True, small=False):
        self._need(eng, self._deps(reads, writes))
        if eng != "pe":
            own = 0
            for b in list(reads) + list(writes):
                if b.w is not None and b.w[0] == eng and b.w[1] > own:
                    own = b.w[1]
            if own and self.waited.get((eng, eng), 0) < own:
                self.items[eng].append(("wait", eng, own))
                self.waited[(eng, eng)] = own
        if sig:
            self.cnt[eng] += 1
            tick = self.cnt[eng]
        else:
            tick = self.cnt[eng] + 1
        self.items[eng].append(("inst", fn, sig, self.phase))
        for b in reads:
            if b.excl:
                b.w = (eng, tick)
                b.r = {}
            else:
                if b.r.get(eng, 0) < tick:
                    b.r[eng] = tick
        for b in writes:
            b.w = (eng, tick)
            b.r = {}
            b.wsmall = small

    def dma(self, q, out_ap, in_ap, reads=(), writes=()):
        i = self.dma_i.get(q, 0)
        self.dma_i[q] = i + 1
        semkey = ("d", q, i % self.KRING)
        self.semkeys.add(semkey)
        n_prev = self.dma_n.get(semkey, 0)
        deps = self._deps(reads, writes)
        if n_prev:
            deps[semkey] = 16 * n_prev
        self._need(q, deps, skip_same=False)
        self.dma_n[semkey] = n_prev + 1
        val = 16 * (n_prev + 1)
        self.items[q].append(("dma", out_ap, in_ap, semkey))
        for b in reads:
            b.r[semkey] = val
        for b in writes:
            b.w = (semkey, val)
            b.r = {}
            b.wsmall = False

    def finish(self, eng="sp"):
        deps = {k: 16 * n for k, n in self.dma_n.items()}
        self._need(eng, deps)

    def materialize(self, nc, st):
        sems = {}
        for i, k in enumerate(sorted(self.semkeys, key=str)):
            sems[k] = st.enter_context(nc.semaphore("s%d" % i))
        block = st.enter_context(nc.Block())
        starters = {"pe": block.tensor, "act": block.scalar, "dve": block.vector,
                    "pool": block.gpsimd, "sp": block.sync}
        for e in self.ENG:
            items = self.items[e]
            if not items:
                continue

            def body(eng, items=items, e=e):
                for it in items:
                    if it[0] == "wait":
                        eng.wait_ge(sems[it[1]], it[2])
                    elif it[0] == "inst":
                        ins = it[1](eng)
                        if it[2]:
                            ins.then_inc(sems[e], 1)
                    else:
                        eng.dma_start(out=it[1], in_=it[2]).then_inc(sems[it[3]], 16)

            starters[e](body)


class CB:
    def __init__(self, nc, st, name, n, w, dtype, parts=128):
        self.t = st.enter_context(nc.sbuf_tensor(name, [parts, n, w], dtype))
        self.b = [Buf("%s%d" % (name, i)) for i in range(n)]
        self.n = n


def stage_table():
    stages = []

    def add(name, cols):
        stages.append((name, cols))

    add("PF", 8 * 1024)
    add("AV", 8 * 1024)
    for h in range(NH):
        add("A%d" % h, 8 * 384)
    add("AO", 8 * 1024)
    for l in range(2):
        for j in range(11):
            add("FI%d_%d" % (l, j), 2 * 8 * 256)
        for q in range(4):
            add("FO%d_%d" % (l, q), 2 * HC * 128)
        if l == 0:
            add("KVa", 8 * 1024)
            add("KVb", 8 * 256)
            for c in range(4):
                add("Q%d" % c, 2 * 8 * 256)
            add("BO", 8 * 1024)
    off = {}
    o = 0
    for name, cols in stages:
        off[name] = (o, cols)
        o += cols
    return stages, off, o


STAGES, STOFF, TOTC = stage_table()
CHUNK_BOUNDS = []


def _mk_chunks():
    groups = [["PF", "AV"], ["A%d" % h for h in range(NH)] + ["AO"],
              ["FI0_%d" % j for j in range(11)], ["FO0_%d" % q for q in range(4)],
              ["KVa", "KVb"] + ["Q%d" % c for c in range(4)] + ["BO"],
              ["FI1_%d" % j for j in range(11)], ["FO1_%d" % q for q in range(4)]]
    res = []
    for g in groups:
        a = STOFF[g[0]][0]
        b = STOFF[g[-1]][0] + STOFF[g[-1]][1]
        res.append((a, b, g))
    return res


CHUNKS = _mk_chunks()


def kc_block(W):
    return W.reshape(8, 128, -1).transpose(1, 0, 2)


def build_wall(inp):
    f = np.float32
    w_a_in = np.asarray(inp["w_a_in"], f)[0]
    w_a_out = np.asarray(inp["w_a_out"], f)[0]
    w_kv = np.asarray(inp["w_kv"], f)
    w_q = np.asarray(inp["w_b_q"], f)[0]
    w_bo = np.asarray(inp["w_b_out"], f)[0]
    wall = np.empty((128, TOTC), f)

    def put(name, arr):
        o, c = STOFF[name]
        wall[:, o:o + c] = arr.reshape(128, c)

    put("PF", kc_block(w_a_in[:, 1024:2048]))
    put("AV", kc_block(w_a_in[:, 2048:3072]))
    for h in range(NH):
        s = slice(h * 128, (h + 1) * 128)
        W = np.concatenate([w_a_in[:, 0:1024][:, s], w_a_in[:, 1024:2048][:, s],
                            w_a_in[:, 3072:4096][:, s]], axis=1)
        put("A%d" % h, kc_block(W))
    put("AO", kc_block(w_a_out))
    for l in range(2):
        Wi = np.asarray(inp["w_ffn_in"], f)[l]
        Wo = np.asarray(inp["w_ffn_out"], f)[l].reshape(HC, 128, 1024).transpose(1, 0, 2)
        for j in range(11):
            blks = []
            for hc in (2 * j, 2 * j + 1):
                W = np.concatenate([Wi[:, hc * 128:(hc + 1) * 128],
                                    Wi[:, 2816 + hc * 128:2816 + (hc + 1) * 128]], axis=1)
                blks.append(kc_block(W))
            put("FI%d_%d" % (l, j), np.stack(blks, axis=1))
        for q in range(4):
            blks = [Wo[:, :, oc * 128:(oc + 1) * 128] for oc in (2 * q, 2 * q + 1)]
            put("FO%d_%d" % (l, q), np.stack(blks, axis=1))
    perm = np.concatenate([np.arange(32, 64), np.arange(0, 32)])
    kd, kp = [], []
    for kvh in range(4):
        Kh = w_kv[:, kvh * 64:(kvh + 1) * 64]
        kd += [Kh, Kh]
        kp += [Kh[:, perm], Kh[:, perm]]
    put("KVa", kc_block(np.concatenate(kd + kp, axis=1)))
    put("KVb", kc_block(w_kv[:, 256:512]))
    for c4 in range(4):
        blks = []
        for c in (2 * c4, 2 * c4 + 1):
            Wc = w_q[:, c * 128:(c + 1) * 128]
            Wp = np.concatenate([Wc[:, 0:64][:, perm], Wc[:, 64:128][:, perm]], axis=1)
            blks.append(kc_block(np.concatenate([Wc, Wp], axis=1)))
        put("Q%d" % c4, np.stack(blks, axis=1))
    put("BO", kc_block(w_bo))
    return wall


def rope_tables(pos):
    f = np.float32
    half = 32
    inv = 10000.0 ** (-np.arange(half, dtype=np.float64) / half)
    ang = pos.astype(np.float64)[None, :] * inv[:, None]
    cos = np.cos(ang).astype(f)
    sin = np.sin(ang).astype(f)
    p = np.arange(128)
    d = p % 64
    fi = d % 32
    sign = np.where(d < 32, -1.0, 1.0).astype(f)
    return np.stack([cos[fi], sin[fi] * sign[:, None]], axis=0).astype(f)


def col8(v):
    return np.asarray(v, np.float32).reshape(8, 128).T


class Builder:
    def __init__(self, do_pre=True, do_main=True, do_sample=True, debug=False):
        self.do_pre, self.do_main, self.do_sample, self.debug = do_pre, do_main, do_sample, debug
        self.taps = {}
        self.nc = bass.Bass("TRN2", target_bir_lowering=False)
        self.P = Prog()
        self.st = ExitStack()
        self.bank_i = 0
        self.tmp_i = 0
        self.tmpb_i = 0
        self.pt_i = 0
        self.slot_i = 0

    def dram_in(self, name, shape, dtype=F32):
        return self.nc.dram_tensor(name, list(shape), dtype, kind="ExternalInput").ap()

    def dram_out(self, name, shape, dtype=F32):
        return self.nc.dram_tensor(name, list(shape), dtype, kind="ExternalOutput").ap()

    def tap(self, name, ap, reads):
        if not self.debug or name in self.taps:
            return
        shape = list(ap.shape)
        d = self.nc.dram_tensor("tap_" + name, shape, F32, kind="ExternalOutput").ap()
        self.taps[name] = shape
        self.P.dma("pool", d, ap, reads=reads, writes=[self.youtb])

    def sb(self, name, shape, dtype):
        return self.st.enter_context(self.nc.sbuf_tensor(name, list(shape), dtype))

    def bank(self):
        i = self.bank_i % 8
        self.bank_i += 1
        return self.banks[i], self.bankb[i]

    def tmp(self):
        i = self.tmp_i % NTMP
        self.tmp_i += 1
        return self.TP.t[:, i, :], self.TP.b[i]

    def tmpb(self):
        i = self.tmpb_i % 4
        self.tmpb_i += 1
        return self.TB.t[:, i, :], self.TB.b[i]

    def pt(self):
        i = self.pt_i % 4
        self.pt_i += 1
        return self.PT.t[:, i, :], self.PT.b[i]

    @staticmethod
    def _small(ap):
        n = 1
        for d in ap.shape[1:]:
            n *= d
        return n <= 128

    def MM(self, out, lhsT, rhs, start, stop, reads, bankb, sig=False):
        self.P.op("pe", lambda e: e.matmul(out, lhsT=lhsT, rhs=rhs, start=start, stop=stop),
                  reads=reads, writes=[bankb], sig=sig)

    def TR(self, out, in_, ident, reads, bankb, sig=False):
        self.P.op("pe", lambda e: e.transpose(out, in_, ident), reads=reads, writes=[bankb], sig=sig)

    def ACT(self, out, in_, func, reads, writes, scale=None, bias=None):
        kw = {}
        if scale is not None:
            kw["scale"] = scale
        if bias is not None:
            kw["bias"] = bias
        self.P.op("act", lambda e: e.activation(out=out, in_=in_, func=func, **kw),
                  reads=reads, writes=writes, small=self._small(out))

    def TT(self, eng, out, in0, in1, op, reads, writes):
        self.P.op(eng, lambda e: e.tensor_tensor(out=out, in0=in0, in1=in1, op=op),
                  reads=reads, writes=writes, small=self._small(out))

    def TS(self, eng, out, in0, s1, s2, op0, op1, reads, writes):
        if op1 is None:
            self.P.op(eng, lambda e: e.tensor_scalar(out=out, in0=in0, scalar1=s1, scalar2=None, op0=op0),
                      reads=reads, writes=writes, small=self._small(out))
        else:
            self.P.op(eng, lambda e: e.tensor_scalar(out=out, in0=in0, scalar1=s1, scalar2=s2, op0=op0, op1=op1),
                      reads=reads, writes=writes, small=self._small(out))

    def STT(self, out, in0, scalar, in1, op0, op1, reads, writes):
        self.P.op("dve", lambda e: e.scalar_tensor_tensor(out=out, in0=in0, scalar=scalar, in1=in1,
                                                          op0=op0, op1=op1), reads=reads, writes=writes,
                  small=self._small(out))

    def CP(self, eng, out, in_, reads, writes):
        if eng == "act":
            self.P.op("act", lambda e: e.copy(out=out, in_=in_), reads=reads, writes=writes, small=self._small(out))
        else:
            self.P.op(eng, lambda e: e.tensor_copy(out=out, in_=in_), reads=reads, writes=writes, small=self._small(out))

    def RECIP(self, out, in_, reads, writes):
        self.P.op("dve", lambda e: e.reciprocal(out=out, in_=in_), reads=reads, writes=writes, small=self._small(out))

    def load_stage(self, name):
        o, c = STOFF[name]
        i = self.slot_i % NSLOT
        self.slot_i += 1
        ck = None
        for k, (a, b, g) in enumerate(CHUNKS):
            if a <= o < b:
                ck = self.chunkb[k]
        self.P.dma("sp", self.ring[i][:, 0:c], self.WB[:, o:o + c], reads=[ck], writes=[self.ringb[i]])
        return self.ring[i], self.ringb[i]

    def build(self):
        nc, P = self.nc, self.P
        self.xT = self.dram_in("xT", [D, NMAIN])
        self.xpT = self.dram_in("xpT", [D, NPRE])
        self.xsT = self.dram_in("xsT", [D, NSMP])
        self.st0 = self.dram_in("st0", [2, NH, 128, 128])
        self.ck = self.dram_in("ck", [2, 128, 256])
        self.cv = self.dram_in("cv", [2, 128, 256])
        self.ckT = self.dram_in("ckT", [2, 4, 64, 128])
        self.ropeM = self.dram_in("ropeM", [2, 128, NMAIN])
        self.ropeS = self.dram_in("ropeS", [2, 128, NSMP])
        self.wall = self.dram_in("wall", [128, TOTC])
        self.par = self.dram_in("par", [128, NPAR])
        self.cst = self.dram_in("cst", [128, NCONST])
        self.WB = nc.dram_tensor("WB", [128, TOTC], BF16, kind="Internal").ap()

        self.yT = self.dram_out("yT", [D, NMAIN])
        self.ysT = self.dram_out("ysT", [D, NSMP])
        self.Sp = self.dram_out("Sp", [NH, 128, 128])
        self.Ss = self.dram_out("Ss", [2, NH, 128, 128])
        self.kpT = self.dram_out("kpT", [4, 64, 128])
        self.vp = self.dram_out("vp", [128, 256])
        self.ksT = self.dram_out("ksT", [4, 64, NSMP])
        self.vs = self.dram_out("vs", [2, 32, 256])
        self.kc = self.dram_out("kc", [2, 96, 256])
        self.vc = self.dram_out("vc", [2, 96, 256])

        st = self.st
        self.banks = [st.enter_context(nc.psum_tensor("pb%d" % i, [128, 512], F32)) for i in range(8)]
        self.bankb = [Buf("pb%d" % i, excl=True) for i in range(8)]
        self.X = CB(nc, st, "X", FC, NTM, F32)
        self.H = CB(nc, st, "H", FC, NTM, BF16)
        self.SQ = CB(nc, st, "SQ", FC, NTM, BF16)
        self.MIX = CB(nc, st, "MIX", FC, NTM, F32)
        self.AR = CB(nc, st, "AR", 24, NTM, BF16)
        self.SG = CB(nc, st, "SG", FC, NTM, BF16)
        self.ON = CB(nc, st, "ON", FC, NTM, BF16)
        self.VT = CB(nc, st, "VT", NTM // 128, 1024, BF16)
        self.TP = CB(nc, st, "TP", NTMP, NTM, F32)
        self.TB = CB(nc, st, "TB", 4, NTM, BF16)
        self.PT = CB(nc, st, "PT", 4, NTM, BF16)
        self.KH = CB(nc, st, "KH", 3, NTM, F32)
        self.LF = CB(nc, st, "LF", 1, NTM, F32)
        self.KHT = CB(nc, st, "KHT", 2, NTM, BF16)
        self.S = CB(nc, st, "S", NH, 128, F32)
        self.SS = CB(nc, st, "SS", 2 * NH, 128, F32)
        self.SBF = CB(nc, st, "SBF", NH, NTM, BF16)
        self.EBL = CB(nc, st, "EBL", NH, 4, F32)
        self.KT = CB(nc, st, "KT", 4, 128 + NTM, BF16)
        self.VA = CB(nc, st, "VA", 2 + NTM // 64, 256, BF16, parts=64)
        self.VAF = CB(nc, st, "VAF", 2, 256, F32, parts=64)
        self.KTC = CB(nc, st, "KTC", 8, 128, BF16)
        self.KTCF = CB(nc, st, "KTCF", 8, 128, F32)
        self.VC = CB(nc, st, "VC", 2, 256, BF16)
        self.VCF = CB(nc, st, "VCF", 2, 256, F32)
        self.ROPE = CB(nc, st, "ROPE", 2, NTM, F32)
        self.ring = [self.sb("ring%d" % i, [128, SLOT], BF16) for i in range(NSLOT)]
        self.ringb = [Buf("ring%d" % i) for i in range(NSLOT)]
        self.CST = self.sb("CST", [128, NCONST], F32)
        self.cstb = Buf("cst")
        self.PAR = self.sb("PAR", [128, NPAR], F32)
        self.parb = Buf("par")
        self.DER = self.sb("DER", [128, 40], F32)
        self.derb = Buf("der")
        self.ONESB = self.sb("ONESB", [128, 128], BF16)
        self.onesb = Buf("onesb")
        self.chunkb = [Buf("wchunk%d" % k) for k in range(len(CHUNKS))]
        self.youtb = Buf("yout")

        self.ident = self.CST[:, 0:128]
        self.mask = self.CST[:, 128:256]
        self.rmask128 = self.CST[:, 256:256 + NTM]
        self.rmask32 = self.CST[:, 256 + NTM:256 + NTM + 64]
        onesf = self.CST[:, 256 + NTM + 64:256 + NTM + 64 + 128]

        PIECE = 8192
        self.pieces = []
        o = 0
        while o < TOTC:
            e = min(TOTC, o + PIECE)
            self.pieces.append((o, e, Buf("wpiece%d" % o)))
            o = e
        self.piece_next = 0
        P.dma("act", self.CST[:, :], self.cst, writes=[self.cstb])
        P.dma("act", self.PAR[:, :], self.par, writes=[self.parb])
        self.CP("dve", self.ONESB[:, :], onesf, reads=[self.cstb], writes=[self.onesb])
        DER = self.DER
        a0 = self.PAR[:, 80:88]
        a1 = self.PAR[:, 88:96]
        self.one_col = self.PAR[:, 105:106]
        self.eps_col = self.PAR[:, 104:105]
        self.TT("dve", DER[:, 0:8], a1, a0, ALU.subtract, reads=[self.parb], writes=[self.derb])
        self.ACT(DER[:, 0:8], DER[:, 0:8], AF.Exp, reads=[self.derb], writes=[self.derb])
        self.ACT(DER[:, 0:8], DER[:, 0:8], AF.Ln, reads=[self.derb, self.parb], writes=[self.derb],
                 bias=self.one_col, scale=1.0)
        self.ACT(DER[:, 0:8], DER[:, 0:8], AF.Exp, reads=[self.derb], writes=[self.derb], scale=-1.0)
        self.TS("dve", DER[:, 8:16], DER[:, 0:8], -1.0, 1.0, ALU.mult, ALU.add, reads=[self.derb], writes=[self.derb])
        self.TS("dve", DER[:, 16:24], DER[:, 8:16], -1.0, None, ALU.mult, None, reads=[self.derb], writes=[self.derb])
        self.ACT(DER[:, 24:32], self.PAR[:, 96:104], AF.Exp, reads=[self.parb, self.derb], writes=[self.derb])
        self.lb, self.oml, self.noml, self.esink = DER[:, 0:8], DER[:, 8:16], DER[:, 16:24], DER[:, 24:32]
        self.tap("DER", DER[:, 0:32], [self.derb])

        if self.do_sample:
            self.sample_loads()
        for h in range(NH):
            self.P.op("dve", lambda e, h=h: e.memset(self.S.t[:, h, :], 0.0), writes=[self.S.b[h]])

        t0 = 0
        npre_tiles = (NPRE + NTM - 1) // NTM
        per_tile = (len(self.pieces) - 2 + max(1, npre_tiles - 2) - 1) // max(1, npre_tiles - 2)
        while t0 < NPRE and self.do_pre:
            nt = min(NTM, NPRE - t0)
            self.load_x(self.xpT, NPRE, t0, nt)
            if t0 == 0:
                self.emit_conv(2, after=self.X.b)
            else:
                self.emit_conv(per_tile)
            self.hgrn_layer(nt, 128, full=False)
            t0 += nt
        t0 = 0
        ti = 0
        while t0 < NMAIN and self.do_main:
            nt = min(NTM, NMAIN - t0)
            last = (t0 + nt == NMAIN)
            self.load_x(self.xT, NMAIN, t0, nt)
            self.emit_conv(len(self.pieces))
            self.load_rope(self.ropeM, NMAIN, t0, nt)
            self.hgrn_layer(nt, 128, full=True)
            self.ffn(0, nt)
            self.attn_layer_main(nt, first=(ti == 0), last=last)
            self.ffn(1, nt)
            self.store_y(self.yT, NMAIN, t0, nt)
            t0 += nt
            ti += 1
        for h in range(NH):
            P.dma("pool", self.Sp[h], self.S.t[:, h, :], reads=[self.S.b[h]], writes=[self.youtb])
        if self.do_sample:
            self.emit_conv(len(self.pieces))
            self.sample_phase()
        P.finish("sp")
        P.materialize(nc, st)
        return nc

    def _fix_chunk_deps(self):
        P = self.P
        i = 0
        cnt = {}
        self.chunk_deps = []
        for k, (a, b, g) in enumerate(CHUNKS):
            deps = {}
            for o in range(a, b, 16384):
                semkey = ("d", "pool", i % P.KRING)
                cnt[semkey] = cnt.get(semkey, 0) + 1
                deps[semkey] = 16 * cnt[semkey]
                i += 1
            self.chunk_deps.append(deps)
        self.chunk_bufs = []
        for k, deps in enumerate(self.chunk_deps):
            bl = []
            for sk, v in deps.items():
                b = Buf("wchunk%d_%s" % (k, sk[2]))
                b.w = (sk, v)
                bl.append(b)
            self.chunk_bufs.append(bl)

    def emit_conv(self, n, after=()):
        for _ in range(n):
            if self.piece_next >= len(self.pieces):
                return
            a, b, pb = self.pieces[self.piece_next]
            self.piece_next += 1
            self.P.dma("pool", self.WB[:, a:b], self.wall[:, a:b], reads=list(after), writes=[pb])

    def load_stage(self, name):
        o, c = STOFF[name]
        i = self.slot_i % NSLOT
        self.slot_i += 1
        bl = [pb for (a, b, pb) in self.pieces if a < o + c and b > o]
        for (a, b, pb) in self.pieces:
            if a < o + c and b > o:
                assert pb.w is not None, "conversion piece for %s not issued yet" % name
        self.P.dma("sp", self.ring[i][:, 0:c], self.WB[:, o:o + c], reads=bl, writes=[self.ringb[i]])
        return self.ring[i], self.ringb[i]

    def _old_load_stage(self, name):
        o, c = STOFF[name]
        i = self.slot_i % NSLOT
        self.slot_i += 1
        bl = None
        for k, (a, b, g) in enumerate(CHUNKS):
            if a <= o < b:
                bl = self.chunk_bufs[k]
        self.P.dma("sp", self.ring[i][:, 0:c], self.WB[:, o:o + c], reads=bl, writes=[self.ringb[i]])
        return self.ring[i], self.ringb[i]

    def load_x(self, src, ntot, t0, nt):
        v = src.rearrange("(c p) t -> p c t", p=128)
        for c in range(FC):
            self.P.dma("act", self.X.t[:, c, 0:nt], v[:, c, t0:t0 + nt], writes=[self.X.b[c]])

    def load_rope(self, src, ntot, t0, nt):
        v = src.rearrange("a p t -> p a t")
        self.P.dma("act", self.ROPE.t[:, :, 0:nt], v[:, :, t0:t0 + nt], writes=self.ROPE.b)

    def store_y(self, dst, ntot, t0, nt):
        v = dst.rearrange("(c p) t -> p c t", p=128)
        self.P.dma("pool", v[:, :, t0:t0 + nt], self.MIX.t[:, :, 0:nt], reads=self.MIX.b, writes=[self.youtb])

    def stats_rstd(self, srcs, nt, scale):
        bk, bb = self.bank()
        n = len(srcs)
        for i, (ap, b) in enumerate(srcs):
            self.MM(bk[:, 0:nt], self.ONESB[:, :], ap, i == 0, i == n - 1, [self.onesb, b], bb, sig=(i == n - 1))
        r, rb = self.tmp()
        self.ACT(r[:, 0:nt], bk[:, 0:nt], AF.Ln, reads=[bb, self.parb], writes=[rb], scale=scale, bias=self.eps_col)
        self.ACT(r[:, 0:nt], r[:, 0:nt], AF.Exp, reads=[rb], writes=[rb], scale=-0.5)
        return r, rb

    def norm_sq(self, nt):
        for c in range(FC):
            self.ACT(self.SQ.t[:, c, 0:nt], self.X.t[:, c, 0:nt], AF.Square, reads=[self.X.b[c]], writes=[self.SQ.b[c]])

    def prenorm(self, nt, widx_list, dsts):
        self.norm_sq(nt)
        r, rb = self.stats_rstd([(self.SQ.t[:, c, 0:nt], self.SQ.b[c]) for c in range(FC)], nt, 1.0 / D)
        for widx, dst in zip(widx_list, dsts):
            for c in range(FC):
                self.STT(dst.t[:, c, 0:nt], self.X.t[:, c, 0:nt], self.PAR[:, widx * 8 + c:widx * 8 + c + 1],
                         r[:, 0:nt], ALU.mult, ALU.mult, reads=[self.X.b[c], self.parb, rb], writes=[dst.b[c]])

    def evac_mix(self, bk, bb, oc, nt, widx):
        self.ACT(self.MIX.t[:, oc, 0:nt], bk[:, 0:nt], AF.Copy, reads=[bb, self.parb], writes=[self.MIX.b[oc]],
                 scale=self.PAR[:, widx * 8 + oc:widx * 8 + oc + 1])
        self.ACT(self.SQ.t[:, oc, 0:nt], bk[:, 0:nt], AF.Square, reads=[bb], writes=[self.SQ.b[oc]])

    def postnorm_add(self, nt, widx, to_mix=False):
        r, rb = self.stats_rstd([(self.SQ.t[:, c, 0:nt], self.SQ.b[c]) for c in range(FC)], nt, 1.0 / D)
        dst = self.MIX if to_mix else self.X
        for c in range(FC):
            t, tb = self.tmp()
            self.TT("dve", t[:, 0:nt], self.MIX.t[:, c, 0:nt], r[:, 0:nt], ALU.mult, reads=[self.MIX.b[c], rb], writes=[tb])
            self.TT("pool" if c % 2 == 0 else "dve", dst.t[:, c, 0:nt], self.X.t[:, c, 0:nt], t[:, 0:nt], ALU.add,
                    reads=[tb, self.X.b[c]], writes=[dst.b[c]])

    def recip_sig(self, src_ap, src_reads, nt):
        t, tb = self.tmp()
        self.ACT(t[:, 0:nt], src_ap, AF.Exp, reads=src_reads, writes=[tb], scale=-1.0)
        self.TS("dve", t[:, 0:nt], t[:, 0:nt], 1.0, None, ALU.add, None, reads=[tb], writes=[tb])
        self.RECIP(t[:, 0:nt], t[:, 0:nt], reads=[tb], writes=[tb])
        return t, tb

    def sigmoid_from(self, src_ap, src_reads, nt):
        t, tb = self.tmp()
        self.ACT(t[:, 0:nt], src_ap, AF.Exp, reads=src_reads, writes=[tb], scale=-1.0)
        self.ACT(t[:, 0:nt], t[:, 0:nt], AF.Ln, reads=[tb, self.parb], writes=[tb], bias=self.one_col, scale=1.0)
        self.ACT(t[:, 0:nt], t[:, 0:nt], AF.Exp, reads=[tb], writes=[tb], scale=-1.0)
        return t, tb

    def hgrn_layer(self, nt, BL, full, sample=False):
        P = self.P
        nb = nt // BL
        rmask = self.rmask32 if BL == 32 else self.rmask128
        P.phase = "hg_norm"
        self.prenorm(nt, [0], [self.H])
        H = self.H
        if sample:
            self.tap("X0", self.X.t[:, :, 0:nt], self.X.b)
            self.tap("H0", H.t[:, :, 0:nt], H.b)

        def state(h, j):
            if sample:
                return self.SS.t[:, j * NH + h, :], self.SS.b[j * NH + h]
            return self.S.t[:, h, :], self.S.b[h]

        P.phase = "hg_v"
        wv, wvb = self.load_stage("AV")
        wv3 = wv[:, 0:8192].rearrange("p (k n) -> p k n", k=8)
        for j in range(nb):
            for half in range(2):
                bk, bb = self.bank()
                for kc in range(8):
                    self.MM(bk[0:BL, 0:512], H.t[:, kc, j * BL:(j + 1) * BL], wv3[:, kc, half * 512:(half + 1) * 512],
                            kc == 0, kc == 7, [H.b[kc], wvb], bb, sig=(kc == 7))
                self.CP("act" if half == 0 else "dve", self.VT.t[0:BL, j, half * 512:(half + 1) * 512], bk[0:BL, 0:512],
                        reads=[bb], writes=[self.VT.b[j]])
        if sample:
            self.tap("VT", self.VT.t[0:BL, 0:nb, :], self.VT.b)
        if not full:
            wf, wfb = self.load_stage("PF")
            wf3 = wf[:, 0:8192].rearrange("p (k n) -> p k n", k=8)

        SF = self.MIX

        def phase1(h):
            P.phase = "hg_p1"
            if full:
                wa, wab = self.load_stage("A%d" % h)
                wa3 = wa[:, 0:3072].rearrange("p (k n) -> p k n", k=8)
            pf, pfb = self.bank()
            for kc in range(8):
                lw = wa3[:, kc, 128:256] if full else wf3[:, kc, h * 128:(h + 1) * 128]
                self.MM(pf[:, 0:nt], lw, H.t[:, kc, 0:nt], kc == 0, kc == 7, [H.b[kc], wab if full else wfb], pfb, sig=(kc == 7))
            self.ACT(SF.t[:, h, 0:nt], pf[:, 0:nt], AF.Sigmoid, reads=[pfb], writes=[SF.b[h]])
            if full:
                pq, pqb = self.bank()
                for kc in range(8):
                    self.MM(pq[:, 0:nt], wa3[:, kc, 0:128], H.t[:, kc, 0:nt], kc == 0, kc == 7, [H.b[kc], wab], pqb, sig=(kc == 7))
                sq_, sqb = self.tmp()
                self.ACT(sq_[:, 0:nt], pq[:, 0:nt], AF.Sigmoid, reads=[pqb], writes=[sqb])
                self.TT("dve", self.AR.t[:, h, 0:nt], pq[:, 0:nt], sq_[:, 0:nt], ALU.mult, reads=[pqb, sqb], writes=[self.AR.b[h]])
                pg, pgb = self.bank()
                for kc in range(8):
                    self.MM(pg[:, 0:nt], wa3[:, kc, 256:384], H.t[:, kc, 0:nt], kc == 0, kc == 7, [H.b[kc], wab], pgb, sig=(kc == 7))
                sg_, sgb = self.tmp()
                self.ACT(sg_[:, 0:nt], pg[:, 0:nt], AF.Sigmoid, reads=[pgb], writes=[sgb])
                self.TT("dve", self.SG.t[:, h, 0:nt], pg[:, 0:nt], sg_[:, 0:nt], ALU.mult, reads=[pgb, sgb], writes=[self.SG.b[h]])

        ctx = {}

        def s1ln(h):
            P.phase = "hg_s1"
            s, sb_ = SF.t[:, h, :], SF.b[h]
            lf, lfb = self.LF.t[:, 0, :], self.LF.b[0]
            self.ACT(lf[:, 0:nt], s[:, 0:nt], AF.Ln, reads=[sb_, self.derb], writes=[lfb],
                     scale=self.oml[:, h:h + 1], bias=self.lb[:, h:h + 1])

        def s1a(h):
            P.phase = "hg_s1"
            s, sb_ = SF.t[:, h, :], SF.b[h]
            lf, lfb = self.LF.t[:, 0, :], self.LF.b[0]
            kk, kkb = self.tmp()
            self.TS("dve", kk[:, 0:nt], s[:, 0:nt], self.noml[:, h:h + 1], self.oml[:, h:h + 1], ALU.mult, ALU.add,
                    reads=[sb_, self.derb], writes=[kkb])
            bcs, bcb = self.tmp()
            P.op("dve", lambda e, bcs=bcs, lf=lf: e.tensor_tensor_scan(out=bcs[:, 0:nt], data0=rmask[:, 0:nt], data1=lf[:, 0:nt],
                                                                        initial=0.0, op0=ALU.mult, op1=ALU.add),
                 reads=[lfb, self.cstb], writes=[bcb])
            ctx[("s1", h)] = (lf, lfb, kk, kkb, bcs, bcb)

        def s1b(h):
            P.phase = "hg_s1"
            lf, lfb, kk, kkb, bcs, bcb = ctx.pop(("s1", h))
            b3 = bcs[:, 0:nt].rearrange("p (j t) -> p j t", t=BL)
            self.ACT(self.EBL.t[:, h, 0:nb], b3[:, :, BL - 1], AF.Exp, reads=[bcb], writes=[self.EBL.b[h]])
            eh, ehb = self.tmp()
            for j in range(nb):
                self.ACT(eh[:, j * BL:(j + 1) * BL], bcs[:, j * BL:(j + 1) * BL], AF.Exp, reads=[bcb], writes=[ehb],
                         scale=-1.0, bias=bcs[:, (j + 1) * BL - 1:(j + 1) * BL])
            kh, khb = self.KH.t[:, h % 3, :], self.KH.b[h % 3]
            self.TT("pool", kh[:, 0:nt], kk[:, 0:nt], eh[:, 0:nt], ALU.mult, reads=[kkb, ehb], writes=[khb])
            if sample and h == 0:
                self.tap("lf", lf[:, 0:nt], [lfb])
                self.tap("kk", kk[:, 0:nt], [kkb])
                self.tap("bcs", bcs[:, 0:nt], [bcb])
                self.tap("kh", kh[:, 0:nt], [khb])
                self.tap("eh", eh[:, 0:nt], [ehb])
                self.tap("ebl", self.EBL.t[:, h, 0:nb], [self.EBL.b[h]])
            if full:
                eb, ebb = self.tmp()
                self.ACT(eb[:, 0:nt], bcs[:, 0:nt], AF.Exp, reads=[bcb], writes=[ebb])
                enb, enbb = self.tmp()
                self.ACT(enb[:, 0:nt], bcs[:, 0:nt], AF.Exp, reads=[bcb], writes=[enbb], scale=-1.0)
                QT, KTt = self.AR.t[:, h, :], self.AR.t[:, 8 + h, :]
                qtb, ktb = self.AR.b[h], self.AR.b[8 + h]
                self.TT("pool", KTt[:, 0:nt], kk[:, 0:nt], enb[:, 0:nt], ALU.mult, reads=[kkb, enbb], writes=[ktb])
                self.TT("pool", QT[:, 0:nt], QT[:, 0:nt], eb[:, 0:nt], ALU.mult, reads=[qtb, ebb], writes=[qtb])
                if sample and h == 0:
                    self.tap("QT", QT[:, 0:nt], [qtb])
                    self.tap("KTt", KTt[:, 0:nt], [ktb])
                    self.tap("SG", self.SG.t[:, h, 0:nt], [self.SG.b[h]])

        def s2a(h):
            P.phase = "hg_s2"
            kh, khb = self.KH.t[:, h % 3, :], self.KH.b[h % 3]
            QT, KTt = self.AR.t[:, h, :], self.AR.t[:, 8 + h, :]
            qtb, ktb = self.AR.b[h], self.AR.b[8 + h]
            ki = h % 2
            KHT, khtb = self.KHT.t[:, ki, :], self.KHT.b[ki]
            tb_, tbb = self.bank()
            for j in range(nb):
                self.TR(tb_[0:BL, j * 128:(j + 1) * 128], kh[:, j * BL:(j + 1) * BL], self.ident, [khb, self.cstb], tbb, sig=(j == nb - 1))
            kht3 = KHT[:, 0:nb * 128].rearrange("p (j k) -> p j k", k=128)
            tb3 = tb_[:, 0:nb * 128].rearrange("p (j k) -> p j k", k=128)
            self.CP("act", kht3[0:BL, :, :], tb3[0:BL, :, :], reads=[tbb], writes=[khtb])
            if sample and h == 0:
                self.tap("KHT", kht3[0:BL, :, :], [khtb])
            su, sub = self.bank()
            for j in range(nb):
                self.MM(su[:, j * 128:(j + 1) * 128], kht3[0:BL, j, :], self.VT.t[0:BL, j, h * 128:(h + 1) * 128], True, True,
                        [khtb, self.VT.b[j]], sub, sig=(j == nb - 1))
            SBF3 = self.SBF.t[:, h, 0:nb * 128].rearrange("p (j v) -> p j v", v=128)
            for j in range(nb):
                Sap, Sb = state(h, j)
                if full:
                    self.CP("dve", SBF3[:, j, :], Sap, reads=[Sb], writes=[self.SBF.b[h]])
                self.STT(Sap, Sap, self.EBL.t[:, h, j:j + 1], su[:, j * 128:(j + 1) * 128], ALU.mult, ALU.add,
                         reads=[Sb, self.EBL.b[h], sub], writes=[Sb])
            ctx[("s2", h)] = SBF3

        def s2b(h):
            P.phase = "hg_s2"
            SBF3 = ctx.pop(("s2", h))
            if not full:
                return
            QT, KTt = self.AR.t[:, h, :], self.AR.t[:, 8 + h, :]
            qtb, ktb = self.AR.b[h], self.AR.b[8 + h]
            sc, scb = self.bank()
            for j in range(nb):
                self.MM(sc[0:BL, j * BL:(j + 1) * BL], KTt[:, j * BL:(j + 1) * BL], QT[:, j * BL:(j + 1) * BL], True, True,
                        [ktb, qtb], scb, sig=(j == nb - 1))
            pm, pmb = self.pt()
            pm3 = pm[:, 0:nt].rearrange("p (j t) -> p j t", t=BL)
            sc3 = sc[:, 0:nt].rearrange("p (j t) -> p j t", t=BL)
            for j in range(nb):
                self.TT("dve", pm3[0:BL, j, :], sc3[0:BL, j, :], self.mask[0:BL, 0:BL], ALU.mult, reads=[scb, self.cstb], writes=[pmb])
            bo, bob = self.bank()
            for j in range(nb):
                self.MM(bo[:, j * BL:(j + 1) * BL], self.VT.t[0:BL, j, h * 128:(h + 1) * 128], pm3[0:BL, j, :], True, False,
                        [self.VT.b[j], pmb], bob)
                self.MM(bo[:, j * BL:(j + 1) * BL], SBF3[:, j, :], QT[:, j * BL:(j + 1) * BL], False, True,
                        [self.SBF.b[h], qtb], bob, sig=(j == nb - 1))
            osq, osqb = self.tmpb()
            self.ACT(osq[:, 0:nt], bo[:, 0:nt], AF.Square, reads=[bob], writes=[osqb])
            stb, stbb = self.bank()
            self.MM(stb[:, 0:nt], self.ONESB[:, :], osq[:, 0:nt], True, True, [self.onesb, osqb], stbb, sig=True)
            ctx[("s2c", h)] = (bo, bob, stb, stbb, pm, pmb)

        def s2c(h):
            P.phase = "hg_s2"
            if not full:
                return
            bo, bob, stb, stbb, pm, pmb = ctx.pop(("s2c", h))
            r, rb = self.tmp()
            self.ACT(r[:, 0:nt], stb[:, 0:nt], AF.Ln, reads=[stbb, self.parb], writes=[rb], scale=1.0 / 128, bias=self.eps_col)
            self.ACT(r[:, 0:nt], r[:, 0:nt], AF.Exp, reads=[rb], writes=[rb], scale=-0.5)
            t2, t2b = self.tmp()
            self.STT(t2[:, 0:nt], bo[:, 0:nt], self.PAR[:, 72 + h:73 + h], r[:, 0:nt], ALU.mult, ALU.mult,
                     reads=[bob, self.parb, rb], writes=[t2b])
            self.TT("pool", self.ON.t[:, h, 0:nt], t2[:, 0:nt], self.SG.t[:, h, 0:nt], ALU.mult,
                    reads=[t2b, self.SG.b[h]], writes=[self.ON.b[h]])
            if sample and h == 0:
                self.tap("pm", pm[0:BL, 0:nt], [pmb])
                self.tap("t2", t2[:, 0:nt], [t2b])
                self.tap("ON0", self.ON.t[:, h, 0:nt], [self.ON.b[h]])

        for h in range(NH):
            phase1(h)
        SK = 2
        s1ln(0)
        for step in range(NH + SK + 1):
            if step < NH:
                s1a(step)
            if 0 <= step - SK - 1 < NH:
                s2c(step - SK - 1)
            if 0 <= step - SK < NH:
                s2a(step - SK)
            if step + 1 < NH:
                s1ln(step + 1)
            if step < NH:
                s1b(step)
            if 0 <= step - SK < NH:
                s2b(step - SK)
        if not full:
            return
        P.phase = "hg_out"
        self.out_proj("AO", nt, 1)
        if sample:
            self.tap("X1", self.X.t[:, :, 0:nt], self.X.b)

    def out_proj(self, stage, nt, widx):
        w, wb = self.load_stage(stage)
        w3 = w[:, 0:8192].rearrange("p (k n) -> p k n", k=8)
        for oc in range(FC):
            bk, bb = self.bank()
            for kc in range(8):
                self.MM(bk[:, 0:nt], w3[:, kc, oc * 128:(oc + 1) * 128], self.ON.t[:, kc, 0:nt], kc == 0, kc == 7,
                        [self.ON.b[kc], wb], bb, sig=(kc == 7))
            self.evac_mix(bk, bb, oc, nt, widx)
        self.P.phase = self.P.phase + "_post"
        self.postnorm_add(nt, widx)

    def ffn(self, l, nt):
        self.P.phase = "ffn_norm"
        self.prenorm(nt, [2 + 4 * l], [self.H])
        H = self.H
        self.P.phase = "ffn_in"
        for j in range(11):
            w, wb = self.load_stage("FI%d_%d" % (l, j))
            w4 = w[:, 0:4096].rearrange("p (a k n) -> p a k n", a=2, k=8)
            for a in range(2):
                hc = 2 * j + a
                pa, pab = self.bank()
                for kc in range(8):
                    self.MM(pa[:, 0:nt], w4[:, a, kc, 0:128], H.t[:, kc, 0:nt], kc == 0, kc == 7, [H.b[kc], wb], pab, sig=(kc == 7))
                pb, pbb = self.bank()
                for kc in range(8):
                    self.MM(pb[:, 0:nt], w4[:, a, kc, 128:256], H.t[:, kc, 0:nt], kc == 0, kc == 7, [H.b[kc], wb], pbb, sig=(kc == 7))
                sg_, sgb = self.sigmoid_from(pa[:, 0:nt], [pab], nt)
                t1, t1b = self.tmp()
                self.TT("dve", t1[:, 0:nt], pa[:, 0:nt], sg_[:, 0:nt], ALU.mult, reads=[pab, sgb], writes=[t1b])
                self.TT("dve", self.AR.t[:, hc, 0:nt], pb[:, 0:nt], t1[:, 0:nt], ALU.mult, reads=[pbb, t1b], writes=[self.AR.b[hc]])
        self.P.phase = "ffn_out"
        for q in range(4):
            w, wb = self.load_stage("FO%d_%d" % (l, q))
            w4 = w[:, 0:2 * HC * 128].rearrange("p (a k n) -> p a k n", a=2, k=HC)
            for a in range(2):
                oc = 2 * q + a
                bk, bb = self.bank()
                for hc in range(HC):
                    self.MM(bk[:, 0:nt], w4[:, a, hc, :], self.AR.t[:, hc, 0:nt], hc == 0, hc == HC - 1,
                            [self.AR.b[hc], wb], bb, sig=(hc == HC - 1))
                self.evac_mix(bk, bb, oc, nt, 3 + 4 * l)
        self.P.phase = "ffn_post"
        self.postnorm_add(nt, 3 + 4 * l, to_mix=(l == 1))

    def rope_evac(self, p1, p1b, p2, p2b, nt, parts=128):
        t1, t1b = self.tmp()
        t2, t2b = self.tmp()
        self.TT("dve", t1[:, 0:nt], p1[:, 0:nt], self.ROPE.t[:, 0, 0:nt], ALU.mult, reads=[p1b, self.ROPE.b[0]], writes=[t1b])
        self.TT("dve", t2[:, 0:nt], p2[:, 0:nt], self.ROPE.t[:, 1, 0:nt], ALU.mult, reads=[p2b, self.ROPE.b[1]], writes=[t2b])
        self.TT("pool", t1[:, 0:nt], t1[:, 0:nt], t2[:, 0:nt], ALU.add, reads=[t1b, t2b], writes=[t1b])
        return t1, t1b

    def kv_q_proj(self, nt, kcol0, vchunk0, vrows, kout=None, vout=None):
        self.P.phase = "kv_norm"
        self.prenorm(nt, [8, 4], [self.SQ_H2(), self.H])
        self.P.phase = "kvq_proj"
        HK = self.H2
        H = self.H
        w, wb = self.load_stage("KVa")
        w3 = w[:, 0:8192].rearrange("p (k n) -> p k n", k=8)
        for kvh in range(4):
            p1, p1b = self.bank()
            for kc in range(8):
                self.MM(p1[:, 0:nt], w3[:, kc, kvh * 128:(kvh + 1) * 128], HK.t[:, kc, 0:nt], kc == 0, kc == 7, [HK.b[kc], wb], p1b, sig=(kc == 7))
            p2, p2b = self.bank()
            for kc in range(8):
                self.MM(p2[:, 0:nt], w3[:, kc, 512 + kvh * 128:512 + (kvh + 1) * 128], HK.t[:, kc, 0:nt], kc == 0, kc == 7, [HK.b[kc], wb], p2b, sig=(kc == 7))
            kf, kfb = self.rope_evac(p1, p1b, p2, p2b, nt)
            self.CP("act", self.KT.t[:, kvh, kcol0:kcol0 + nt], kf[:, 0:nt], reads=[kfb], writes=[self.KT.b[kvh]])
            if kout is not None:
                kout(kvh, kf, kfb)
        w, wb = self.load_stage("KVb")
        w3 = w[:, 0:2048].rearrange("p (k n) -> p k n", k=8)
        nch = nt // vrows
        for i in range(nch):
            bk, bb = self.bank()
            for kc in range(8):
                self.MM(bk[0:vrows, 0:256], HK.t[:, kc, i * vrows:(i + 1) * vrows], w3[:, kc, :], kc == 0, kc == 7, [HK.b[kc], wb], bb, sig=(kc == 7))
            self.CP("act", self.VA.t[0:vrows, vchunk0 + i, :], bk[0:vrows, 0:256], reads=[bb], writes=[self.VA.b[vchunk0 + i]])
            if vout is not None:
                vout(i, bk, bb)
        for c4 in range(4):
            w, wb = self.load_stage("Q%d" % c4)
            w4 = w[:, 0:4096].rearrange("p (a k n) -> p a k n", a=2, k=8)
            for a in range(2):
                c = 2 * c4 + a
                p1, p1b = self.bank()
                for kc in range(8):
                    self.MM(p1[:, 0:nt], w4[:, a, kc, 0:128], H.t[:, kc, 0:nt], kc == 0, kc == 7, [H.b[kc], wb], p1b, sig=(kc == 7))
                p2, p2b = self.bank()
                for kc in range(8):
                    self.MM(p2[:, 0:nt], w4[:, a, kc, 128:256], H.t[:, kc, 0:nt], kc == 0, kc == 7, [H.b[kc], wb], p2b, sig=(kc == 7))
                qf, qfb = self.rope_evac(p1, p1b, p2, p2b, nt)
                self.CP("act", self.AR.t[:, c, 0:nt], qf[:, 0:nt], reads=[qfb], writes=[self.AR.b[c]])

    def SQ_H2(self):
        class V:
            pass
        v = V()
        v.t = self.AR.t[:, 8:16, :]
        v.b = self.AR.b[8:16]
        self.H2 = v
        return v

    def attn_group(self, kvh, base, q_rhs, q_reads, nq, keyblocks, O_ap, D_ap, ob, db):
        nk = len(keyblocks)
        sb_, sbb = self.bank()
        N = 2 * nq
        for i, (kT, kr, vv, vr, ns) in enumerate(keyblocks):
            self.MM(sb_[0:ns, i * 128:i * 128 + N], kT, q_rhs, True, True, kr + q_reads, sbb, sig=(i == nk - 1))
        pt, ptb = self.pt()
        nsmax = max(kb[4] for kb in keyblocks)
        if all(kb[4] == nsmax for kb in keyblocks) and N == 128:
            self.ACT(pt[0:nsmax, 0:nk * 128], sb_[0:nsmax, 0:nk * 128], AF.Exp, reads=[sbb], writes=[ptb], scale=0.125)
        else:
            for i, kb in enumerate(keyblocks):
                self.ACT(pt[0:kb[4], i * 128:i * 128 + N], sb_[0:kb[4], i * 128:i * 128 + N], AF.Exp, reads=[sbb], writes=[ptb], scale=0.125)

        def part_b():
            for i, (kT, kr, vv, vr, ns) in enumerate(keyblocks):
                self.MM(O_ap, vv, pt[0:ns, i * 128:i * 128 + N], i == 0, i == nk - 1, vr + [ptb], ob, sig=(i == nk - 1))
            for i, (kT, kr, vv, vr, ns) in enumerate(keyblocks):
                self.MM(D_ap, self.ONESB[0:ns, 0:64], pt[0:ns, i * 128:i * 128 + N], i == 0, i == nk - 1, [self.onesb, ptb], db, sig=(i == nk - 1))
        return part_b

    def attn_finish(self, kvh, Ob, obb, Db, dbb, ncols, dst_cols):
        g, nq, col0 = dst_cols
        r, rb = self.tmp()
        D4 = Db[:, 0:ncols].rearrange("p (g i t) -> p g i t", i=2, t=nq)
        O4 = Ob[:, 0:ncols].rearrange("p (g i t) -> p g i t", i=2, t=nq)
        r4 = r[:, 0:ncols].rearrange("p (g i t) -> p g i t", i=2, t=nq)
        for i in range(2):
            c = 2 * kvh + i
            self.TS("dve", r4[:, :, i, :], D4[:, :, i, :], self.esink[:, c:c + 1], None, ALU.add, None,
                    reads=[dbb, self.derb], writes=[rb])
        self.RECIP(r[:, 0:ncols], r[:, 0:ncols], reads=[rb], writes=[rb])
        for i in range(2):
            c = 2 * kvh + i
            dst = self.ON.t[:, c, col0:col0 + g * nq].rearrange("p (g t) -> p g t", t=nq)
            self.TT("dve", dst, O4[:, :, i, :], r4[:, :, i, :], ALU.mult, reads=[obb, rb], writes=[self.ON.b[c]])

    def attn_layer_main(self, nt, first, last):
        P = self.P
        nch = nt // 64

        def kout(kvh, kf, kfb):
            if last:
                P.dma("pool", self.kpT[kvh], kf[0:64, nt - 128:nt], reads=[kfb], writes=[self.youtb])

        def vout(i, bk, bb):
            if last and i >= nch - 2:
                k = i - (nch - 2)
                self.CP("dve", self.VAF.t[0:64, k, :], bk[0:64, 0:256], reads=[bb], writes=[self.VAF.b[k]])
                P.dma("pool", self.vp[k * 64:(k + 1) * 64, :], self.VAF.t[0:64, k, :], reads=[self.VAF.b[k]], writes=[self.youtb])

        self.kv_q_proj(nt, 128, 2, 64, kout, vout)
        self.P.phase = "attn"
        GQ = min(4, NTM // 128)
        DEPTH = 3
        queue = []
        for kvh in range(4):
            for g0 in range(0, nch, GQ):
                gn = min(GQ, nch - g0)
                Ob, obb = self.bank()
                Db, dbb = self.bank()
                for qi in range(gn):
                    qc = g0 + qi
                    for base in (0, 64):
                        q_rhs = self.AR.t[base:base + 64, 2 * kvh:2 * kvh + 2, qc * 64:(qc + 1) * 64]
                        q_reads = [self.AR.b[2 * kvh], self.AR.b[2 * kvh + 1]]
                        kbs = []
                        for i in (qc, qc + 1, qc + 2):
                            if first and i < 2:
                                continue
                            kbs.append((self.KT.t[base:base + 64, kvh, i * 64:(i + 1) * 64], [self.KT.b[kvh]],
                                        self.VA.t[0:64, i, kvh * 64:(kvh + 1) * 64], [self.VA.b[i]], 64))
                        cols = slice(qi * 128, (qi + 1) * 128)
                        pb = self.attn_group(kvh, base, q_rhs, q_reads, 64, kbs, Ob[base:base + 64, cols], Db[base:base + 64, cols], obb, dbb)
                        lastg = (qi == gn - 1 and base == 64)

                        def item(pb=pb, lastg=lastg, kvh=kvh, Ob=Ob, obb=obb, Db=Db, dbb=dbb, gn=gn, g0=g0):
                            pb()
                            if lastg:
                                self.attn_finish(kvh, Ob, obb, Db, dbb, gn * 128, (gn, 64, g0 * 64))
                        queue.append(item)
                        if len(queue) > DEPTH:
                            queue.pop(0)()
        while queue:
            queue.pop(0)()
        for kvh in range(4):
            self.CP("pool", self.KT.t[:, kvh, 0:128], self.KT.t[:, kvh, nt:nt + 128], reads=[self.KT.b[kvh]], writes=[self.KT.b[kvh]])
        for k in range(2):
            self.CP("pool", self.VA.t[0:64, k, :], self.VA.t[0:64, nch + k, :], reads=[self.VA.b[nch + k]], writes=[self.VA.b[k]])
        self.P.phase = "attn_out"
        self.out_proj("BO", nt, 5)

    def sample_loads(self):
        P = self.P
        for j in range(2):
            for h in range(NH):
                P.dma("sp", self.SS.t[:, j * NH + h, :], self.st0[j, h], writes=[self.SS.b[j * NH + h]])
        for j in range(2):
            for kvh in range(4):
                i = j * 4 + kvh
                P.dma("sp", self.KTCF.t[0:64, i, :], self.ckT[j, kvh], writes=[self.KTCF.b[i]])
                P.dma("sp", self.KTCF.t[64:128, i, :], self.ckT[j, kvh], writes=[self.KTCF.b[i]])
                self.CP("pool", self.KTC.t[:, i, :], self.KTCF.t[:, i, :], reads=[self.KTCF.b[i]], writes=[self.KTC.b[i]])
            P.dma("sp", self.VCF.t[:, j, :], self.cv[j], writes=[self.VCF.b[j]])
            self.CP("pool", self.VC.t[:, j, :], self.VCF.t[:, j, :], reads=[self.VCF.b[j]], writes=[self.VC.b[j]])
            P.dma("pool", self.kc[j], self.ck[j, 32:128, :], writes=[self.youtb])
            P.dma("pool", self.vc[j], self.cv[j, 32:128, :], writes=[self.youtb])

    def sample_phase(self):
        P = self.P
        nt = NSMP
        self.load_x(self.xsT, NSMP, 0, nt)
        self.load_rope(self.ropeS, NSMP, 0, nt)
        self.hgrn_layer(nt, 32, full=True, sample=True)
        for j in range(2):
            for h in range(NH):
                P.dma("pool", self.Ss[j, h], self.SS.t[:, j * NH + h, :], reads=[self.SS.b[j * NH + h]], writes=[self.youtb])
        self.ffn(0, nt)
        self.tap("X2", self.X.t[:, :, 0:nt], self.X.b)

        def kout(kvh, kf, kfb):
            P.dma("pool", self.ksT[kvh], kf[0:64, 0:nt], reads=[kfb], writes=[self.youtb])

        def vout(i, bk, bb):
            self.CP("dve", self.VAF.t[0:32, i, :], bk[0:32, 0:256], reads=[bb], writes=[self.VAF.b[i]])
            P.dma("pool", self.vs[i], self.VAF.t[0:32, i, :], reads=[self.VAF.b[i]], writes=[self.youtb])

        self.kv_q_proj(nt, 0, 0, 32, kout, vout)
        for kvh in range(4):
            Ob, obb = self.bank()
            Db, dbb = self.bank()
            for j in range(2):
                for base in (0, 64):
                    q_rhs = self.AR.t[base:base + 64, 2 * kvh:2 * kvh + 2, j * 32:(j + 1) * 32]
                    q_reads = [self.AR.b[2 * kvh], self.AR.b[2 * kvh + 1]]
                    ci = j * 4 + kvh
                    kbs = [(self.KTC.t[base:base + 64, ci, :], [self.KTC.b[ci]], self.VC.t[:, j, kvh * 64:(kvh + 1) * 64], [self.VC.b[j]], 128),
                           (self.KT.t[base:base + 64, kvh, j * 32:(j + 1) * 32], [self.KT.b[kvh]],
                            self.VA.t[0:32, j, kvh * 64:(kvh + 1) * 64], [self.VA.b[j]], 32)]
                    cols = slice(j * 64, (j + 1) * 64)
                    self.attn_group(kvh, base, q_rhs, q_reads, 32, kbs, Ob[base:base + 64, cols], Db[base:base + 64, cols], obb, dbb)()
            self.attn_finish(kvh, Ob, obb, Db, dbb, 128, (2, 32, 0))
        self.tap("QR", self.AR.t[:, 0:8, 0:nt], self.AR.b[0:8])
        self.tap("KR", self.KT.t[:, :, 0:nt], self.KT.b)
        self.tap("ATT", self.ON.t[:, :, 0:nt], self.ON.b)
        self.out_proj("BO", nt, 5)
        self.tap("X3", self.X.t[:, :, 0:nt], self.X.b)
        self.ffn(1, nt)
        self.store_y(self.ysT, NSMP, 0, nt)


_NC_CACHE = {}


def get_nc():
    if "nc" not in _NC_CACHE:
        b = Builder()
        _NC_CACHE["nc"] = b.build()
    return _NC_CACHE["nc"]


def make_in_maps(x_prompt, x_sample, state_hgrn, cache_k, cache_v,
                 norm_mix_pre, norm_mix_post, norm_ffn_pre, norm_ffn_post, w_ffn_in, w_ffn_out,
                 w_a_in, a_lower_bound, a_out_norm, w_a_out, kv_norm, w_kv, w_b_q, b_sinks, w_b_out):
    f = np.float32
    inp = dict(w_ffn_in=w_ffn_in, w_ffn_out=w_ffn_out, w_a_in=w_a_in, w_a_out=w_a_out, w_kv=w_kv,
               w_b_q=w_b_q, w_b_out=w_b_out)
    wall = build_wall(inp)
    par = np.zeros((128, NPAR), f)
    vecs = [norm_mix_pre[0], norm_mix_post[0], norm_ffn_pre[0], norm_ffn_post[0],
            norm_mix_pre[1], norm_mix_post[1], norm_ffn_pre[1], norm_ffn_post[1], kv_norm]
    for i, v in enumerate(vecs):
        par[:, i * 8:(i + 1) * 8] = col8(v)
    par[:, 72:80] = np.asarray(a_out_norm, f)[0].T
    par[:, 80:88] = col8(np.asarray(a_lower_bound)[0])
    par[:, 88:96] = col8(np.asarray(a_lower_bound)[1])
    sk = np.asarray(b_sinks, f)[0]
    for c in range(8):
        par[0:64, 96 + c] = sk[2 * c]
        par[64:128, 96 + c] = sk[2 * c + 1]
    par[:, 104] = EPS
    par[:, 105] = 1.0
    cst = np.zeros((128, NCONST), f)
    cst[:, 0:128] = np.eye(128, dtype=f)
    cst[:, 128:256] = np.triu(np.ones((128, 128), f))
    rm = np.ones(NTM, f)
    rm[0::128] = 0.0
    cst[:, 256:256 + NTM] = rm[None, :]
    rm32 = np.ones(64, f)
    rm32[0::32] = 0.0
    cst[:, 256 + NTM:256 + NTM + 64] = rm32[None, :]
    cst[:, 256 + NTM + 64:] = 1.0

    xp = np.asarray(x_prompt, f)
    xs = np.asarray(x_sample, f)
    ck = np.asarray(cache_k, f)
    cv = np.asarray(cache_v, f)
    st = np.asarray(state_hgrn, f)
    ropeS = rope_tables(np.concatenate([2048 + np.arange(32), 2048 + np.arange(32)]))
    in_maps = []
    zeros_pre = np.zeros((D, NPRE), f)
    for c in range(8):
        b, role = c // 2, c % 2
        t0 = 0 if role == 0 else 8192 - NMAIN
        m = {
            "xT": np.ascontiguousarray(xp[b, t0:t0 + NMAIN, :].T),
            "xpT": zeros_pre if role == 0 else np.ascontiguousarray(xp[b, 0:NPRE, :].T),
            "xsT": np.ascontiguousarray(xs[2 * c:2 * c + 2].reshape(NSMP, D).T),
            "st0": np.ascontiguousarray(st[0, 2 * c:2 * c + 2]),
            "ck": np.ascontiguousarray(ck[2 * c:2 * c + 2].reshape(2, 128, 256)),
            "cv": np.ascontiguousarray(cv[2 * c:2 * c + 2].reshape(2, 128, 256)),
            "ckT": np.ascontiguousarray(ck[2 * c:2 * c + 2].transpose(0, 2, 3, 1)),
            "ropeM": rope_tables(t0 + np.arange(NMAIN)),
            "ropeS": ropeS,
            "wall": wall,
            "par": par,
            "cst": cst,
        }
        in_maps.append(m)
    return in_maps


def kernel(**inputs):
    f = np.float32
    in_maps = make_in_maps(**inputs)
    nc = get_nc()
    res = run_bass_kernel_spmd(nc, in_maps, core_ids=list(range(8)))
    R = res.results
    y_p = np.empty((4, 8192, D), f)
    st_p = np.empty((1, 4, NH, 128, 128), f)
    k_p = np.empty((4, 128, 4, 64), f)
    v_p = np.empty((4, 128, 4, 64), f)
    y_s = np.empty((16, 32, D), f)
    st_s = np.empty((1, 16, NH, 128, 128), f)
    k_s = np.empty((16, 128, 4, 64), f)
    v_s = np.empty((16, 128, 4, 64), f)
    for c in range(8):
        b, role = c // 2, c % 2
        r = R[c]
        yT = np.asarray(r["yT"])
        if role == 0:
            y_p[b, 0:4096] = yT[:, 0:4096].T
        else:
            y_p[b, 4096:8192] = yT[:, NMAIN - 4096:NMAIN].T
            st_p[0, b] = np.asarray(r["Sp"])
            k_p[b] = np.asarray(r["kpT"]).transpose(2, 0, 1)
            v_p[b] = np.asarray(r["vp"]).reshape(128, 4, 64)
        y_s[2 * c:2 * c + 2] = np.asarray(r["ysT"]).T.reshape(2, 32, D)
        st_s[0, 2 * c:2 * c + 2] = np.asarray(r["Ss"])
        ksT = np.asarray(r["ksT"])
        knew = ksT.transpose(2, 0, 1).reshape(2, 32, 4, 64)
        k_s[2 * c:2 * c + 2, 0:96] = np.asarray(r["kc"]).reshape(2, 96, 4, 64)
        k_s[2 * c:2 * c + 2, 96:128] = knew
        v_s[2 * c:2 * c + 2, 0:96] = np.asarray(r["vc"]).reshape(2, 96, 4, 64)
        v_s[2 * c:2 * c + 2, 96:128] = np.asarray(r["vs"]).reshape(2, 32, 4, 64)
    return (y_p, y_s, st_p, st_s, k_p, v_p, k_s, v_s)
```
